# Optimizing a Trainium2 kernel written in Bass

```python
import jax
import jax.numpy as jnp
from jax import lax
import numpy as np

D_MODEL = 1024
BATCH = 2
SEQ = 8192
DEPTH = 2

HQ_A = 8
HKV_A = 2
DH_A = 64
WINDOW = 128
BLOCK_A = 128
H_B = 4
DK_B = 128
DV_B = 256
GK_RANK = 16
GATE_NORMALIZER = 16.0
CHUNK_B = 64
SUB_B = 16
D_FF = 2816
CONV_W = 3
EPS = 1e-6

QA_W = HQ_A * DH_A
KVA_W = HKV_A * DH_A
QKB_W = H_B * DK_B
VB_W = H_B * DV_B
SPLITS = (QA_W, KVA_W, KVA_W, QKB_W, QKB_W, VB_W, VB_W, GK_RANK, D_MODEL, D_MODEL)
IN_W = QA_W + 2 * KVA_W + 2 * QKB_W + 2 * VB_W + GK_RANK + 2 * D_MODEL

kernel_name = "hybrid_swa_sink_gla_convffn"


def rmsnorm(x, g):
    xf = x.astype(jnp.float32)
    y = xf * lax.rsqrt(jnp.mean(xf * xf, axis=-1, keepdims=True) + EPS)
    return (y * g.astype(jnp.float32)).astype(x.dtype)


def split_cols(z):
    idx = []
    acc = 0
    for w in SPLITS[:-1]:
        acc += w
        idx.append(acc)
    return jnp.split(z, idx, axis=-1)


def sliding_window_sink_attention(q, k, v, sinks):
    B, S = q.shape[0], q.shape[1]
    nb = S // BLOCK_A
    G = HQ_A // HKV_A
    qb = q.reshape(B, nb, BLOCK_A, HKV_A, G, DH_A)
    kb = k.reshape(B, nb, BLOCK_A, HKV_A, DH_A)
    vb = v.reshape(B, nb, BLOCK_A, HKV_A, DH_A)
    shift = lambda t: jnp.concatenate([jnp.zeros_like(t[:, :1]), t[:, :-1]], axis=1)
    kk = jnp.concatenate([shift(kb), kb], axis=2)
    vv = jnp.concatenate([shift(vb), vb], axis=2)
    s = jnp.einsum("bnqhgd,bnkhd->bnhgqk", qb, kk).astype(jnp.float32) * (DH_A ** -0.5)
    qpos = jnp.arange(BLOCK_A)[:, None]
    kpos = jnp.arange(2 * BLOCK_A)[None, :] - BLOCK_A
    band = (kpos <= qpos) & (kpos > qpos - WINDOW)
    kabs = (jnp.arange(nb) * BLOCK_A)[:, None, None] + kpos[None]
    mask = band[None] & (kabs >= 0)
    s = jnp.where(mask[None, :, None, None], s, -jnp.inf)
    sink = sinks.astype(jnp.float32).reshape(HKV_A, G)[None, None, :, :, None, None]
    m = jnp.maximum(jnp.max(s, axis=-1, keepdims=True), sink)
    p = jnp.exp(s - m)
    p = p / (jnp.sum(p, axis=-1, keepdims=True) + jnp.exp(sink - m))
    o = jnp.einsum("bnhgqk,bnkhd->bnqhgd", p.astype(v.dtype), vv)
    return o.reshape(B, S, QA_W)


def gla_chunked(q, k, v, gk):
    out_dtype = v.dtype
    f32 = jnp.float32
    B, S = q.shape[0], q.shape[1]
    N = S // CHUNK_B
    NS = CHUNK_B // SUB_B
    to_chunks = lambda t: t.astype(f32).reshape(B, N, CHUNK_B, H_B, t.shape[-1]).transpose(0, 3, 1, 2, 4)
    q = to_chunks(q) * (DK_B ** -0.5)
    k = to_chunks(k)
    v = to_chunks(v)
    b = jnp.cumsum(to_chunks(gk), axis=3)
    b_last = b[:, :, :, -1:]
    kv = jnp.einsum("bhncd,bhnce->bhnde", k * jnp.exp(b_last - b), v)
    decay = jnp.exp(b_last[:, :, :, 0])

    def step(state, inp):
        kv_n, d_n = inp
        return d_n[..., None] * state + kv_n, state

    _, s_prev = lax.scan(step, jnp.zeros((B, H_B, DK_B, DV_B), f32),
                         (jnp.moveaxis(kv, 2, 0), jnp.moveaxis(decay, 2, 0)))
    s_prev = jnp.moveaxis(s_prev, 0, 2)
    o_inter = jnp.einsum("bhncd,bhnde->bhnce", q * jnp.exp(b), s_prev)
    sub = lambda t: t.reshape(B, H_B, N, NS, SUB_B, t.shape[-1])
    qs, ks, bs = sub(q), sub(k), sub(b)
    r = bs[:, :, :, :, -1]
    lower = jnp.arange(NS)[:, None] > jnp.arange(NS)[None, :]
    dq = jnp.where(lower[:, :, None, None], bs[:, :, :, :, None] - r[:, :, :, None, :, None], 0.0)
    q_ref = qs[:, :, :, :, None] * jnp.exp(dq)
    k_ref = ks * jnp.exp(r[:, :, :, :, None] - bs)
    a_off = jnp.einsum("bhnijtd,bhnjsd->bhnijts", q_ref, k_ref)
    a_off = jnp.where(lower[:, :, None, None], a_off, 0.0)
    tri = jnp.arange(SUB_B)[:, None] >= jnp.arange(SUB_B)[None, :]
    dd = jnp.where(tri[:, :, None], bs[..., :, None, :] - bs[..., None, :, :], 0.0)
    a_diag = jnp.einsum("bhnitsd,bhnisd->bhnits", qs[..., :, None, :] * jnp.exp(dd), ks)
    a_diag = jnp.where(tri, a_diag, 0.0)
    a_blocks = a_off + jnp.eye(NS, dtype=f32)[:, :, None, None] * a_diag[:, :, :, :, None]
    a_intra = a_blocks.transpose(0, 1, 2, 3, 5, 4, 6).reshape(B, H_B, N, CHUNK_B, CHUNK_B)
    o = o_inter + jnp.einsum("bhnts,bhnse->bhnte", a_intra, v)
    return o.transpose(0, 2, 3, 1, 4).reshape(B, S, H_B, DV_B).astype(out_dtype)


def causal_dwconv(a, w, bias):
    y = lax.conv_general_dilated(a, w[:, None, :].astype(a.dtype), window_strides=(1,),
                                 padding=[(CONV_W - 1, 0)], dimension_numbers=("NWC", "WIO", "NWC"),
                                 feature_group_count=a.shape[-1])
    return y + bias


def setup_inputs(seed: int = 0) -> dict:
    key = jax.random.key(seed)
    ks = jax.random.split(key, 17)
    nrm = lambda k, shape, scale: jax.random.normal(k, shape, jnp.float32) * scale
    return {
        "x": nrm(ks[0], (BATCH, SEQ, D_MODEL), 1.0),
        "ln_mix_g": 1.0 + nrm(ks[1], (DEPTH, D_MODEL), 0.05),
        "w_in": nrm(ks[2], (DEPTH, D_MODEL, IN_W), D_MODEL ** -0.5),
        "q_norm_g": 1.0 + nrm(ks[3], (DEPTH, DH_A), 0.05),
        "k_norm_g": 1.0 + nrm(ks[4], (DEPTH, DH_A), 0.05),
        "sinks": nrm(ks[5], (DEPTH, HQ_A), 1.0),
        "w_gk_up": nrm(ks[6], (DEPTH, GK_RANK, QKB_W), GK_RANK ** -0.5),
        "b_gk": nrm(ks[7], (DEPTH, QKB_W), 0.1),
        "gla_norm_g": 1.0 + nrm(ks[8], (DEPTH, DV_B), 0.05),
        "w_branch_a": nrm(ks[9], (DEPTH, QA_W, D_MODEL), QA_W ** -0.5),
        "w_branch_b": nrm(ks[10], (DEPTH, VB_W, D_MODEL), VB_W ** -0.5),
        "w_out": nrm(ks[11], (DEPTH, D_MODEL, D_MODEL), D_MODEL ** -0.5),
        "ln_ffn_g": 1.0 + nrm(ks[12], (DEPTH, D_MODEL), 0.05),
        "w_ffn_in": nrm(ks[13], (DEPTH, D_MODEL, 2 * D_FF), D_MODEL ** -0.5),
        "conv_w": nrm(ks[14], (DEPTH, CONV_W, D_FF), CONV_W ** -0.5),
        "conv_b": nrm(ks[15], (DEPTH, D_FF), 0.02),
        "w_ffn_out": nrm(ks[16], (DEPTH, D_FF, D_MODEL), D_FF ** -0.5),
    }


def reference(x, ln_mix_g, w_in, q_norm_g, k_norm_g, sinks, w_gk_up, b_gk, gla_norm_g,
              w_branch_a, w_branch_b, w_out, ln_ffn_g, w_ffn_in, conv_w, conv_b, w_ffn_out):
    B, S = x.shape[0], x.shape[1]
    for l in range(DEPTH):
        h = rmsnorm(x, ln_mix_g[l])
        z = h @ w_in[l]
        qa, ka, va, qb, kb, vb, rb, gk_lr, ga, gb = split_cols(z)
        qa = rmsnorm(qa.reshape(B, S, HQ_A, DH_A), q_norm_g[l])
        ka = rmsnorm(ka.reshape(B, S, HKV_A, DH_A), k_norm_g[l])
        ya = sliding_window_sink_attention(qa, ka, va.reshape(B, S, HKV_A, DH_A), sinks[l])
        gk = jax.nn.log_sigmoid((gk_lr @ w_gk_up[l] + b_gk[l]).astype(jnp.float32)) / GATE_NORMALIZER
        yb = gla_chunked(qb.reshape(B, S, H_B, DK_B), kb.reshape(B, S, H_B, DK_B),
                         vb.reshape(B, S, H_B, DV_B), gk.reshape(B, S, H_B, DK_B))
        yb = rmsnorm(yb, gla_norm_g[l]) * jax.nn.silu(rb.reshape(B, S, H_B, DV_B))
        yb = yb.reshape(B, S, VB_W)
        merged = jax.nn.sigmoid(ga) * (ya @ w_branch_a[l]) + jax.nn.sigmoid(gb) * (yb @ w_branch_b[l])
        x = x + merged @ w_out[l]
        h = rmsnorm(x, ln_ffn_g[l])
        a, u = jnp.split(h @ w_ffn_in[l], 2, axis=-1)
        a = causal_dwconv(a, conv_w[l], conv_b[l])
        x = x + (jax.nn.silu(a) * u) @ w_ffn_out[l]
    return x
```

```python
import numpy as np
from contextlib import ExitStack
import concourse.bass as bass
import concourse.mybir as mybir
from concourse.bass_utils import run_bass_kernel_spmd

F32 = mybir.dt.float32
BF16 = mybir.dt.bfloat16
I32 = mybir.dt.int32
AF = mybir.ActivationFunctionType
ALU = mybir.AluOpType

D = 1024
TOK = 2048
G = 512
NG = TOK // G
DEPTH = 2
IN_W = 5904
D_FF = 2816
NJ = D_FF // 128
EPS = 1e-6
C_QA, C_KA, C_VA, C_QB, C_KB, C_VB, C_RB, C_GK, C_GA, C_GB = 0, 512, 640, 768, 1280, 1792, 2816, 3840, 3856, 4880
NSP = 120
HORD = (0, 2, 1, 3)
MSGW = 1024 + 4 + 256 + 128
EPOCH = 12000


import types


def _snap(fn):
    if fn is None or fn.__closure__ is None:
        return fn
    cells = tuple(types.CellType(c.cell_contents) for c in fn.__closure__)
    return types.FunctionType(fn.__code__, fn.__globals__, fn.__name__, fn.__defaults__, cells)


class _Stop(Exception):
    pass


def _bk(B, k):
    return B[k] if isinstance(B, list) else B


class Buf:
    __slots__ = ("name", "w", "r", "excl")

    def __init__(self, name, excl=False):
        self.name, self.w, self.r, self.excl = name, None, [], excl


class Sched:
    ENG = ("pe", "act", "dve", "pool", "sp")

    def __init__(self, nc, es):
        self.nc, self.es = nc, es
        self.prog = {e: [] for e in self.ENG}
        self.cnt = {e: 0 for e in self.ENG}
        self.nsem = 0
        self.sems = {e: self._newsem() for e in self.ENG}
        self.waited = {e: {} for e in self.ENG}
        self.dsem = {}
        self.last_tok = {e: None for e in self.ENG}
        self.dma_toks = {}

    def _newsem(self):
        self.nsem += 1
        return self.es.enter_context(self.nc.semaphore("s%d" % self.nsem))

    def _waits(self, eng, deps):
        waits = []
        for (sem, val, src) in deps:
            if src == eng == "pe":
                continue
            key = id(sem)
            if self.waited[eng].get(key, 0) >= val:
                continue
            self.waited[eng][key] = val
            waits.append((sem, val))
        return waits

    def _deps(self, reads, writes):
        deps = []
        for b in reads:
            if b.excl:
                deps += b.r
            if b.w is not None:
                deps.append(b.w)
        for b in writes:
            deps += b.r
            if b.w is not None:
                deps.append(b.w)
        return deps

    def _commit(self, tok, reads, writes):
        for b in reads:
            if b.excl:
                b.w, b.r = tok, []
            else:
                b.r.append(tok)
        for b in writes:
            b.w, b.r = tok, []

    def op(self, eng, fn, reads=(), writes=()):
        waits = self._waits(eng, self._deps(reads, writes))
        if self.cnt[eng] >= EPOCH:
            self.sems[eng] = self._newsem()
            self.cnt[eng] = 0
        self.cnt[eng] += 1
        sem = self.sems[eng]
        tok = (sem, self.cnt[eng], eng)
        self.prog[eng].append((waits, _snap(fn), sem, 1))
        self.last_tok[eng] = tok
        self._commit(tok, reads, writes)

    def dma(self, eng, fn, semname, reads=(), writes=()):
        waits = self._waits(eng, self._deps(reads, writes))
        if semname not in self.dsem:
            self.dsem[semname] = [self._newsem(), 0]
        s = self.dsem[semname]
        s[1] += 16
        tok = (s[0], s[1], "dma")
        self.prog[eng].append((waits, _snap(fn), s[0], 16))
        self.dma_toks[semname] = tok
        self._commit(tok, reads, writes)

    def cc(self, fn, semname, reads=(), writes=()):
        waits = self._waits("pool", self._deps(reads, writes))
        if semname not in self.dsem:
            self.dsem[semname] = [self._newsem(), 0]
        s = self.dsem[semname]
        s[1] += 1
        tok = (s[0], s[1], "dma")
        self.prog["pool"].append((waits, _snap(fn), s[0], None))
        self.dma_toks[semname] = tok
        self._commit(tok, reads, writes)

    def barrier(self, skip=()):
        toks = [t for t in self.last_tok.values() if t is not None] + \
               [t for n_, t in self.dma_toks.items() if n_ not in skip]
        for e in self.ENG:
            w = self._waits(e, [t for t in toks if t[2] != e])
            if w:
                self.prog[e].append((w, None, None, 0))

    def final_wait(self, eng):
        toks = [t for t in self.last_tok.values() if t is not None] + list(self.dma_toks.values())
        w = self._waits(eng, [t for t in toks if t[2] != eng])
        self.prog[eng].append((w, None, None, 0))

    def emit(self):
        nc = self.nc
        prog = self.prog

        def replay(e, lst):
            for waits, fn, sem, inc in lst:
                for (s, v) in waits:
                    e.wait_ge(s, v)
                if fn is None:
                    continue
                ins = fn(e)
                if inc is None:
                    ins.then_inc(sem)
                else:
                    ins.then_inc(sem, inc)

        with nc.Block() as block:
            @block.tensor
            def _(e):
                replay(e, prog["pe"])

            @block.scalar
            def _(e):
                replay(e, prog["act"])

            @block.vector
            def _(e):
                replay(e, prog["dve"])

            @block.gpsimd
            def _(e):
                replay(e, prog["pool"])

            @block.sync
            def _(e):
                replay(e, prog["sp"])


class Arena:
    def __init__(self, ap_f32, nwords):
        self.base, self.n, self.off = ap_f32, nwords, 0

    def reset(self):
        self.off = 0

    def view(self, off, shape, dtype, parts=128):
        save = self.off
        self.off = off
        ap = self.get(shape, dtype, parts)
        self.off = save
        return ap

    def get(self, shape, dtype, parts=128):
        n = int(np.prod(shape))
        words = n if dtype in (F32, I32) else (n + 1) // 2
        assert self.off + words <= self.n, ("arena overflow", self.off, words, self.n)
        ap = self.base[0:parts, self.off:self.off + words]
        self.off += words
        if dtype != F32:
            ap = ap.bitcast(dtype)
            if dtype == BF16 and n % 2:
                ap = ap[:, 0:n]
        if len(shape) == 2:
            ap = ap.rearrange("p (a b) -> p a b", a=shape[0])
        elif len(shape) == 3:
            ap = ap.rearrange("p (a b c) -> p a b c", a=shape[0], b=shape[1])
        return ap


def build(stop=None):
    nc = bass.Bass("TRN2", target_bir_lowering=False)
    dt = nc.dram_tensor
    x_d = dt("xT", [128, 8, TOK], F32, kind="ExternalInput").ap()
    flags_d = dt("flags", [128, 8], F32, kind="ExternalInput").ap()
    sp_d = dt("sp", [DEPTH, 128, NSP], F32, kind="ExternalInput").ap()
    wup_d = dt("w_gk_up", [DEPTH, 16, 512], F32, kind="ExternalInput").ap()
    win_d = dt("w_in", [DEPTH, D, IN_W], F32, kind="ExternalInput").ap()
    wba_d = dt("w_branch_a", [DEPTH, 512, D], F32, kind="ExternalInput").ap()
    wbb_d = dt("w_branch_b", [DEPTH, D, D], F32, kind="ExternalInput").ap()
    wout_d = dt("w_out", [DEPTH, D, D], F32, kind="ExternalInput").ap()
    wfi_d = dt("w_ffn_in", [DEPTH, D, 2 * D_FF], F32, kind="ExternalInput").ap()
    wfo_d = dt("w_ffn_out", [DEPTH, D_FF, D], F32, kind="ExternalInput").ap()
    y_d = dt("yT", [128, 8, TOK], F32, kind="ExternalOutput").ap()
    cin1 = [dt("cin1_%d" % l, [128, MSGW], F32, kind="Internal").ap() for l in range(DEPTH)]
    cout1 = [dt("cout1_%d" % l, [512, MSGW], F32, kind="Internal").ap() for l in range(DEPTH)]
    cin2 = [dt("cin2_%d" % l, [128, 16], F32, kind="Internal").ap() for l in range(DEPTH)]
    cout2 = [dt("cout2_%d" % l, [512, 16], F32, kind="Internal").ap() for l in range(DEPTH)]
    RG = [[0, 1, 2, 3], [4, 5, 6, 7]]

    with ExitStack() as es:
        S = Sched(nc, es)

        def sb(name, shape, dtype):
            return es.enter_context(nc.sbuf_tensor(name, shape, dtype))

        xT = sb("xTs", [128, 8, TOK], F32)
        oloc = sb("oloc", [128, 8, TOK], BF16)
        qhat = sb("qhat", [128, 4, TOK], BF16)
        B_x = [Buf("x%d" % g) for g in range(NG)]
        B_oloc = [Buf("oloc%d" % g) for g in range(NG)]
        B_qhat = [Buf("qhat%d" % g) for g in range(NG)]
        NSLOT = 6
        wsl = [sb("wslot%d" % i, [128, 2048], BF16) for i in range(NSLOT)]
        B_w = [Buf("w%d" % i) for i in range(NSLOT)]
        ident = sb("ident", [128, 128], BF16)
        ones_dm = sb("ones_dm", [128, 128], BF16)
        ones_dv = sb("ones_dv", [128, 128], BF16)
        bd64 = sb("bd64", [128, 128], BF16)
        ones64 = sb("ones64", [128, 64], BF16)
        ones128 = sb("ones128", [128, 128], BF16)
        maskc = sb("maskc", [128, 512], BF16)
        maskp = sb("maskp", [128, 512], BF16)
        maskp0 = sb("maskp0", [128, 512], BF16)
        onesf = sb("onesf", [128, 1], F32)
        zerof = sb("zerof", [128, 1], F32)
        epsf = sb("epsf", [128, 1], F32)
        flags = sb("flags_s", [128, 8], F32)
        spt = sb("spt", [128, DEPTH, NSP], F32)
        wup = sb("wup", [16, DEPTH, 512], BF16)
        nbgk = sb("nbgk", [128, 4], F32)
        qg8 = sb("qg8", [128, 1], F32)
        es8 = sb("es8", [128, 8], F32)
        esink = sb("esink", [64, 2, 512], F32)
        Sinit_bf = sb("Sinit_bf", [128, 1024], BF16)
        kprev = sb("kprev", [128, 2, 128], BF16)
        vprev = sb("vprev", [128, 128], BF16)
        Sst = sb("Sst", [128, 1024], F32)
        Cend = sb("Cend", [128, 4], F32)
        sc = sb("sc", [128, 64], F32)
        ahalo = sb("ahalo", [128, NJ, 2], F32)
        h2halo = sb("h2halo", [128, 8, 2], BF16)
        B_const = Buf("const")
        B_lay = Buf("layerconst")
        B_Sinit = Buf("Sinit")
        B_kvprev = Buf("kvprev")
        B_S = Buf("S")
        B_Cend = Buf("Cend")
        B_ahalo = Buf("ahalo")
        B_h2halo = Buf("h2halo")
        ARW = 13312
        arena_t = sb("arena", [128, ARW], F32)
        AR = Arena(arena_t, ARW)
        pst = [es.enter_context(nc.psum_tensor("ps%d" % i, [128, 512], F32)) for i in range(8)]
        B_ps = [Buf("ps%d" % i, excl=True) for i in range(8)]
        psrr = [0]

        def psum():
            i = psrr[0] % 8
            psrr[0] += 1
            return pst[i], B_ps[i]

        slot_rr = [0]

        def wload(src_ap, view_fn, parts=128):
            i = slot_rr[0] % NSLOT
            slot_rr[0] += 1
            dst = view_fn(wsl[i])
            S.dma("pool", lambda e, d=dst, s=src_ap: e.dma_start(out=d, in_=s), "w%d" % i, writes=[B_w[i]])
            return wsl[i], B_w[i]

        def wload_multi(pairs):
            i = slot_rr[0] % NSLOT
            slot_rr[0] += 1
            for (src_ap, view_fn) in pairs:
                dst = view_fn(wsl[i])
                S.dma("pool", lambda e, d=dst, s=src_ap: e.dma_start(out=d, in_=s), "w%d" % i, writes=[B_w[i]])
            return wsl[i], B_w[i]

        def v3(t, k, n):
            return t[:, 0:k * n].rearrange("p (k n) -> p k n", k=k)

        def win_cols(l, c0, n):
            return win_d[l].rearrange("(k p) n -> p k n", p=128)[:, :, c0:c0 + n]

        iot_i = AR.get([512], F32).bitcast(I32)
        iot_f = AR.get([512], F32)
        S.op("pool", lambda e: e.iota(iot_i[:], pattern=[[0, 4], [1, 128]], base=0, channel_multiplier=-1),
             writes=[B_const])
        S.op("dve", lambda e: e.tensor_copy(out=iot_f[:], in_=iot_i[:]), reads=[B_const], writes=[B_const])
        S.op("dve", lambda e: e.tensor_single_scalar(out=maskc[:], in_=iot_f[:], scalar=0.0, op=ALU.is_ge),
             writes=[B_const])
        S.op("dve", lambda e: e.tensor_single_scalar(out=maskp[:], in_=iot_f[:], scalar=0.0, op=ALU.is_lt),
             writes=[B_const])
        S.op("dve", lambda e: e.tensor_single_scalar(out=ident[:], in_=iot_f[:, 0:128], scalar=0.0, op=ALU.is_equal),
             writes=[B_const])
        S.op("dve", lambda e: e.memset(ones_dm[:], 1.0 / 1024.0), writes=[B_const])
        S.op("dve", lambda e: e.memset(ones_dv[:], 1.0 / 256.0), writes=[B_const])
        S.op("dve", lambda e: e.memset(bd64[:], 0.0), writes=[B_const])
        S.op("dve", lambda e: e.memset(bd64[0:64, 0:64], 1.0 / 64.0), writes=[B_const])
        S.op("dve", lambda e: e.memset(bd64[64:128, 64:128], 1.0 / 64.0), writes=[B_const])
        S.op("dve", lambda e: e.memset(ones64[:], 1.0), writes=[B_const])
        S.op("dve", lambda e: e.memset(ones128[:], 1.0), writes=[B_const])
        S.op("dve", lambda e: e.memset(onesf[:], 1.0), writes=[B_const])
        S.op("dve", lambda e: e.memset(zerof[:], 0.0), writes=[B_const])
        S.op("dve", lambda e: e.memset(epsf[:], EPS), writes=[B_const])

        def rsqrt_eps(out_ap, in_ap, reads, B_out):
            S.op("act", lambda e: e.activation(out=out_ap, in_=in_ap, func=AF.Ln, bias=epsf[0:out_ap.shape[0], 0:1],
                                               scale=1.0), reads=list(reads) + [B_const], writes=[B_out])
            S.op("act", lambda e: e.activation(out=out_ap, in_=out_ap, func=AF.Exp, scale=-0.5), reads=[B_out],
                 writes=[B_out])
        S.dma("sp", lambda e: e.dma_start(out=flags[:], in_=flags_d), "misc", writes=[B_const])
        S.dma("sp", lambda e: e.dma_start(out=spt[:], in_=sp_d.rearrange("l p n -> p l n")), "misc", writes=[B_const])
        S.dma("pool", lambda e: e.dma_start(out=wup[:], in_=wup_d.rearrange("l r n -> r l n")), "misc2",
              writes=[B_const])
        for g in range(NG):
            S.dma("sp", lambda e, g=g: e.dma_start(out=xT[:, :, g * G:(g + 1) * G], in_=x_d[:, :, g * G:(g + 1) * G]),
                  "xin%d" % g, writes=[B_x[g]])
        S.op("dve", lambda e: e.tensor_scalar(out=maskp0[:], in0=maskp[:], scalar1=flags[:, 7:8], scalar2=None,
                                              op0=ALU.mult), reads=[B_const], writes=[B_const])

        def rms_stats(l, g, sqr, B_sqr, rstd, B_rstd):
            ps, Bp = psum()
            for k in range(8):
                r = k % 2
                S.op("act", lambda e, k=k, r=r: e.activation(out=sqr[:, r, :], in_=xT[:, k, g * G:(g + 1) * G],
                                                             func=AF.Square),
                     reads=[B_x[g]], writes=[B_sqr[r]])
                S.op("pe", lambda e, k=k, r=r: e.matmul(ps[:], lhsT=ones_dm[:], rhs=sqr[:, r, :], start=(k == 0),
                                                        stop=(k == 7)),
                     reads=[B_sqr[r], B_const], writes=[Bp])
            rsqrt_eps(rstd[:], ps[:], [Bp], B_rstd)

        def rms_scale(l, g, gcol, hg, B_hg, rstd, B_rstd):
            for k in range(8):
                S.op("dve", lambda e, k=k: e.scalar_tensor_tensor(out=hg[:, k, :], in0=xT[:, k, g * G:(g + 1) * G],
                                                                  scalar=spt[:, l, gcol + k:gcol + k + 1], in1=rstd[:],
                                                                  op0=ALU.mult, op1=ALU.mult),
                     reads=[B_x[g], B_rstd, B_const], writes=[_bk(B_hg, k)])

        def rmsnorm_group(l, g, gcol, hg, B_hg, sqr, B_sqr, rstd, B_rstd):
            rms_stats(l, g, sqr, B_sqr, rstd, B_rstd)
            rms_scale(l, g, gcol, hg, B_hg, rstd, B_rstd)

        def proj_chunk(wt, Bw, kview_fn, hg, B_hg, nk=8):
            ps, Bp = psum()
            for k in range(nk):
                S.op("pe", lambda e, k=k: e.matmul(ps[:], lhsT=kview_fn(wt, k), rhs=hg[:, k, :], start=(k == 0),
                                                   stop=(k == nk - 1)),
                     reads=[Bw, _bk(B_hg, k)], writes=[Bp])
            return ps, Bp

        def qknorm(ps, Bp, sq, B_sq, rs, B_rs, out_ap, B_out, gain_ap, ncols):
            S.op("act", lambda e: e.activation(out=sq[:, 0:ncols], in_=ps[:, 0:ncols], func=AF.Square), reads=[Bp],
                 writes=[B_sq])
            ps2, Bp2 = psum()
            S.op("pe", lambda e: e.matmul(ps2[:, 0:ncols], lhsT=bd64[:], rhs=sq[:, 0:ncols], start=True, stop=True),
                 reads=[B_sq, B_const], writes=[Bp2])
            rsqrt_eps(rs[:, 0:ncols], ps2[:, 0:ncols], [Bp2], B_rs)
            S.op("dve", lambda e: e.scalar_tensor_tensor(out=out_ap, in0=ps[:, 0:ncols], scalar=gain_ap,
                                                         in1=rs[:, 0:ncols], op0=ALU.mult, op1=ALU.mult),
                 reads=[Bp, B_rs, B_lay, B_const], writes=[B_out])

        def kdup_load(l):
            cols = []
            for hk in range(2):
                for dup in range(2):
                    o = (hk * 2 + dup) * 64
                    cols.append((win_cols(l, C_KA + hk * 64, 64),
                                 lambda t, o=o: v3(t, 8, 256)[:, :, o:o + 64]))
            return wload_multi(cols)

        try:
            for l in range(DEPTH if stop != "init" else 0):
                S.op("dve", lambda e: e.tensor_scalar(out=nbgk[:], in0=spt[:, l, 18:22], scalar1=-1.0, scalar2=None,
                                                      op0=ALU.mult), reads=[B_const], writes=[B_lay])
                S.op("dve", lambda e: e.tensor_scalar(out=qg8[:], in0=spt[:, l, 16:17], scalar1=0.125, scalar2=None,
                                                      op0=ALU.mult), reads=[B_const], writes=[B_lay])
                S.op("act", lambda e: e.activation(out=es8[:], in_=spt[:, l, 112:120], func=AF.Exp), reads=[B_const],
                     writes=[B_lay])
                for hk in range(2):
                    for hh in range(4):
                        S.op("dve", lambda e, hk=hk, hh=hh: e.tensor_copy(
                            out=esink[:, hk, hh * 128:(hh + 1) * 128],
                            in_=es8[0:64, hk * 4 + HORD[hh]:hk * 4 + HORD[hh] + 1].to_broadcast([64, 128])),
                             reads=[B_lay], writes=[B_lay])
                S.op("dve", lambda e: e.memset(Sst[:], 0.0), writes=[B_S])
                S.op("dve", lambda e: e.memset(Cend[:], 0.0), writes=[B_Cend])

                S.barrier()
                AR.reset()
                hg = AR.get([8, G], BF16); B_hg = [Buf("hgA%d" % k_) for k_ in range(8)]
                sqr = AR.get([2, G], BF16); B_sqr = [Buf("sq0"), Buf("sq1")]
                rstd = AR.get([G], F32); B_rstd = Buf("rstd")
                qt = AR.get([4, G], BF16); B_qt = Buf("qt")
                kt = AR.get([4, G], BF16); B_kt = Buf("kt")
                vtok = AR.get([4, 1024], BF16); B_vtok = Buf("vtok")
                lsp = AR.get([G], F32); B_lsp = Buf("lsp")
                Cc = AR.get([G], F32); B_Cc = Buf("Cc")
                E1 = AR.get([G], F32); B_E1 = Buf("E1")
                E2 = AR.get([G], F32); B_E2 = Buf("E2")
                gklr = AR.get([G], BF16); B_gklr = Buf("gklr")
                ATs = AR.get([2, G], BF16); B_AT = [Buf("AT0"), Buf("AT1")]
                ktok = AR.get([2, G], BF16); B_ktok = [Buf("ktok0"), Buf("ktok1")]
                U = AR.get([1024], F32); B_U = Buf("U")
                Ubf = AR.get([1024], BF16); B_Ubf = Buf("Ubf")
                dS = AR.get([4], F32); B_dS = Buf("dS")
                msg = AR.get([MSGW - 1024], F32); B_msg = Buf("msg")
                sqk = AR.get([G], BF16); B_sqk = Buf("sqk")
                rsk = AR.get([G], F32); B_rsk = Buf("rsk")

                for g in range(NG):
                    if g == 0:
                        rms_stats(l, g, sqr, B_sqr, rstd, B_rstd)
                        rms_scale(l, g, 0, hg, B_hg, rstd, B_rstd)
                    wt, Bw = wload(win_cols(l, C_GK, 128), lambda t: v3(t, 8, 128))
                    ps, Bp = psum()
                    for k in range(8):
                        S.op("pe", lambda e, k=k, wt=wt: e.matmul(ps[:, :], lhsT=v3(wt, 8, 128)[:, k, :], rhs=hg[:, k, :],
                                                                  start=(k == 0), stop=(k == 7)),
                             reads=[Bw, _bk(B_hg, k)], writes=[Bp])
                    S.op("act", lambda e, ps=ps: e.activation(out=gklr[0:16, :], in_=ps[0:16, :], func=AF.Copy),
                         reads=[Bp], writes=[B_gklr])
                    wqs = [wload(win_cols(l, C_QB + hp_ * 256, 256), lambda t: v3(t, 8, 256)) for hp_ in range(2)]
                    wks = [wload(win_cols(l, C_KB + hp_ * 256, 256), lambda t: v3(t, 8, 256)) for hp_ in range(2)]
                    def v_unit(q4, half, wv, Bwv):
                        ps, Bp = psum()
                        for bb in range(2):
                            blk = half * 2 + bb
                            for k in range(8):
                                S.op("pe", lambda e, k=k, blk=blk, bb=bb, ps=ps, wv=wv: e.matmul(
                                    ps[:, bb * 256:(bb + 1) * 256], lhsT=hg[:, k, blk * 128:(blk + 1) * 128],
                                    rhs=v3(wv, 8, 256)[:, k, :], start=(k == 0), stop=(k == 7)),
                                     reads=[Bwv, _bk(B_hg, k)], writes=[Bp])
                        S.op("act", lambda e, half=half, q4=q4, ps=ps: e.activation(
                            out=vtok[:, half * 2:half * 2 + 2, q4 * 256:(q4 + 1) * 256],
                            in_=ps[:].rearrange("p (b n) -> p b n", b=2), func=AF.Copy),
                             reads=[Bp], writes=[B_vtok])
                    for h in range(4):
                        psg, Bpg = psum()
                        S.op("pe", lambda e, h=h, psg=psg: e.matmul(psg[:], lhsT=wup[0:16, l, h * 128:(h + 1) * 128],
                                                                    rhs=gklr[0:16, :], start=True, stop=True),
                             reads=[B_gklr, B_const], writes=[Bpg])
                        S.op("act", lambda e, h=h, psg=psg: e.activation(out=lsp[:], in_=psg[:], func=AF.Exp,
                                                                         bias=nbgk[:, h:h + 1], scale=-1.0),
                             reads=[Bpg, B_lay], writes=[B_lsp])
                        S.op("act", lambda e: e.activation(out=lsp[:], in_=lsp[:], func=AF.Ln, bias=onesf[:, 0:1], scale=1.0),
                             reads=[B_lsp, B_const], writes=[B_lsp])
                        S.op("dve", lambda e, h=h: e.tensor_copy(out=sc[:, 0:1], in_=Cend[:, h:h + 1]), reads=[B_Cend],
                             writes=[B_dS])
                        S.op("dve", lambda e, h=h: e.tensor_tensor_scan(out=Cc[:], data0=onesf[:, 0:1].to_broadcast([128, G]),
                                                                        data1=lsp[:], initial=sc[:, 0:1], op0=ALU.mult,
                                                                        op1=ALU.add),
                             reads=[B_lsp, B_dS, B_const], writes=[B_Cc])
                        S.op("dve", lambda e, h=h: e.tensor_copy(out=Cend[:, h:h + 1], in_=Cc[:, G - 1:G]), reads=[B_Cc],
                             writes=[B_Cend])
                        S.op("dve", lambda e: e.tensor_scalar(out=sc[:, 1:2], in0=Cc[:, 255:256], scalar1=1.0 / 16.0,
                                                              scalar2=None, op0=ALU.mult), reads=[B_Cc], writes=[B_dS])
                        S.op("dve", lambda e: e.tensor_scalar(out=sc[:, 2:3], in0=Cc[:, 255:256], scalar1=-1.0 / 16.0,
                                                              scalar2=None, op0=ALU.mult), reads=[B_Cc], writes=[B_dS])
                        S.op("act", lambda e: e.activation(out=E1[:], in_=Cc[:], func=AF.Exp, bias=sc[:, 1:2],
                                                           scale=-1.0 / 16.0), reads=[B_Cc, B_dS], writes=[B_E1])
                        S.op("act", lambda e: e.activation(out=E2[:], in_=Cc[:], func=AF.Exp, bias=sc[:, 2:3],
                                                           scale=1.0 / 16.0), reads=[B_Cc, B_dS], writes=[B_E2])
                        S.op("act", lambda e: e.activation(out=sc[:, 3:4], in_=sc[:, 0:1], func=AF.Exp, bias=sc[:, 2:3],
                                                           scale=1.0 / 16.0), reads=[B_dS], writes=[B_dS])
                        S.op("act", lambda e: e.activation(out=sc[:, 4:5], in_=Cc[:, 255:256], func=AF.Exp,
                                                           scale=-1.0 / 16.0), reads=[B_Cc, B_dS], writes=[B_dS])
                        S.op("dve", lambda e, h=h: e.tensor_copy(out=dS[:, h:h + 1], in_=E1[:, G - 1:G]), reads=[B_E1],
                             writes=[B_dS])
                        wq, Bwq = wqs[h // 2]
                        wk, Bwk = wks[h // 2]
                        psq, Bpq = proj_chunk(wq, Bwq, lambda t, k, h=h: v3(t, 8, 256)[:, k, (h % 2) * 128:(h % 2 + 1) * 128], hg,
                                              B_hg)
                        S.op("dve", lambda e, h=h, psq=psq: e.scalar_tensor_tensor(out=qt[:, h, :], in0=psq[:],
                                                                                   scalar=128.0 ** -0.5, in1=E1[:],
                                                                                   op0=ALU.mult, op1=ALU.mult),
                             reads=[Bpq, B_E1], writes=[B_qt])
                        psk, Bpk = proj_chunk(wk, Bwk, lambda t, k, h=h: v3(t, 8, 256)[:, k, (h % 2) * 128:(h % 2 + 1) * 128], hg,
                                              B_hg)
                        S.op("dve", lambda e, h=h, psk=psk: e.tensor_tensor(out=kt[:, h, :], in0=psk[:], in1=E2[:],
                                                                            op=ALU.mult),
                             reads=[Bpk, B_E2], writes=[B_kt])
                        S.op("dve", lambda e, h=h: e.tensor_scalar(out=qhat[:, h, g * G:(g + 1) * G], in0=qt[:, h, :],
                                                                    scalar1=sc[:, 4:5], scalar2=None, op0=ALU.mult),
                             reads=[B_qt, B_dS], writes=[B_qhat[g]])
                        S.op("dve", lambda e, h=h: e.tensor_scalar(out=U[:, h * 256:(h + 1) * 256],
                                                                   in0=Sst[:, h * 256:(h + 1) * 256], scalar1=sc[:, 3:4],
                                                                   scalar2=None, op0=ALU.mult),
                             reads=[B_S, B_dS], writes=[B_U])
                    S.op("act", lambda e: e.activation(out=Ubf[:], in_=U[:], func=AF.Copy), reads=[B_U], writes=[B_Ubf])
                    for q4 in range(4):
                        wv, Bwv = wload(win_cols(l, C_VB + q4 * 256, 256), lambda t: v3(t, 8, 256))
                        for half in range(2):
                            v_unit(q4, half, wv, Bwv)
                    def blk_stage1(j, par):
                        tsl = slice(j * 128, (j + 1) * 128)
                        psA, BpA = psum()
                        for h in range(4):
                            S.op("pe", lambda e, h=h, psA=psA, tsl=tsl: e.matmul(psA[:, h * 128:(h + 1) * 128],
                                                                                 lhsT=kt[:, h, tsl], rhs=qt[:, h, tsl],
                                                                                 start=True, stop=True),
                                 reads=[B_kt, B_qt], writes=[BpA])
                        S.op("dve", lambda e, psA=psA, par=par: e.tensor_tensor(out=ATs[:, par, :], in0=psA[:], in1=maskc[:],
                                                                                op=ALU.mult),
                             reads=[BpA, B_const], writes=[B_AT[par]])
                        psT, BpT = psum()
                        psTb = psT[:, 0:256].bitcast(BF16)
                        for h in range(4):
                            S.op("pe", lambda e, h=h, psTb=psTb, tsl=tsl: e.transpose(psTb[:, h * 128:(h + 1) * 128],
                                                                                      kt[:, h, tsl], ident[:]),
                                 reads=[B_kt, B_const], writes=[BpT])
                        S.op("act", lambda e, psTb=psTb, par=par: e.activation(out=ktok[:, par, :], in_=psTb, func=AF.Copy),
                             reads=[BpT], writes=[B_ktok[par]])

                    def blk_stage2(j, par):
                        tsl = slice(j * 128, (j + 1) * 128)
                        kvs = []
                        for hp in range(2):
                            pskv, Bpkv = psum()
                            for hh in range(2):
                                h = hp * 2 + hh
                                S.op("pe", lambda e, pskv=pskv, hh=hh, h=h, j=j, par=par: e.matmul(
                                    pskv[:, hh * 256:(hh + 1) * 256], lhsT=ktok[:, par, h * 128:(h + 1) * 128],
                                    rhs=vtok[:, j, h * 256:(h + 1) * 256], start=True, stop=True),
                                     reads=[B_ktok[par], B_vtok], writes=[Bpkv])
                            kvs.append((pskv, Bpkv))
                        for hp in range(2):
                            pso, Bpo = psum()
                            for hh in range(2):
                                h = hp * 2 + hh
                                for c in range(2):
                                    oc = (hh * 2 + c) * 128
                                    ec = h * 256 + c * 128
                                    S.op("pe", lambda e, pso=pso, oc=oc, ec=ec, h=h, j=j, par=par: e.matmul(
                                        pso[:, oc:oc + 128], lhsT=vtok[:, j, ec:ec + 128],
                                        rhs=ATs[:, par, h * 128:(h + 1) * 128], start=True, stop=False),
                                         reads=[B_vtok, B_AT[par]], writes=[Bpo])
                                    S.op("pe", lambda e, pso=pso, oc=oc, ec=ec, h=h, tsl=tsl: e.matmul(
                                        pso[:, oc:oc + 128], lhsT=Ubf[:, ec:ec + 128], rhs=qt[:, h, tsl],
                                        start=False, stop=True), reads=[B_Ubf, B_qt], writes=[Bpo])
                            S.op("act", lambda e, pso=pso, hp=hp, j=j: e.activation(
                                out=oloc[:, hp * 4:hp * 4 + 4, g * G + j * 128:g * G + (j + 1) * 128],
                                in_=pso[:].rearrange("p (c t) -> p c t", c=4), func=AF.Copy),
                                 reads=[Bpo], writes=[B_oloc[g]])
                        for hp in range(2):
                            pskv, Bpkv = kvs[hp]
                            S.op("dve", lambda e, pskv=pskv, hp=hp: e.tensor_tensor(
                                out=U[:, hp * 512:(hp + 1) * 512], in0=pskv[:], in1=U[:, hp * 512:(hp + 1) * 512],
                                op=ALU.add), reads=[Bpkv, B_U], writes=[B_U])
                        S.op("act", lambda e: e.activation(out=Ubf[:], in_=U[:], func=AF.Copy), reads=[B_U], writes=[B_Ubf])

                    blk_stage1(0, 0)
                    for j in range(4):
                        if j + 1 < 4:
                            blk_stage1(j + 1, (j + 1) % 2)
                        blk_stage2(j, j % 2)
                        if j == 0 and g + 1 < NG:
                            rms_stats(l, g + 1, sqr, B_sqr, rstd, B_rstd)
                        if j == 1 and g + 1 < NG:
                            rms_scale(l, g + 1, 0, hg, B_hg, rstd, B_rstd)
                    for h in range(4):
                        S.op("dve", lambda e, h=h: e.tensor_scalar(out=Sst[:, h * 256:(h + 1) * 256],
                                                                   in0=U[:, h * 256:(h + 1) * 256], scalar1=dS[:, h:h + 1],
                                                                   scalar2=None, op0=ALU.mult),
                             reads=[B_U, B_dS], writes=[B_S])
                    if g == NG - 1:
                        wkd, Bwkd = kdup_load(l)
                        for hk in range(2):
                            ps, Bp = psum()
                            for k in range(8):
                                S.op("pe", lambda e, k=k, hk=hk, ps=ps, wkd=wkd: e.matmul(
                                    ps[:, 0:128], lhsT=v3(wkd, 8, 256)[:, k, hk * 128:(hk + 1) * 128],
                                    rhs=hg[:, k, 384:512], start=(k == 0), stop=(k == 7)), reads=[Bwkd, _bk(B_hg, k)], writes=[Bp])
                            qknorm(ps, Bp, sqk, B_sqk, rsk, B_rsk, msg[:, 4 + hk * 128:4 + (hk + 1) * 128], B_msg,
                                   spt[:, l, 17:18], 128)
                        wv, Bwv = wload(win_cols(l, C_VA, 128), lambda t: v3(t, 8, 128))
                        ps, Bp = psum()
                        for k in range(8):
                            S.op("pe", lambda e, k=k, ps=ps, wv=wv: e.matmul(ps[:, 0:128], lhsT=hg[:, k, 384:512],
                                                                             rhs=v3(wv, 8, 128)[:, k, :], start=(k == 0),
                                                                             stop=(k == 7)), reads=[Bwv, _bk(B_hg, k)], writes=[Bp])
                        S.op("act", lambda e, ps=ps: e.activation(out=msg[:, 260:388], in_=ps[:, 0:128], func=AF.Copy),
                             reads=[Bp], writes=[B_msg])
                if stop == ("A", l):
                    break
                S.op("act", lambda e: e.activation(out=msg[:, 0:4], in_=Cend[:], func=AF.Exp, scale=-1.0 / 16.0),
                     reads=[B_Cend], writes=[B_msg])
                B_cin = Buf("cin")
                B_cout = Buf("cout")
                S.dma("sp", lambda e: e.dma_start(out=cin1[l][:, 0:1024], in_=Sst[:]), "cin", reads=[B_S], writes=[B_cin])
                S.dma("sp", lambda e: e.dma_start(out=cin1[l][:, 1024:MSGW], in_=msg[:]), "cin", reads=[B_msg],
                      writes=[B_cin])
                S.cc(lambda e: e.collective_compute("AllGather", ALU.bypass, replica_groups=RG, ins=[cin1[l].opt()],
                                                    outs=[cout1[l].opt()]), "cc", reads=[B_cin], writes=[B_cout])
                S.barrier(skip=("cc",))
                AR.reset()
                hg = AR.get([8, G], BF16); B_hg = [Buf("hgB%d" % k_) for k_ in range(8)]
                sqr = AR.get([2, G], BF16); B_sqr = [Buf("sq0B"), Buf("sq1B")]
                rstd = AR.get([G], F32); B_rstd = Buf("rstdB")
                qn_off = AR.off
                qn = AR.get([4, G], BF16); B_qn = Buf("qn")
                kdup = AR.get([2, 128 + G], BF16); B_kdup = Buf("kdup")
                vaug = AR.get([5, 2, 128], BF16); B_vaug = Buf("vaug")
                pT_off = AR.off
                pT = AR.get([2, 2, G], BF16); B_pT = [[Buf("pT00"), Buf("pT01")], [Buf("pT10"), Buf("pT11")]]
                ya = AR.get([8, G], BF16, parts=64); B_ya = Buf("ya")
                ofp = AR.get([2, G], F32); B_ofp = Buf("ofp")
                merged_off = AR.off
                merged = AR.get([8, G], BF16); B_merged = Buf("merged")
                sr = AR.view(merged_off, [2, G], BF16); B_sr = B_merged
                sa = AR.get([G], F32); B_sa = Buf("sa")
                sbb = AR.get([G], F32); B_sbb = Buf("sbb")
                sqk = sqr[:, 0, :]; B_sqk = B_sqr[0]
                rsk = rstd; B_rsk = B_rstd
                dn = AR.get([G], F32, parts=64); B_dn = Buf("dn")
                ofp2 = AR.view(qn_off, [2, G], F32)
                sr2 = AR.view(pT_off, [2, G], BF16)
                sqr2 = AR.view(pT_off + 512, [2, G], BF16)
                GF = [dict(ofp=ofp, Bofp=[B_ofp], sr=sr, Bsr=[B_sr], sq=sqr, Bsq=[[B_sqr[0]], [B_sqr[1]]], rs=rstd,
                           Brs=[B_rstd]),
                      dict(ofp=ofp2, Bofp=[B_qn], sr=sr2, Bsr=[B_pT[0][0], B_pT[0][1]],
                           sq=sqr2, Bsq=[[B_pT[1][0]], [B_pT[1][1]]], rs=sa, Brs=[B_sa])]

                S.op("dve", lambda e: e.memset(vaug[:], 0.0), writes=[B_vaug])
                XOFF = ARW - (4 * MSGW + 1024 + 384)
                assert XOFF >= pT_off
                gath = AR.view(XOFF, [4, MSGW], F32); B_gath = Buf("gath")
                acc = AR.view(XOFF + 4 * MSGW, [1024], F32); B_acc = Buf("acc")
                kvacc = AR.view(XOFF + 4 * MSGW + 1024, [384], F32); B_kvacc = Buf("kvacc")
                fct = sc[:, 8:24]; B_fct = Buf("fct")
                S.dma("sp", lambda e: e.dma_start(out=gath[:], in_=cout1[l].rearrange("(j p) n -> p j n", p=128)), "gath",
                      reads=[B_cout], writes=[B_gath])

                def exchange_compute():
                    S.op("dve", lambda e: e.memset(acc[:], 0.0), writes=[B_acc])
                    for jr in range(3):
                        mj = flags[:, 4 + jr:5 + jr]
                        S.op("dve", lambda e, jr=jr, mj=mj: e.tensor_scalar(out=fct[:, 0:4], in0=gath[:, jr, 1024:1028],
                                                                            scalar1=-1.0, scalar2=mj, op0=ALU.add,
                                                                            op1=ALU.mult), reads=[B_gath, B_const],
                             writes=[B_fct])
                        S.op("dve", lambda e: e.tensor_scalar(out=fct[:, 0:4], in0=fct[:, 0:4], scalar1=1.0, scalar2=None,
                                                              op0=ALU.add), reads=[B_fct], writes=[B_fct])
                        for h in range(4):
                            hs = slice(h * 256, (h + 1) * 256)
                            S.op("dve", lambda e, h=h, hs=hs: e.tensor_scalar(out=acc[:, hs], in0=acc[:, hs],
                                                                              scalar1=fct[:, h:h + 1], scalar2=None,
                                                                              op0=ALU.mult), reads=[B_fct, B_acc],
                                 writes=[B_acc])
                            S.op("dve", lambda e, jr=jr, hs=hs, mj=mj: e.scalar_tensor_tensor(out=acc[:, hs],
                                                                                              in0=gath[:, jr, hs], scalar=mj,
                                                                                              in1=acc[:, hs], op0=ALU.mult,
                                                                                              op1=ALU.add),
                                 reads=[B_gath, B_acc, B_const], writes=[B_acc])
                    S.op("act", lambda e: e.activation(out=Sinit_bf[:], in_=acc[:], func=AF.Copy), reads=[B_acc],
                         writes=[B_Sinit])
                    S.op("dve", lambda e: e.tensor_scalar(out=kvacc[:], in0=gath[:, 0, 1028:1412], scalar1=flags[:, 0:1],
                                                          scalar2=None, op0=ALU.mult), reads=[B_gath, B_const],
                         writes=[B_kvacc])
                    for jr in range(1, 4):
                        S.op("dve", lambda e, jr=jr: e.scalar_tensor_tensor(out=kvacc[:], in0=gath[:, jr, 1028:1412],
                                                                            scalar=flags[:, jr:jr + 1], in1=kvacc[:],
                                                                            op0=ALU.mult, op1=ALU.add),
                             reads=[B_gath, B_kvacc, B_const], writes=[B_kvacc])
                    S.op("act", lambda e: e.activation(out=kprev[:], in_=kvacc[:, 0:256].rearrange("p (a b) -> p a b", a=2),
                                                       func=AF.Copy), reads=[B_kvacc], writes=[B_kvprev])
                    S.op("act", lambda e: e.activation(out=vprev[:], in_=kvacc[:, 256:384], func=AF.Copy), reads=[B_kvacc],
                         writes=[B_kvprev])
                    S.op("dve", lambda e: e.memset(sc[:, 30:31], 0.0), reads=[B_Sinit, B_kvprev],
                         writes=[B_gath, B_acc, B_kvacc, B_pT[0][0], B_pT[0][1], B_pT[1][0], B_pT[1][1], B_ya, B_ofp, B_merged,
                                 B_sa, B_sbb, B_dn])

                if stop == ("X", l):
                    break
                def outproj_chunk(gp, j, st_):
                    gsp = slice(gp * G, (gp + 1) * G)
                    if j % 2 == 0:
                        jj = j // 2
                        st_["w"] = wload(wout_d[l].rearrange("(k p) n -> p k n", p=128)[:, :, jj * 256:(jj + 1) * 256],
                                         lambda t: v3(t, 8, 256))
                    wo, Bwo = st_["w"]
                    c = j % 2
                    ps, Bp = proj_chunk(wo, Bwo, lambda t, k, c=c: v3(t, 8, 256)[:, k, c * 128:(c + 1) * 128], merged,
                                        B_merged)
                    S.op("dve", lambda e, ps=ps, j=j, gsp=gsp: e.tensor_tensor(out=xT[:, j, gsp], in0=ps[:],
                                                                               in1=xT[:, j, gsp], op=ALU.add),
                         reads=[Bp, B_x[gp]], writes=[B_x[gp]])

                pend_out = None
                for g in range(NG):
                    gs = slice(g * G, (g + 1) * G)
                    if g == 0:
                        rms_stats(l, g, sqr, B_sqr, rstd, B_rstd)
                    rms_scale(l, g, 0, hg, B_hg, rstd, B_rstd)
                    wqs = [wload(win_cols(l, C_QA + hp_ * 256, 256), lambda t: v3(t, 8, 256)) for hp_ in range(2)]
                    wkd, Bwkd = kdup_load(l)
                    if g > 0:
                        S.op("dve", lambda e: e.tensor_copy(out=kdup[:, :, 0:128], in_=kdup[:, :, G:G + 128]),
                             reads=[B_kdup], writes=[B_kdup])
                        S.op("dve", lambda e: e.tensor_copy(out=vaug[:, 0, :, :], in_=vaug[:, 4, :, :]), reads=[B_vaug],
                             writes=[B_vaug])
                    units = []
                    for c in range(4):
                        units.append((wqs[c // 2], (c % 2) * 128, qn[:, c, :], B_qn, qg8[:, 0:1]))
                    for hk in range(2):
                        units.append(((wkd, Bwkd), hk * 128, kdup[:, hk, 128:128 + G], B_kdup, spt[:, l, 17:18]))
                    prev_u = None
                    for u_ in units + [None]:
                        cur_u = None
                        if u_ is not None:
                            (wt_, Bwt_), co_, out_, Bout_, gain_ = u_
                            ps_, Bp_ = proj_chunk(wt_, Bwt_, lambda t, k, co_=co_: v3(t, 8, 256)[:, k, co_:co_ + 128], hg, B_hg)
                            cur_u = (ps_, Bp_, out_, Bout_, gain_)
                        if prev_u is not None:
                            qknorm(prev_u[0], prev_u[1], sqk, B_sqk, rsk, B_rsk, prev_u[2], prev_u[3], prev_u[4], G)
                        prev_u = cur_u
                    wv, Bwv = wload(win_cols(l, C_VA, 128), lambda t: v3(t, 8, 128))
                    ps, Bp = psum()
                    for blk in range(4):
                        for k in range(8):
                            S.op("pe", lambda e, k=k, blk=blk, ps=ps, wv=wv: e.matmul(
                                ps[:, blk * 128:(blk + 1) * 128], lhsT=hg[:, k, blk * 128:(blk + 1) * 128],
                                rhs=v3(wv, 8, 128)[:, k, :], start=(k == 0), stop=(k == 7)), reads=[Bwv, _bk(B_hg, k)], writes=[Bp])
                    S.op("act", lambda e, ps=ps: e.activation(
                        out=vaug[:, 1:5, :, 0:64], in_=ps[:].rearrange("p (b h d) -> p b h d", b=4, h=2), func=AF.Copy),
                         reads=[Bp], writes=[B_vaug])
                    if g == 0:
                        exchange_compute()
                        S.op("dve", lambda e: e.tensor_copy(out=kdup[:, :, 0:128], in_=kprev[:]), reads=[B_kvprev],
                             writes=[B_kdup])
                        for hk in range(2):
                            S.op("dve", lambda e, hk=hk: e.tensor_copy(out=vaug[:, 0, hk, 0:64],
                                                                        in_=vprev[:, hk * 64:(hk + 1) * 64]),
                                 reads=[B_kvprev], writes=[B_vaug])
                    if stop == ("B1", l):
                        raise _Stop()
                    def attn_stage1(i, hk, par):
                        psE, BpE = psum()
                        psO, BpO = psum()
                        for kb in range(2):
                            koff = i * 128 + kb * 128
                            for b in range(4):
                                hh = HORD[b]
                                chunk = 2 * hk + hh // 2
                                p0 = 64 * (hh % 2)
                                pst_, Bpt_ = (psE, BpE) if b < 2 else (psO, BpO)
                                cb = kb * 256 + (b % 2) * 128
                                S.op("pe", lambda e, pst_=pst_, cb=cb, chunk=chunk, p0=p0, koff=koff, hk=hk, i=i: e.matmul(
                                    pst_[:, cb:cb + 128], lhsT=kdup[p0:p0 + 64, hk, koff:koff + 128],
                                    rhs=qn[p0:p0 + 64, chunk, i * 128:(i + 1) * 128], start=True, stop=True),
                                     reads=[B_kdup, B_qn], writes=[Bpt_])
                        S.op("act", lambda e, psE=psE, par=par: e.activation(
                            out=pT[:, par, :, 0:256], in_=psE[:].rearrange("p (k n) -> p k n", k=2), func=AF.Exp),
                             reads=[BpE], writes=[B_pT[par][0], B_pT[par][1]])
                        S.op("act", lambda e, psO=psO, par=par: e.activation(
                            out=pT[:, par, :, 256:512], in_=psO[:].rearrange("p (k n) -> p k n", k=2), func=AF.Exp),
                             reads=[BpO], writes=[B_pT[par][0], B_pT[par][1]])
                        for kb in range(2):
                            if kb == 1:
                                mk = maskc
                            else:
                                mk = maskp0 if (g == 0 and i == 0) else maskp
                            S.op("dve", lambda e, kb=kb, mk=mk, par=par: e.tensor_tensor(
                                out=pT[:, par, kb, :], in0=pT[:, par, kb, :], in1=mk[:], op=ALU.mult),
                                 reads=[B_pT[par][kb], B_const], writes=[B_pT[par][kb]])

                    dnb = [(dn, B_dn), (sbb[0:64, :], B_sbb)]

                    def attn_stage2a(i, hk, par):
                        pso, Bpo = psum()
                        psd, Bpd = psum()
                        dn_, Bdn_ = dnb[par]
                        for kb in range(2):
                            S.op("pe", lambda e, pso=pso, kb=kb, hk=hk, i=i, par=par: e.matmul(
                                pso[:, :], lhsT=vaug[:, i + kb, hk, :], rhs=pT[:, par, kb, :], start=(kb == 0),
                                stop=(kb == 1)), reads=[B_vaug, B_pT[par][kb]], writes=[Bpo])
                        for kb in range(2):
                            S.op("pe", lambda e, psd=psd, kb=kb, par=par: e.matmul(
                                psd[:, :], lhsT=ones128[:], rhs=pT[:, par, kb, :], start=(kb == 0), stop=(kb == 1)),
                                 reads=[B_const, B_pT[par][kb]], writes=[Bpd])
                        S.op("dve", lambda e, psd=psd, hk=hk, dn_=dn_: e.tensor_tensor(out=dn_, in0=psd[0:64, :],
                                                                                       in1=esink[:, hk, :], op=ALU.add),
                             reads=[Bpd, B_lay], writes=[Bdn_])
                        return pso, Bpo

                    def attn_stage2b(i, hk, par, pso, Bpo):
                        dn_, Bdn_ = dnb[par]
                        S.op("act", lambda e, dn_=dn_: e.activation(out=dn_, in_=dn_, func=AF.Ln), reads=[Bdn_],
                             writes=[Bdn_])
                        S.op("act", lambda e, dn_=dn_: e.activation(out=dn_, in_=dn_, func=AF.Exp, scale=-1.0),
                             reads=[Bdn_], writes=[Bdn_])
                        S.op("dve", lambda e, pso=pso, hk=hk, i=i, dn_=dn_: e.tensor_tensor(
                            out=ya[:, hk * 4:(hk + 1) * 4, i * 128:(i + 1) * 128],
                            in0=pso[0:64, :].rearrange("p (h q) -> p h q", h=4),
                            in1=dn_.rearrange("p (h q) -> p h q", h=4), op=ALU.mult),
                             reads=[Bpo, Bdn_], writes=[B_ya])

                    its = [(i, hk) for i in range(4) for hk in range(2)]
                    attn_stage1(its[0][0], its[0][1], 0)
                    pend = None
                    for n_, (i, hk) in enumerate(its):
                        if n_ + 1 < len(its):
                            attn_stage1(its[n_ + 1][0], its[n_ + 1][1], (n_ + 1) % 2)
                        pso_, Bpo_ = attn_stage2a(i, hk, n_ % 2)
                        if pend is not None:
                            attn_stage2b(*pend)
                        pend = (i, hk, n_ % 2, pso_, Bpo_)
                        if pend_out is not None:
                            outproj_chunk(pend_out[0], n_, pend_out[1])
                    attn_stage2b(*pend)
                    pend_out = None
                    if stop == ("B2", l):
                        raise _Stop()
                    def gla_p1(hp):
                        for h in (2 * hp, 2 * hp + 1):
                            F_ = GF[h % 2]
                            wr, Bwr = wload(win_cols(l, C_RB + h * 256, 256), lambda t: v3(t, 8, 256))
                            for c in range(2):
                                psc, Bpc = psum()
                                ec = h * 256 + c * 128
                                S.op("pe", lambda e, psc=psc, ec=ec, h=h: e.matmul(psc[:], lhsT=Sinit_bf[:, ec:ec + 128],
                                                                                   rhs=qhat[:, h, gs], start=True, stop=True),
                                     reads=[B_Sinit, B_qhat[g]], writes=[Bpc])
                                S.op("dve", lambda e, psc=psc, c=c, h=h, F_=F_: e.tensor_tensor(
                                    out=F_["ofp"][:, c, :], in0=psc[:], in1=oloc[:, h * 2 + c, gs], op=ALU.add),
                                     reads=[Bpc, B_oloc[g]], writes=F_["Bofp"])
                                S.op("act", lambda e, c=c, F_=F_: e.activation(out=F_["sq"][:, c, :], in_=F_["ofp"][:, c, :],
                                                                               func=AF.Square),
                                     reads=F_["Bofp"], writes=F_["Bsq"][c])
                            for c in range(2):
                                psr, Bpr = proj_chunk(wr, Bwr, lambda t, k, c=c: v3(t, 8, 256)[:, k, c * 128:(c + 1) * 128],
                                                      hg, B_hg)
                                S.op("act", lambda e, c=c, psr=psr, F_=F_: e.activation(out=F_["sr"][:, c, :], in_=psr[:],
                                                                                        func=AF.Silu),
                                     reads=[Bpr], writes=F_["Bsr"])

                    def gla_p2(hp):
                        for h in (2 * hp, 2 * hp + 1):
                            F_ = GF[h % 2]
                            pss, Bpss = psum()
                            for c in range(2):
                                S.op("pe", lambda e, c=c, pss=pss, F_=F_: e.matmul(pss[:], lhsT=ones_dv[:],
                                                                                   rhs=F_["sq"][:, c, :], start=(c == 0),
                                                                                   stop=(c == 1)),
                                     reads=F_["Bsq"][c] + [B_const], writes=[Bpss])
                            S.op("act", lambda e, pss=pss, F_=F_: e.activation(out=F_["rs"][:], in_=pss[:], func=AF.Ln,
                                                                               bias=epsf[:, 0:1], scale=1.0),
                                 reads=[Bpss, B_const], writes=F_["Brs"])
                            S.op("act", lambda e, F_=F_: e.activation(out=F_["rs"][:], in_=F_["rs"][:], func=AF.Exp,
                                                                      scale=-0.5), reads=F_["Brs"], writes=F_["Brs"])
                            for c in range(2):
                                S.op("dve", lambda e, c=c, F_=F_: e.scalar_tensor_tensor(
                                    out=F_["ofp"][:, c, :], in0=F_["ofp"][:, c, :], scalar=spt[:, l, 22 + c:23 + c],
                                    in1=F_["rs"][:], op0=ALU.mult, op1=ALU.mult),
                                     reads=F_["Bofp"] + F_["Brs"] + [B_const], writes=F_["Bofp"])
                                S.op("dve", lambda e, c=c, h=h, F_=F_: e.tensor_tensor(
                                    out=oloc[:, h * 2 + c, gs], in0=F_["ofp"][:, c, :], in1=F_["sr"][:, c, :], op=ALU.mult),
                                     reads=F_["Bofp"] + F_["Bsr"], writes=[B_oloc[g]])
                    def merge_load(st2):
                        c2 = slice(st2 * 256, (st2 + 1) * 256)
                        wA, BwA = wload(wba_d[l].rearrange("(h p) n -> p h n", p=64)[:, :, c2],
                                        lambda t: t[0:64, :].rearrange("p (k n) -> p k n", k=8))
                        wGa, BwGa = wload(win_cols(l, C_GA + st2 * 256, 256), lambda t: v3(t, 8, 256))
                        wGb, BwGb = wload(win_cols(l, C_GB + st2 * 256, 256), lambda t: v3(t, 8, 256))
                        wB, BwB = wload(wbb_d[l].rearrange("(k p) n -> p k n", p=128)[:, :, c2], lambda t: v3(t, 8, 256))
                        return (wA, BwA, wGa, BwGa, wGb, BwGb, wB, BwB)

                    def merge_pre(st2, jj, W_):
                        if True:
                            (wA, BwA, wGa, BwGa, wGb, BwGb, wB, BwB) = W_
                            j = st2 * 2 + jj
                            cj = slice(jj * 128, (jj + 1) * 128)
                            psA, BpA = psum()
                            for hd in range(8):
                                S.op("pe", lambda e, psA=psA, hd=hd, wA=wA, cj=cj: e.matmul(
                                    psA[:], lhsT=wA[0:64, :].rearrange("p (k n) -> p k n", k=8)[:, (hd // 4) * 4 + HORD[hd % 4], cj],
                                    rhs=ya[:, hd, :], start=(hd == 0), stop=(hd == 7)),
                                     reads=[BwA, B_ya], writes=[BpA])
                            psGa, BpGa = proj_chunk(wGa, BwGa, lambda t, k, cj=cj: v3(t, 8, 256)[:, k, cj], hg, B_hg)
                            psGb, BpGb = proj_chunk(wGb, BwGb, lambda t, k, cj=cj: v3(t, 8, 256)[:, k, cj], hg, B_hg)
                            return (psA, BpA, psGa, BpGa, psGb, BpGb)

                    def merge_post(st2, jj, W_, P_):
                        if True:
                            (wA, BwA, wGa, BwGa, wGb, BwGb, wB, BwB) = W_
                            (psA, BpA, psGa, BpGa, psGb, BpGb) = P_
                            j = st2 * 2 + jj
                            cj = slice(jj * 128, (jj + 1) * 128)
                            S.op("act", lambda e, psGa=psGa: e.activation(out=sa[:], in_=psGa[:], func=AF.Sigmoid),
                                 reads=[BpGa], writes=[B_sa])
                            S.op("act", lambda e, psGb=psGb: e.activation(out=sbb[:], in_=psGb[:], func=AF.Sigmoid),
                                 reads=[BpGb], writes=[B_sbb])
                            S.op("dve", lambda e, psA=psA: e.tensor_tensor(out=sa[:], in0=psA[:], in1=sa[:], op=ALU.mult),
                                 reads=[BpA, B_sa], writes=[B_sa])
                            psB, BpB = psum()
                            for k in range(8):
                                S.op("pe", lambda e, psB=psB, k=k, wB=wB, cj=cj: e.matmul(
                                    psB[:], lhsT=v3(wB, 8, 256)[:, k, cj], rhs=oloc[:, k, gs], start=(k == 0), stop=(k == 7)),
                                     reads=[BwB, B_oloc[g]], writes=[BpB])
                            S.op("dve", lambda e, psB=psB: e.tensor_tensor(out=sbb[:], in0=psB[:], in1=sbb[:], op=ALU.mult),
                                 reads=[BpB, B_sbb], writes=[B_sbb])
                            S.op("dve", lambda e, j=j: e.tensor_tensor(out=merged[:, j, :], in0=sa[:], in1=sbb[:], op=ALU.add),
                                 reads=[B_sa, B_sbb], writes=[B_merged])


                    gla_p1(0)
                    gla_p2(0)
                    gla_p1(1)
                    W0_ = merge_load(0)
                    P0_ = merge_pre(0, 0, W0_)
                    P1_ = merge_pre(0, 1, W0_)
                    gla_p2(1)
                    merge_post(0, 0, W0_, P0_)
                    merge_post(0, 1, W0_, P1_)
                    if g + 1 < NG:
                        rms_stats(l, g + 1, sqr, B_sqr, rstd, B_rstd)
                    for st2 in range(1, 4):
                        W_ = merge_load(st2)
                        for jj in range(2):
                            merge_post(st2, jj, W_, merge_pre(st2, jj, W_))
                    if stop == ("B3", l):
                        raise _Stop()
                    if stop == ("B4", l):
                        raise _Stop()
                    if g == NG - 1:
                        st_ = {}
                        for j in range(8):
                            outproj_chunk(g, j, st_)
                    else:
                        pend_out = (g, {})
                if stop == ("mix", l):
                    break
                S.barrier()
                AR.reset()
                hgs_ = [(AR.get([8, G], BF16), [Buf("hgC0_%d" % k_) for k_ in range(8)]), (AR.get([8, G], BF16), [Buf("hgC1_%d" % k_) for k_ in range(8)])]
                sqr = AR.get([2, G], BF16); B_sqr = [Buf("sq0C"), Buf("sq1C")]
                rstd = AR.get([G], F32); B_rstd = Buf("rstdC")
                gt = AR.get([NJ, G], BF16); B_gt = Buf("gt")
                aext = AR.get([2, G + 2], F32); B_aext = [Buf("aext0"), Buf("aext1")]
                yv = AR.get([2, G], F32); B_yv = [Buf("yv0"), Buf("yv1")]
                msg2 = AR.get([16], F32); B_msg2 = Buf("msg2")
                gath2 = AR.get([4, 16], F32); B_gath2 = Buf("gath2")
                xh = AR.get([8, 2], F32); B_xh = Buf("xh")
                sq2 = AR.get([8, 2], BF16); B_sq2 = Buf("sq2")
                rs2 = AR.get([2], F32); B_rs2 = Buf("rs2")
                h2h1 = AR.get([8, 2], BF16); B_h2h1 = Buf("h2h1")
                B_cin2 = Buf("cin2"); B_cout2 = Buf("cout2")

                def halo_norm(src3, B_src, dst, B_dst):
                    S.op("act", lambda e: e.activation(out=sq2[:], in_=src3, func=AF.Square), reads=B_src, writes=[B_sq2])
                    ps, Bp = psum()
                    for k in range(8):
                        S.op("pe", lambda e, k=k, ps=ps: e.matmul(ps[:, 0:2], lhsT=ones_dm[:], rhs=sq2[:, k, :],
                                                                  start=(k == 0), stop=(k == 7)),
                             reads=[B_sq2, B_const], writes=[Bp])
                    rsqrt_eps(rs2[:], ps[:, 0:2], [Bp], B_rs2)
                    for k in range(8):
                        S.op("dve", lambda e, k=k: e.scalar_tensor_tensor(out=dst[:, k, :], in0=src3[:, k, :],
                                                                          scalar=spt[:, l, 8 + k:9 + k], in1=rs2[:],
                                                                          op0=ALU.mult, op1=ALU.mult),
                             reads=B_src + [B_rs2, B_const], writes=[B_dst])

                S.op("act", lambda e: e.activation(out=msg2[:].rearrange("p (k t) -> p k t", k=8),
                                                   in_=xT[:, :, TOK - 2:TOK], func=AF.Copy), reads=[B_x[NG - 1]],
                     writes=[B_msg2])
                S.dma("sp", lambda e: e.dma_start(out=cin2[l], in_=msg2[:]), "cin", reads=[B_msg2], writes=[B_cin2])
                S.cc(lambda e: e.collective_compute("AllGather", ALU.bypass, replica_groups=RG, ins=[cin2[l].opt()],
                                                    outs=[cout2[l].opt()]), "cc", reads=[B_cin2], writes=[B_cout2])
                S.dma("sp", lambda e: e.dma_start(out=gath2[:], in_=cout2[l].rearrange("(j p) n -> p j n", p=128)), "gath",
                      reads=[B_cout2], writes=[B_gath2])
                halo_norm(xT[:, :, G - 2:G], [B_x[0]], h2h1, B_h2h1)
                order_ = [1, 2, 3, 0]
                rmsnorm_group(l, order_[0], 8, hgs_[0][0], hgs_[0][1], sqr, B_sqr, rstd, B_rstd)
                for p_, g in enumerate(order_):
                    gs = slice(g * G, (g + 1) * G)
                    if g == 0:
                        xhf = xh[:].rearrange("p k t -> p (k t)")
                        S.op("dve", lambda e: e.tensor_scalar(out=xhf, in0=gath2[:, 0, :], scalar1=flags[:, 0:1],
                                                              scalar2=None, op0=ALU.mult), reads=[B_gath2, B_const],
                             writes=[B_xh])
                        for jr in range(1, 4):
                            S.op("dve", lambda e, jr=jr: e.scalar_tensor_tensor(out=xhf, in0=gath2[:, jr, :],
                                                                                scalar=flags[:, jr:jr + 1], in1=xhf,
                                                                                op0=ALU.mult, op1=ALU.add),
                                 reads=[B_gath2, B_xh, B_const], writes=[B_xh])
                        halo_norm(xh[:], [B_xh], h2halo, B_h2halo)
                    hg, B_hg = hgs_[p_ % 2]
                    for jp in range(NJ // 2):
                        wfa, Bwfa = wload(wfi_d[l].rearrange("(k p) n -> p k n", p=128)[:, :, jp * 256:(jp + 1) * 256],
                                          lambda t: v3(t, 8, 256))
                        wfu, Bwfu = wload(
                            wfi_d[l].rearrange("(k p) n -> p k n", p=128)[:, :, D_FF + jp * 256:D_FF + (jp + 1) * 256],
                            lambda t: v3(t, 8, 256))
                        for jj in range(2):
                            j = jp * 2 + jj
                            r = j % 2
                            cw = 46 + j * 3
                            psa, Bpa = proj_chunk(wfa, Bwfa, lambda t, k, jj=jj: v3(t, 8, 256)[:, k, jj * 128:(jj + 1) * 128],
                                                  hg, B_hg)
                            psu, Bpu = proj_chunk(wfu, Bwfu, lambda t, k, jj=jj: v3(t, 8, 256)[:, k, jj * 128:(jj + 1) * 128],
                                                  hg, B_hg)
                            if g in (0, 1):
                                hsrc, B_hsrc = (h2halo, B_h2halo) if g == 0 else (h2h1, B_h2h1)
                                psh, Bph = psum()
                                for k in range(8):
                                    S.op("pe", lambda e, k=k, psh=psh, jj=jj, wfa=wfa, hsrc=hsrc: e.matmul(
                                        psh[:, 0:2], lhsT=v3(wfa, 8, 256)[:, k, jj * 128:(jj + 1) * 128], rhs=hsrc[:, k, :],
                                        start=(k == 0), stop=(k == 7)), reads=[Bwfa, B_hsrc], writes=[Bph])
                                S.op("act", lambda e, psh=psh, r=r: e.activation(out=aext[:, r, 0:2], in_=psh[:, 0:2],
                                                                                 func=AF.Copy), reads=[Bph],
                                     writes=[B_aext[r]])
                            else:
                                S.op("act", lambda e, r=r, j=j: e.activation(out=aext[:, r, 0:2], in_=ahalo[:, j, :], func=AF.Copy),
                                     reads=[B_ahalo], writes=[B_aext[r]])
                            S.op("act", lambda e, psa=psa, r=r: e.activation(out=aext[:, r, 2:G + 2], in_=psa[:],
                                                                             func=AF.Copy), reads=[Bpa], writes=[B_aext[r]])
                            S.op("act", lambda e, r=r, j=j: e.activation(out=ahalo[:, j, :], in_=aext[:, r, G:G + 2], func=AF.Copy),
                                 reads=[B_aext[r]], writes=[B_ahalo])
                            S.op("dve", lambda e, r=r, cw=cw, j=j: e.tensor_scalar(
                                out=yv[:, r, :], in0=aext[:, r, 2:G + 2], scalar1=spt[:, l, cw + 2:cw + 3],
                                scalar2=spt[:, l, 24 + j:25 + j], op0=ALU.mult, op1=ALU.add),
                                 reads=[B_aext[r], B_const], writes=[B_yv[r]])
                            S.op("dve", lambda e, r=r, cw=cw: e.scalar_tensor_tensor(
                                out=yv[:, r, :], in0=aext[:, r, 1:G + 1], scalar=spt[:, l, cw + 1:cw + 2], in1=yv[:, r, :],
                                op0=ALU.mult, op1=ALU.add), reads=[B_aext[r], B_yv[r], B_const], writes=[B_yv[r]])
                            S.op("dve", lambda e, r=r, cw=cw: e.scalar_tensor_tensor(
                                out=yv[:, r, :], in0=aext[:, r, 0:G], scalar=spt[:, l, cw:cw + 1], in1=yv[:, r, :],
                                op0=ALU.mult, op1=ALU.add), reads=[B_aext[r], B_yv[r], B_const], writes=[B_yv[r]])
                            S.op("act", lambda e, r=r: e.activation(out=yv[:, r, :], in_=yv[:, r, :], func=AF.Silu),
                                 reads=[B_yv[r]], writes=[B_yv[r]])
                            S.op("dve", lambda e, r=r, j=j, psu=psu: e.tensor_tensor(out=gt[:, j, :], in0=psu[:],
                                                                                     in1=yv[:, r, :], op=ALU.mult),
                                 reads=[Bpu, B_yv[r]], writes=[B_gt])
                    for jp2 in range(4):
                        pss2 = [psum(), psum()]
                        for part in range(3):
                            k0 = part * 8
                            nk = min(8, NJ - k0)
                            wo, Bwo = wload(
                                wfo_d[l].rearrange("(c p) n -> p c n", p=128)[:, k0:k0 + nk, jp2 * 256:(jp2 + 1) * 256],
                                lambda t, nk=nk: v3(t, nk, 256))
                            for jj in range(2):
                                ps, Bp = pss2[jj]
                                for kk in range(nk):
                                    c = k0 + kk
                                    S.op("pe", lambda e, ps=ps, kk=kk, c=c, jj=jj, wo=wo, nk=nk: e.matmul(
                                        ps[:], lhsT=v3(wo, nk, 256)[:, kk, jj * 128:(jj + 1) * 128], rhs=gt[:, c, :],
                                        start=(c == 0), stop=(c == NJ - 1)), reads=[Bwo, B_gt], writes=[Bp])
                        for jj in range(2):
                            ps, Bp = pss2[jj]
                            jo = jp2 * 2 + jj
                            S.op("dve", lambda e, ps=ps, jo=jo: e.tensor_tensor(out=xT[:, jo, gs], in0=ps[:], in1=xT[:, jo, gs],
                                                                                op=ALU.add),
                                 reads=[Bp, B_x[g]], writes=[B_x[g]])
                        if jp2 == 0 and p_ + 1 < NG:
                            rmsnorm_group(l, order_[p_ + 1], 8, hgs_[(p_ + 1) % 2][0], hgs_[(p_ + 1) % 2][1], sqr, B_sqr,
                                          rstd, B_rstd)
                if stop == ("layer", l):
                    break
        except _Stop:
            pass

        for g in (1, 2, 3, 0):
            S.dma("sp", lambda e, g=g: e.dma_start(out=y_d[:, :, g * G:(g + 1) * G], in_=xT[:, :, g * G:(g + 1) * G]),
                  "out", reads=[B_x[g]])
        S.final_wait("sp")
        S.emit()
    return nc


def _layout_inputs(inputs):
    f32 = np.float32
    x = np.asarray(inputs["x"], f32)
    L = DEPTH
    sp = np.zeros((L, 128, NSP), f32)
    p = np.arange(128)
    for l in range(L):
        sp[l, :, 0:8] = np.asarray(inputs["ln_mix_g"][l], f32).reshape(8, 128).T
        sp[l, :, 8:16] = np.asarray(inputs["ln_ffn_g"][l], f32).reshape(8, 128).T
        sp[l, :, 16] = np.asarray(inputs["q_norm_g"][l], f32)[p % 64]
        sp[l, :, 17] = np.asarray(inputs["k_norm_g"][l], f32)[p % 64]
        sp[l, :, 18:22] = np.asarray(inputs["b_gk"][l], f32).reshape(4, 128).T
        sp[l, :, 22:24] = np.asarray(inputs["gla_norm_g"][l], f32).reshape(2, 128).T
        sp[l, :, 24:46] = np.asarray(inputs["conv_b"][l], f32).reshape(NJ, 128).T
        cw = np.asarray(inputs["conv_w"][l], f32)
        sp[l, :, 46:112] = cw.reshape(3, NJ, 128).transpose(2, 1, 0).reshape(128, NJ * 3)
        sp[l, :, 112:120] = np.asarray(inputs["sinks"][l], f32)[None, :]
    shared = {
        "sp": sp,
        "w_gk_up": np.ascontiguousarray(np.asarray(inputs["w_gk_up"], f32)),
        "w_in": np.ascontiguousarray(np.asarray(inputs["w_in"], f32)),
        "w_branch_a": np.ascontiguousarray(np.asarray(inputs["w_branch_a"], f32)),
        "w_branch_b": np.ascontiguousarray(np.asarray(inputs["w_branch_b"], f32)),
        "w_out": np.ascontiguousarray(np.asarray(inputs["w_out"], f32)),
        "w_ffn_in": np.ascontiguousarray(np.asarray(inputs["w_ffn_in"], f32)),
        "w_ffn_out": np.ascontiguousarray(np.asarray(inputs["w_ffn_out"], f32)),
    }
    in_maps = []
    for c in range(8):
        b, r = c // 4, c % 4
        xs = x[b, r * TOK:(r + 1) * TOK, :]
        xt = np.ascontiguousarray(xs.T.reshape(8, 128, TOK).transpose(1, 0, 2))
        fl = np.zeros((128, 8), f32)
        if r > 0:
            fl[:, r - 1] = 1.0
            fl[:, 7] = 1.0
        for j in range(3):
            fl[:, 4 + j] = 1.0 if j < r else 0.0
        m = dict(shared)
        m["xT"] = xt
        m["flags"] = fl
        in_maps.append(m)
    return in_maps


def _gather_out(results):
    out = np.zeros((2, 4 * TOK, D), np.float32)
    for c in range(8):
        b, r = c // 4, c % 4
        yt = np.asarray(results[c]["yT"])
        out[b, r * TOK:(r + 1) * TOK, :] = yt.transpose(2, 1, 0).reshape(TOK, D)
    return out


_NC_CACHE = {}


def kernel(**inputs):
    if "nc" not in _NC_CACHE:
        _NC_CACHE["nc"] = build()
    nc = _NC_CACHE["nc"]
    in_maps = _layout_inputs(inputs)
    res = run_bass_kernel_spmd(nc, in_maps, core_ids=list(range(8)))
    return _gather_out(res.results)
```

```python
import numpy as np
from contextlib import ExitStack
import concourse.bass as bass
import concourse.mybir as mybir
from concourse.bass_utils import run_bass_kernel_spmd

F32 = mybir.dt.float32
BF16 = mybir.dt.bfloat16
I32 = mybir.dt.int32
AF = mybir.ActivationFunctionType
ALU = mybir.AluOpType

D = 1024
TOK = 2048
G = 512
NG = TOK // G
DEPTH = 2
IN_W = 5904
D_FF = 2816
NJ = D_FF // 128
EPS = 1e-6
C_QA, C_KA, C_VA, C_QB, C_KB, C_VB, C_RB, C_GK, C_GA, C_GB = 0, 512, 640, 768, 1280, 1792, 2816, 3840, 3856, 4880
NSP = 120
HORD = (0, 2, 1, 3)
MSGW = 1024 + 4 + 256 + 128
EPOCH = 12000


import types


def _snap(fn):
    if fn is None or fn.__closure__ is None:
        return fn
    cells = tuple(types.CellType(c.cell_contents) for c in fn.__closure__)
    return types.FunctionType(fn.__code__, fn.__globals__, fn.__name__, fn.__defaults__, cells)


class _Stop(Exception):
    pass


def _bk(B, k):
    return B[k] if isinstance(B, list) else B


class Buf:
    __slots__ = ("name", "w", "r", "excl")

    def __init__(self, name, excl=False):
        self.name, self.w, self.r, self.excl = name, None, [], excl


class Sched:
    ENG = ("pe", "act", "dve", "pool", "sp")

    def __init__(self, nc, es):
        self.nc, self.es = nc, es
        self.prog = {e: [] for e in self.ENG}
        self.cnt = {e: 0 for e in self.ENG}
        self.nsem = 0
        self.sems = {e: self._newsem() for e in self.ENG}
        self.waited = {e: {} for e in self.ENG}
        self.dsem = {}
        self.last_tok = {e: None for e in self.ENG}
        self.dma_toks = {}

    def _newsem(self):
        self.nsem += 1
        return self.es.enter_context(self.nc.semaphore("s%d" % self.nsem))

    def _waits(self, eng, deps):
        waits = []
        for (sem, val, src) in deps:
            if src == eng == "pe":
                continue
            key = id(sem)
            if self.waited[eng].get(key, 0) >= val:
                continue
            self.waited[eng][key] = val
            waits.append((sem, val))
        return waits

    def _deps(self, reads, writes):
        deps = []
        for b in reads:
            if b.excl:
                deps += b.r
            if b.w is not None:
                deps.append(b.w)
        for b in writes:
            deps += b.r
            if b.w is not None:
                deps.append(b.w)
        return deps

    def _commit(self, tok, reads, writes):
        for b in reads:
            if b.excl:
                b.w, b.r = tok, []
            else:
                b.r.append(tok)
        for b in writes:
            b.w, b.r = tok, []

    def op(self, eng, fn, reads=(), writes=()):
        waits = self._waits(eng, self._deps(reads, writes))
        if self.cnt[eng] >= EPOCH:
            self.sems[eng] = self._newsem()
            self.cnt[eng] = 0
        self.cnt[eng] += 1
        sem = self.sems[eng]
        tok = (sem, self.cnt[eng], eng)
        self.prog[eng].append((waits, _snap(fn), sem, 1))
        self.last_tok[eng] = tok
        self._commit(tok, reads, writes)

    def dma(self, eng, fn, semname, reads=(), writes=()):
        waits = self._waits(eng, self._deps(reads, writes))
        if semname not in self.dsem:
            self.dsem[semname] = [self._newsem(), 0]
        s = self.dsem[semname]
        s[1] += 16
        tok = (s[0], s[1], "dma")
        self.prog[eng].append((waits, _snap(fn), s[0], 16))
        self.dma_toks[semname] = tok
        self._commit(tok, reads, writes)

    def cc(self, fn, semname, reads=(), writes=()):
        waits = self._waits("pool", self._deps(reads, writes))
        if semname not in self.dsem:
            self.dsem[semname] = [self._newsem(), 0]
        s = self.dsem[semname]
        s[1] += 1
        tok = (s[0], s[1], "dma")
        self.prog["pool"].append((waits, _snap(fn), s[0], None))
        self.dma_toks[semname] = tok
        self._commit(tok, reads, writes)

    def barrier(self, skip=()):
        toks = [t for t in self.last_tok.values() if t is not None] + \
               [t for n_, t in self.dma_toks.items() if n_ not in skip]
        for e in self.ENG:
            w = self._waits(e, [t for t in toks if t[2] != e])
            if w:
                self.prog[e].append((w, None, None, 0))

    def final_wait(self, eng):
        toks = [t for t in self.last_tok.values() if t is not None] + list(self.dma_toks.values())
        w = self._waits(eng, [t for t in toks if t[2] != eng])
        self.prog[eng].append((w, None, None, 0))

    def emit(self):
        nc = self.nc
        prog = self.prog

        def replay(e, lst):
            for waits, fn, sem, inc in lst:
                for (s, v) in waits:
                    e.wait_ge(s, v)
                if fn is None:
                    continue
                ins = fn(e)
                if inc is None:
                    ins.then_inc(sem)
                else:
                    ins.then_inc(sem, inc)

        with nc.Block() as block:
            @block.tensor
            def _(e):
                replay(e, prog["pe"])

            @block.scalar
            def _(e):
                replay(e, prog["act"])

            @block.vector
            def _(e):
                replay(e, prog["dve"])

            @block.gpsimd
            def _(e):
                replay(e, prog["pool"])

            @block.sync
            def _(e):
                replay(e, prog["sp"])


class Arena:
    def __init__(self, ap_f32, nwords):
        self.base, self.n, self.off = ap_f32, nwords, 0

    def reset(self):
        self.off = 0

    def view(self, off, shape, dtype, parts=128):
        save = self.off
        self.off = off
        ap = self.get(shape, dtype, parts)
        self.off = save
        return ap

    def get(self, shape, dtype, parts=128):
        n = int(np.prod(shape))
        words = n if dtype in (F32, I32) else (n + 1) // 2
        assert self.off + words <= self.n, ("arena overflow", self.off, words, self.n)
        ap = self.base[0:parts, self.off:self.off + words]
        self.off += words
        if dtype != F32:
            ap = ap.bitcast(dtype)
            if dtype == BF16 and n % 2:
                ap = ap[:, 0:n]
        if len(shape) == 2:
            ap = ap.rearrange("p (a b) -> p a b", a=shape[0])
        elif len(shape) == 3:
            ap = ap.rearrange("p (a b c) -> p a b c", a=shape[0], b=shape[1])
        return ap


def build(stop=None):
    nc = bass.Bass("TRN2", target_bir_lowering=False)
    dt = nc.dram_tensor
    x_d = dt("xT", [128, 8, TOK], F32, kind="ExternalInput").ap()
    flags_d = dt("flags", [128, 8], F32, kind="ExternalInput").ap()
    sp_d = dt("sp", [DEPTH, 128, NSP], F32, kind="ExternalInput").ap()
    wup_d = dt("w_gk_up", [DEPTH, 16, 512], F32, kind="ExternalInput").ap()
    win_d = dt("w_in", [DEPTH, D, IN_W], F32, kind="ExternalInput").ap()
    wba_d = dt("w_branch_a", [DEPTH, 512, D], F32, kind="ExternalInput").ap()
    wbb_d = dt("w_branch_b", [DEPTH, D, D], F32, kind="ExternalInput").ap()
    wout_d = dt("w_out", [DEPTH, D, D], F32, kind="ExternalInput").ap()
    wfi_d = dt("w_ffn_in", [DEPTH, D, 2 * D_FF], F32, kind="ExternalInput").ap()
    wfo_d = dt("w_ffn_out", [DEPTH, D_FF, D], F32, kind="ExternalInput").ap()
    y_d = dt("yT", [128, 8, TOK], F32, kind="ExternalOutput").ap()
    cin1 = [dt("cin1_%d" % l, [128, MSGW], F32, kind="Internal").ap() for l in range(DEPTH)]
    cout1 = [dt("cout1_%d" % l, [512, MSGW], F32, kind="Internal").ap() for l in range(DEPTH)]
    cin2 = [dt("cin2_%d" % l, [128, 16], F32, kind="Internal").ap() for l in range(DEPTH)]
    cout2 = [dt("cout2_%d" % l, [512, 16], F32, kind="Internal").ap() for l in range(DEPTH)]
    RG = [[0, 1, 2, 3], [4, 5, 6, 7]]

    with ExitStack() as es:
        S = Sched(nc, es)

        def sb(name, shape, dtype):
            return es.enter_context(nc.sbuf_tensor(name, shape, dtype))

        xT = sb("xTs", [128, 8, TOK], F32)
        oloc = sb("oloc", [128, 8, TOK], BF16)
        qhat = sb("qhat", [128, 4, TOK], BF16)
        B_x = [Buf("x%d" % g) for g in range(NG)]
        B_oloc = [Buf("oloc%d" % g) for g in range(NG)]
        B_qhat = [Buf("qhat%d" % g) for g in range(NG)]
        NSLOT = 6
        wsl = [sb("wslot%d" % i, [128, 2048], BF16) for i in range(NSLOT)]
        B_w = [Buf("w%d" % i) for i in range(NSLOT)]
        ident = sb("ident", [128, 128], BF16)
        ones_dm = sb("ones_dm", [128, 128], BF16)
        ones_dv = sb("ones_dv", [128, 128], BF16)
        bd64 = sb("bd64", [128, 128], BF16)
        ones64 = sb("ones64", [128, 64], BF16)
        ones128 = sb("ones128", [128, 128], BF16)
        maskc = sb("maskc", [128, 512], BF16)
        maskp = sb("maskp", [128, 512], BF16)
        maskp0 = sb("maskp0", [128, 512], BF16)
        onesf = sb("onesf", [128, 1], F32)
        zerof = sb("zerof", [128, 1], F32)
        epsf = sb("epsf", [128, 1], F32)
        flags = sb("flags_s", [128, 8], F32)
        spt = sb("spt", [128, DEPTH, NSP], F32)
        wup = sb("wup", [16, DEPTH, 512], BF16)
        nbgk = sb("nbgk", [128, 4], F32)
        qg8 = sb("qg8", [128, 1], F32)
        es8 = sb("es8", [128, 8], F32)
        esink = sb("esink", [64, 2, 512], F32)
        Sinit_bf = sb("Sinit_bf", [128, 1024], BF16)
        kprev = sb("kprev", [128, 2, 128], BF16)
        vprev = sb("vprev", [128, 128], BF16)
        Sst = sb("Sst", [128, 1024], F32)
        Cend = sb("Cend", [128, 4], F32)
        sc = sb("sc", [128, 64], F32)
        ahalo = sb("ahalo", [128, NJ, 2], F32)
        h2halo = sb("h2halo", [128, 8, 2], BF16)
        B_const = Buf("const")
        B_lay = Buf("layerconst")
        B_Sinit = Buf("Sinit")
        B_kvprev = Buf("kvprev")
        B_S = Buf("S")
        B_Cend = Buf("Cend")
        B_ahalo = Buf("ahalo")
        B_h2halo = Buf("h2halo")
        ARW = 13312
        arena_t = sb("arena", [128, ARW], F32)
        AR = Arena(arena_t, ARW)
        pst = [es.enter_context(nc.psum_tensor("ps%d" % i, [128, 512], F32)) for i in range(8)]
        B_ps = [Buf("ps%d" % i, excl=True) for i in range(8)]
        psrr = [0]

        def psum():
            i = psrr[0] % 8
            psrr[0] += 1
            return pst[i], B_ps[i]

        slot_rr = [0]

        def wload(src_ap, view_fn, parts=128):
            i = slot_rr[0] % NSLOT
            slot_rr[0] += 1
            dst = view_fn(wsl[i])
            S.dma("pool", lambda e, d=dst, s=src_ap: e.dma_start(out=d, in_=s), "w%d" % i, writes=[B_w[i]])
            return wsl[i], B_w[i]

        def wload_multi(pairs):
            i = slot_rr[0] % NSLOT
            slot_rr[0] += 1
            for (src_ap, view_fn) in pairs:
                dst = view_fn(wsl[i])
                S.dma("pool", lambda e, d=dst, s=src_ap: e.dma_start(out=d, in_=s), "w%d" % i, writes=[B_w[i]])
            return wsl[i], B_w[i]

        def v3(t, k, n):
            return t[:, 0:k * n].rearrange("p (k n) -> p k n", k=k)

        def win_cols(l, c0, n):
            return win_d[l].rearrange("(k p) n -> p k n", p=128)[:, :, c0:c0 + n]

        iot_i = AR.get([512], F32).bitcast(I32)
        iot_f = AR.get([512], F32)
        S.op("pool", lambda e: e.iota(iot_i[:], pattern=[[0, 4], [1, 128]], base=0, channel_multiplier=-1),
             writes=[B_const])
        S.op("dve", lambda e: e.tensor_copy(out=iot_f[:], in_=iot_i[:]), reads=[B_const], writes=[B_const])
        S.op("dve", lambda e: e.tensor_single_scalar(out=maskc[:], in_=iot_f[:], scalar=0.0, op=ALU.is_ge),
             writes=[B_const])
        S.op("dve", lambda e: e.tensor_single_scalar(out=maskp[:], in_=iot_f[:], scalar=0.0, op=ALU.is_lt),
             writes=[B_const])
        S.op("dve", lambda e: e.tensor_single_scalar(out=ident[:], in_=iot_f[:, 0:128], scalar=0.0, op=ALU.is_equal),
             writes=[B_const])
        S.op("dve", lambda e: e.memset(ones_dm[:], 1.0 / 1024.0), writes=[B_const])
        S.op("dve", lambda e: e.memset(ones_dv[:], 1.0 / 256.0), writes=[B_const])
        S.op("dve", lambda e: e.memset(bd64[:], 0.0), writes=[B_const])
        S.op("dve", lambda e: e.memset(bd64[0:64, 0:64], 1.0 / 64.0), writes=[B_const])
        S.op("dve", lambda e: e.memset(bd64[64:128, 64:128], 1.0 / 64.0), writes=[B_const])
        S.op("dve", lambda e: e.memset(ones64[:], 1.0), writes=[B_const])
        S.op("dve", lambda e: e.memset(ones128[:], 1.0), writes=[B_const])
        S.op("dve", lambda e: e.memset(onesf[:], 1.0), writes=[B_const])
        S.op("dve", lambda e: e.memset(zerof[:], 0.0), writes=[B_const])
        S.op("dve", lambda e: e.memset(epsf[:], EPS), writes=[B_const])

        def rsqrt_eps(out_ap, in_ap, reads, B_out):
            S.op("act", lambda e: e.activation(out=out_ap, in_=in_ap, func=AF.Ln, bias=epsf[0:out_ap.shape[0], 0:1],
                                               scale=1.0), reads=list(reads) + [B_const], writes=[B_out])
            S.op("act", lambda e: e.activation(out=out_ap, in_=out_ap, func=AF.Exp, scale=-0.5), reads=[B_out],
                 writes=[B_out])
        S.dma("sp", lambda e: e.dma_start(out=flags[:], in_=flags_d), "misc", writes=[B_const])
        S.dma("sp", lambda e: e.dma_start(out=spt[:], in_=sp_d.rearrange("l p n -> p l n")), "misc", writes=[B_const])
        S.dma("pool", lambda e: e.dma_start(out=wup[:], in_=wup_d.rearrange("l r n -> r l n")), "misc2",
              writes=[B_const])
        for g in range(NG):
            S.dma("sp", lambda e, g=g: e.dma_start(out=xT[:, :, g * G:(g + 1) * G], in_=x_d[:, :, g * G:(g + 1) * G]),
                  "xin%d" % g, writes=[B_x[g]])
        S.op("dve", lambda e: e.tensor_scalar(out=maskp0[:], in0=maskp[:], scalar1=flags[:, 7:8], scalar2=None,
                                              op0=ALU.mult), reads=[B_const], writes=[B_const])

        def rms_stats(l, g, sqr, B_sqr, rstd, B_rstd):
            ps, Bp = psum()
            for k in range(8):
                r = k % 2
                S.op("act", lambda e, k=k, r=r: e.activation(out=sqr[:, r, :], in_=xT[:, k, g * G:(g + 1) * G],
                                                             func=AF.Square),
                     reads=[B_x[g]], writes=[B_sqr[r]])
                S.op("pe", lambda e, k=k, r=r: e.matmul(ps[:], lhsT=ones_dm[:], rhs=sqr[:, r, :], start=(k == 0),
                                                        stop=(k == 7)),
                     reads=[B_sqr[r], B_const], writes=[Bp])
            rsqrt_eps(rstd[:], ps[:], [Bp], B_rstd)

        def rms_scale(l, g, gcol, hg, B_hg, rstd, B_rstd):
            for k in range(8):
                S.op("dve", lambda e, k=k: e.scalar_tensor_tensor(out=hg[:, k, :], in0=xT[:, k, g * G:(g + 1) * G],
                                                                  scalar=spt[:, l, gcol + k:gcol + k + 1], in1=rstd[:],
                                                                  op0=ALU.mult, op1=ALU.mult),
                     reads=[B_x[g], B_rstd, B_const], writes=[_bk(B_hg, k)])

        def rmsnorm_group(l, g, gcol, hg, B_hg, sqr, B_sqr, rstd, B_rstd):
            rms_stats(l, g, sqr, B_sqr, rstd, B_rstd)
            rms_scale(l, g, gcol, hg, B_hg, rstd, B_rstd)

        def proj_chunk(wt, Bw, kview_fn, hg, B_hg, nk=8):
            ps, Bp = psum()
            for k in range(nk):
                S.op("pe", lambda e, k=k: e.matmul(ps[:], lhsT=kview_fn(wt, k), rhs=hg[:, k, :], start=(k == 0),
                                                   stop=(k == nk - 1)),
                     reads=[Bw, _bk(B_hg, k)], writes=[Bp])
            return ps, Bp

        def qknorm(ps, Bp, sq, B_sq, rs, B_rs, out_ap, B_out, gain_ap, ncols):
            S.op("act", lambda e: e.activation(out=sq[:, 0:ncols], in_=ps[:, 0:ncols], func=AF.Square), reads=[Bp],
                 writes=[B_sq])
            ps2, Bp2 = psum()
            S.op("pe", lambda e: e.matmul(ps2[:, 0:ncols], lhsT=bd64[:], rhs=sq[:, 0:ncols], start=True, stop=True),
                 reads=[B_sq, B_const], writes=[Bp2])
            rsqrt_eps(rs[:, 0:ncols], ps2[:, 0:ncols], [Bp2], B_rs)
            S.op("dve", lambda e: e.scalar_tensor_tensor(out=out_ap, in0=ps[:, 0:ncols], scalar=gain_ap,
                                                         in1=rs[:, 0:ncols], op0=ALU.mult, op1=ALU.mult),
                 reads=[Bp, B_rs, B_lay, B_const], writes=[B_out])

        def kdup_load(l):
            cols = []
            for hk in range(2):
                for dup in range(2):
                    o = (hk * 2 + dup) * 64
                    cols.append((win_cols(l, C_KA + hk * 64, 64),
                                 lambda t, o=o: v3(t, 8, 256)[:, :, o:o + 64]))
            return wload_multi(cols)

        try:
            out_done = [False]
            for l in range(DEPTH if stop != "init" else 0):
                S.op("dve", lambda e: e.tensor_scalar(out=nbgk[:], in0=spt[:, l, 18:22], scalar1=-1.0, scalar2=None,
                                                      op0=ALU.mult), reads=[B_const], writes=[B_lay])
                S.op("dve", lambda e: e.tensor_scalar(out=qg8[:], in0=spt[:, l, 16:17], scalar1=0.125, scalar2=None,
                                                      op0=ALU.mult), reads=[B_const], writes=[B_lay])
                S.op("act", lambda e: e.activation(out=es8[:], in_=spt[:, l, 112:120], func=AF.Exp), reads=[B_const],
                     writes=[B_lay])
                for hk in range(2):
                    for hh in range(4):
                        S.op("dve", lambda e, hk=hk, hh=hh: e.tensor_copy(
                            out=esink[:, hk, hh * 128:(hh + 1) * 128],
                            in_=es8[0:64, hk * 4 + HORD[hh]:hk * 4 + HORD[hh] + 1].to_broadcast([64, 128])),
                             reads=[B_lay], writes=[B_lay])
                S.op("dve", lambda e: e.memset(Sst[:], 0.0), writes=[B_S])
                S.op("dve", lambda e: e.memset(Cend[:], 0.0), writes=[B_Cend])

                S.barrier()
                AR.reset()
                hg = AR.get([8, G], BF16); B_hg = [Buf("hgA%d" % k_) for k_ in range(8)]
                sqr = AR.get([2, G], BF16); B_sqr = [Buf("sq0"), Buf("sq1")]
                rstd = AR.get([G], F32); B_rstd = Buf("rstd")
                qt = AR.get([4, G], BF16); B_qt = Buf("qt")
                kt = AR.get([4, G], BF16); B_kt = Buf("kt")
                vtok = AR.get([4, 1024], BF16); B_vtok = Buf("vtok")
                lsp = AR.get([G], F32); B_lsp = Buf("lsp")
                Cc = AR.get([G], F32); B_Cc = Buf("Cc")
                E1 = AR.get([G], F32); B_E1 = Buf("E1")
                E2 = AR.get([G], F32); B_E2 = Buf("E2")
                gklr = AR.get([G], BF16); B_gklr = Buf("gklr")
                ATs = AR.get([2, G], BF16); B_AT = [Buf("AT0"), Buf("AT1")]
                ktok = AR.get([2, G], BF16); B_ktok = [Buf("ktok0"), Buf("ktok1")]
                U = AR.get([1024], F32); B_U = Buf("U")
                Ubf = AR.get([1024], BF16); B_Ubf = Buf("Ubf")
                dS = AR.get([4], F32); B_dS = Buf("dS")
                msg = AR.get([MSGW - 1024], F32); B_msg = Buf("msg")
                sqk = AR.get([G], BF16); B_sqk = Buf("sqk")
                rsk = AR.get([G], F32); B_rsk = Buf("rsk")

                for g in range(NG):
                    if g == 0:
                        rms_stats(l, g, sqr, B_sqr, rstd, B_rstd)
                        rms_scale(l, g, 0, hg, B_hg, rstd, B_rstd)
                    wt, Bw = wload(win_cols(l, C_GK, 128), lambda t: v3(t, 8, 128))
                    ps, Bp = psum()
                    for k in range(8):
                        S.op("pe", lambda e, k=k, wt=wt: e.matmul(ps[:, :], lhsT=v3(wt, 8, 128)[:, k, :], rhs=hg[:, k, :],
                                                                  start=(k == 0), stop=(k == 7)),
                             reads=[Bw, _bk(B_hg, k)], writes=[Bp])
                    S.op("act", lambda e, ps=ps: e.activation(out=gklr[0:16, :], in_=ps[0:16, :], func=AF.Copy),
                         reads=[Bp], writes=[B_gklr])
                    wqs = [wload(win_cols(l, C_QB + hp_ * 256, 256), lambda t: v3(t, 8, 256)) for hp_ in range(2)]
                    wks = [wload(win_cols(l, C_KB + hp_ * 256, 256), lambda t: v3(t, 8, 256)) for hp_ in range(2)]
                    def v_unit(q4, half, wv, Bwv):
                        ps, Bp = psum()
                        for bb in range(2):
                            blk = half * 2 + bb
                            for k in range(8):
                                S.op("pe", lambda e, k=k, blk=blk, bb=bb, ps=ps, wv=wv: e.matmul(
                                    ps[:, bb * 256:(bb + 1) * 256], lhsT=hg[:, k, blk * 128:(blk + 1) * 128],
                                    rhs=v3(wv, 8, 256)[:, k, :], start=(k == 0), stop=(k == 7)),
                                     reads=[Bwv, _bk(B_hg, k)], writes=[Bp])
                        S.op("act", lambda e, half=half, q4=q4, ps=ps: e.activation(
                            out=vtok[:, half * 2:half * 2 + 2, q4 * 256:(q4 + 1) * 256],
                            in_=ps[:].rearrange("p (b n) -> p b n", b=2), func=AF.Copy),
                             reads=[Bp], writes=[B_vtok])
                    for h in range(4):
                        psg, Bpg = psum()
                        S.op("pe", lambda e, h=h, psg=psg: e.matmul(psg[:], lhsT=wup[0:16, l, h * 128:(h + 1) * 128],
                                                                    rhs=gklr[0:16, :], start=True, stop=True),
                             reads=[B_gklr, B_const], writes=[Bpg])
                        S.op("act", lambda e, h=h, psg=psg: e.activation(out=lsp[:], in_=psg[:], func=AF.Exp,
                                                                         bias=nbgk[:, h:h + 1], scale=-1.0),
                             reads=[Bpg, B_lay], writes=[B_lsp])
                        S.op("act", lambda e: e.activation(out=lsp[:], in_=lsp[:], func=AF.Ln, bias=onesf[:, 0:1], scale=1.0),
                             reads=[B_lsp, B_const], writes=[B_lsp])
                        S.op("dve", lambda e, h=h: e.tensor_copy(out=sc[:, 0:1], in_=Cend[:, h:h + 1]), reads=[B_Cend],
                             writes=[B_dS])
                        S.op("dve", lambda e, h=h: e.tensor_tensor_scan(out=Cc[:], data0=onesf[:, 0:1].to_broadcast([128, G]),
                                                                        data1=lsp[:], initial=sc[:, 0:1], op0=ALU.mult,
                                                                        op1=ALU.add),
                             reads=[B_lsp, B_dS, B_const], writes=[B_Cc])
                        S.op("dve", lambda e, h=h: e.tensor_copy(out=Cend[:, h:h + 1], in_=Cc[:, G - 1:G]), reads=[B_Cc],
                             writes=[B_Cend])
                        S.op("dve", lambda e: e.tensor_scalar(out=sc[:, 1:2], in0=Cc[:, 255:256], scalar1=1.0 / 16.0,
                                                              scalar2=None, op0=ALU.mult), reads=[B_Cc], writes=[B_dS])
                        S.op("dve", lambda e: e.tensor_scalar(out=sc[:, 2:3], in0=Cc[:, 255:256], scalar1=-1.0 / 16.0,
                                                              scalar2=None, op0=ALU.mult), reads=[B_Cc], writes=[B_dS])
                        S.op("act", lambda e: e.activation(out=E1[:], in_=Cc[:], func=AF.Exp, bias=sc[:, 1:2],
                                                           scale=-1.0 / 16.0), reads=[B_Cc, B_dS], writes=[B_E1])
                        S.op("act", lambda e: e.activation(out=E2[:], in_=Cc[:], func=AF.Exp, bias=sc[:, 2:3],
                                                           scale=1.0 / 16.0), reads=[B_Cc, B_dS], writes=[B_E2])
                        S.op("act", lambda e: e.activation(out=sc[:, 3:4], in_=sc[:, 0:1], func=AF.Exp, bias=sc[:, 2:3],
                                                           scale=1.0 / 16.0), reads=[B_dS], writes=[B_dS])
                        S.op("act", lambda e: e.activation(out=sc[:, 4:5], in_=Cc[:, 255:256], func=AF.Exp,
                                                           scale=-1.0 / 16.0), reads=[B_Cc, B_dS], writes=[B_dS])
                        S.op("dve", lambda e, h=h: e.tensor_copy(out=dS[:, h:h + 1], in_=E1[:, G - 1:G]), reads=[B_E1],
                             writes=[B_dS])
                        wq, Bwq = wqs[h // 2]
                        wk, Bwk = wks[h // 2]
                        psq, Bpq = proj_chunk(wq, Bwq, lambda t, k, h=h: v3(t, 8, 256)[:, k, (h % 2) * 128:(h % 2 + 1) * 128], hg,
                                              B_hg)
                        S.op("dve", lambda e, h=h, psq=psq: e.scalar_tensor_tensor(out=qt[:, h, :], in0=psq[:],
                                                                                   scalar=128.0 ** -0.5, in1=E1[:],
                                                                                   op0=ALU.mult, op1=ALU.mult),
                             reads=[Bpq, B_E1], writes=[B_qt])
                        psk, Bpk = proj_chunk(wk, Bwk, lambda t, k, h=h: v3(t, 8, 256)[:, k, (h % 2) * 128:(h % 2 + 1) * 128], hg,
                                              B_hg)
                        S.op("dve", lambda e, h=h, psk=psk: e.tensor_tensor(out=kt[:, h, :], in0=psk[:], in1=E2[:],
                                                                            op=ALU.mult),
                             reads=[Bpk, B_E2], writes=[B_kt])
                        S.op("dve", lambda e, h=h: e.tensor_scalar(out=qhat[:, h, g * G:(g + 1) * G], in0=qt[:, h, :],
                                                                    scalar1=sc[:, 4:5], scalar2=None, op0=ALU.mult),
                             reads=[B_qt, B_dS], writes=[B_qhat[g]])
                        S.op("dve", lambda e, h=h: e.tensor_scalar(out=U[:, h * 256:(h + 1) * 256],
                                                                   in0=Sst[:, h * 256:(h + 1) * 256], scalar1=sc[:, 3:4],
                                                                   scalar2=None, op0=ALU.mult),
                             reads=[B_S, B_dS], writes=[B_U])
                    S.op("act", lambda e: e.activation(out=Ubf[:], in_=U[:], func=AF.Copy), reads=[B_U], writes=[B_Ubf])
                    for q4 in range(4):
                        wv, Bwv = wload(win_cols(l, C_VB + q4 * 256, 256), lambda t: v3(t, 8, 256))
                        for half in range(2):
                            v_unit(q4, half, wv, Bwv)
                    def blk_stage1(j, par):
                        tsl = slice(j * 128, (j + 1) * 128)
                        psA, BpA = psum()
                        for h in range(4):
                            S.op("pe", lambda e, h=h, psA=psA, tsl=tsl: e.matmul(psA[:, h * 128:(h + 1) * 128],
                                                                                 lhsT=kt[:, h, tsl], rhs=qt[:, h, tsl],
                                                                                 start=True, stop=True),
                                 reads=[B_kt, B_qt], writes=[BpA])
                        S.op("dve", lambda e, psA=psA, par=par: e.tensor_tensor(out=ATs[:, par, :], in0=psA[:], in1=maskc[:],
                                                                                op=ALU.mult),
                             reads=[BpA, B_const], writes=[B_AT[par]])
                        psT, BpT = psum()
                        psTb = psT[:, 0:256].bitcast(BF16)
                        for h in range(4):
                            S.op("pe", lambda e, h=h, psTb=psTb, tsl=tsl: e.transpose(psTb[:, h * 128:(h + 1) * 128],
                                                                                      kt[:, h, tsl], ident[:]),
                                 reads=[B_kt, B_const], writes=[BpT])
                        S.op("act", lambda e, psTb=psTb, par=par: e.activation(out=ktok[:, par, :], in_=psTb, func=AF.Copy),
                             reads=[BpT], writes=[B_ktok[par]])

                    def blk_stage2(j, par):
                        tsl = slice(j * 128, (j + 1) * 128)
                        kvs = []
                        for hp in range(2):
                            pskv, Bpkv = psum()
                            for hh in range(2):
                                h = hp * 2 + hh
                                S.op("pe", lambda e, pskv=pskv, hh=hh, h=h, j=j, par=par: e.matmul(
                                    pskv[:, hh * 256:(hh + 1) * 256], lhsT=ktok[:, par, h * 128:(h + 1) * 128],
                                    rhs=vtok[:, j, h * 256:(h + 1) * 256], start=True, stop=True),
                                     reads=[B_ktok[par], B_vtok], writes=[Bpkv])
                            kvs.append((pskv, Bpkv))
                        for hp in range(2):
                            pso, Bpo = psum()
                            for hh in range(2):
                                h = hp * 2 + hh
                                for c in range(2):
                                    oc = (hh * 2 + c) * 128
                                    ec = h * 256 + c * 128
                                    S.op("pe", lambda e, pso=pso, oc=oc, ec=ec, h=h, j=j, par=par: e.matmul(
                                        pso[:, oc:oc + 128], lhsT=vtok[:, j, ec:ec + 128],
                                        rhs=ATs[:, par, h * 128:(h + 1) * 128], start=True, stop=False),
                                         reads=[B_vtok, B_AT[par]], writes=[Bpo])
                                    S.op("pe", lambda e, pso=pso, oc=oc, ec=ec, h=h, tsl=tsl: e.matmul(
                                        pso[:, oc:oc + 128], lhsT=Ubf[:, ec:ec + 128], rhs=qt[:, h, tsl],
                                        start=False, stop=True), reads=[B_Ubf, B_qt], writes=[Bpo])
                            S.op("act", lambda e, pso=pso, hp=hp, j=j: e.activation(
                                out=oloc[:, hp * 4:hp * 4 + 4, g * G + j * 128:g * G + (j + 1) * 128],
                                in_=pso[:].rearrange("p (c t) -> p c t", c=4), func=AF.Copy),
                                 reads=[Bpo], writes=[B_oloc[g]])
                        for hp in range(2):
                            pskv, Bpkv = kvs[hp]
                            S.op("dve", lambda e, pskv=pskv, hp=hp: e.tensor_tensor(
                                out=U[:, hp * 512:(hp + 1) * 512], in0=pskv[:], in1=U[:, hp * 512:(hp + 1) * 512],
                                op=ALU.add), reads=[Bpkv, B_U], writes=[B_U])
                        S.op("act", lambda e: e.activation(out=Ubf[:], in_=U[:], func=AF.Copy), reads=[B_U], writes=[B_Ubf])

                    blk_stage1(0, 0)
                    for j in range(4):
                        if j + 1 < 4:
                            blk_stage1(j + 1, (j + 1) % 2)
                        blk_stage2(j, j % 2)
                        if j == 0 and g + 1 < NG:
                            rms_stats(l, g + 1, sqr, B_sqr, rstd, B_rstd)
                        if j == 1 and g + 1 < NG:
                            rms_scale(l, g + 1, 0, hg, B_hg, rstd, B_rstd)
                    for h in range(4):
                        S.op("dve", lambda e, h=h: e.tensor_scalar(out=Sst[:, h * 256:(h + 1) * 256],
                                                                   in0=U[:, h * 256:(h + 1) * 256], scalar1=dS[:, h:h + 1],
                                                                   scalar2=None, op0=ALU.mult),
                             reads=[B_U, B_dS], writes=[B_S])
                    if g == NG - 1:
                        wkd, Bwkd = kdup_load(l)
                        for hk in range(2):
                            ps, Bp = psum()
                            for k in range(8):
                                S.op("pe", lambda e, k=k, hk=hk, ps=ps, wkd=wkd: e.matmul(
                                    ps[:, 0:128], lhsT=v3(wkd, 8, 256)[:, k, hk * 128:(hk + 1) * 128],
                                    rhs=hg[:, k, 384:512], start=(k == 0), stop=(k == 7)), reads=[Bwkd, _bk(B_hg, k)], writes=[Bp])
                            qknorm(ps, Bp, sqk, B_sqk, rsk, B_rsk, msg[:, 4 + hk * 128:4 + (hk + 1) * 128], B_msg,
                                   spt[:, l, 17:18], 128)
                        wv, Bwv = wload(win_cols(l, C_VA, 128), lambda t: v3(t, 8, 128))
                        ps, Bp = psum()
                        for k in range(8):
                            S.op("pe", lambda e, k=k, ps=ps, wv=wv: e.matmul(ps[:, 0:128], lhsT=hg[:, k, 384:512],
                                                                             rhs=v3(wv, 8, 128)[:, k, :], start=(k == 0),
                                                                             stop=(k == 7)), reads=[Bwv, _bk(B_hg, k)], writes=[Bp])
                        S.op("act", lambda e, ps=ps: e.activation(out=msg[:, 260:388], in_=ps[:, 0:128], func=AF.Copy),
                             reads=[Bp], writes=[B_msg])
                if stop == ("A", l):
                    break
                S.op("act", lambda e: e.activation(out=msg[:, 0:4], in_=Cend[:], func=AF.Exp, scale=-1.0 / 16.0),
                     reads=[B_Cend], writes=[B_msg])
                B_cin = Buf("cin")
                B_cout = Buf("cout")
                S.dma("sp", lambda e: e.dma_start(out=cin1[l][:, 0:1024], in_=Sst[:]), "cin", reads=[B_S], writes=[B_cin])
                S.dma("sp", lambda e: e.dma_start(out=cin1[l][:, 1024:MSGW], in_=msg[:]), "cin", reads=[B_msg],
                      writes=[B_cin])
                S.cc(lambda e: e.collective_compute("AllGather", ALU.bypass, replica_groups=RG, ins=[cin1[l].opt()],
                                                    outs=[cout1[l].opt()]), "cc", reads=[B_cin], writes=[B_cout])
                S.barrier(skip=("cc",))
                AR.reset()
                hg = AR.get([8, G], BF16); B_hg = [Buf("hgB%d" % k_) for k_ in range(8)]
                sqr = AR.get([2, G], BF16); B_sqr = [Buf("sq0B"), Buf("sq1B")]
                rstd = AR.get([G], F32); B_rstd = Buf("rstdB")
                qn_off = AR.off
                qn = AR.get([4, G], BF16); B_qn = Buf("qn")
                kdup = AR.get([2, 128 + G], BF16); B_kdup = Buf("kdup")
                vaug = AR.get([5, 2, 128], BF16); B_vaug = Buf("vaug")
                pT_off = AR.off
                pT = AR.get([2, 2, G], BF16); B_pT = [[Buf("pT00"), Buf("pT01")], [Buf("pT10"), Buf("pT11")]]
                ya = AR.get([8, G], BF16, parts=64); B_ya = Buf("ya")
                ofp = AR.get([2, G], F32); B_ofp = Buf("ofp")
                merged_off = AR.off
                merged = AR.get([8, G], BF16); B_merged = Buf("merged")
                sr = AR.view(merged_off, [2, G], BF16); B_sr = B_merged
                sa = AR.get([G], F32); B_sa = Buf("sa")
                sbb = AR.get([G], F32); B_sbb = Buf("sbb")
                sqk = sqr[:, 0, :]; B_sqk = B_sqr[0]
                rsk = rstd; B_rsk = B_rstd
                dn = AR.get([G], F32, parts=64); B_dn = Buf("dn")
                ofp2 = AR.view(qn_off, [2, G], F32)
                sr2 = AR.view(pT_off, [2, G], BF16)
                sqr2 = AR.view(pT_off + 512, [2, G], BF16)
                GF = [dict(ofp=ofp, Bofp=[B_ofp], sr=sr, Bsr=[B_sr], sq=sqr, Bsq=[[B_sqr[0]], [B_sqr[1]]], rs=rstd,
                           Brs=[B_rstd]),
                      dict(ofp=ofp2, Bofp=[B_qn], sr=sr2, Bsr=[B_pT[0][0], B_pT[0][1]],
                           sq=sqr2, Bsq=[[B_pT[1][0]], [B_pT[1][1]]], rs=sa, Brs=[B_sa])]

                S.op("dve", lambda e: e.memset(vaug[:], 0.0), writes=[B_vaug])
                XOFF = ARW - (4 * MSGW + 1024 + 384)
                assert XOFF >= pT_off
                gath = AR.view(XOFF, [4, MSGW], F32); B_gath = Buf("gath")
                acc = AR.view(XOFF + 4 * MSGW, [1024], F32); B_acc = Buf("acc")
                kvacc = AR.view(XOFF + 4 * MSGW + 1024, [384], F32); B_kvacc = Buf("kvacc")
                fct = sc[:, 8:24]; B_fct = Buf("fct")
                S.dma("sp", lambda e: e.dma_start(out=gath[:], in_=cout1[l].rearrange("(j p) n -> p j n", p=128)), "gath",
                      reads=[B_cout], writes=[B_gath])

                def exchange_compute():
                    S.op("dve", lambda e: e.memset(acc[:], 0.0), writes=[B_acc])
                    for jr in range(3):
                        mj = flags[:, 4 + jr:5 + jr]
                        S.op("dve", lambda e, jr=jr, mj=mj: e.tensor_scalar(out=fct[:, 0:4], in0=gath[:, jr, 1024:1028],
                                                                            scalar1=-1.0, scalar2=mj, op0=ALU.add,
                                                                            op1=ALU.mult), reads=[B_gath, B_const],
                             writes=[B_fct])
                        S.op("dve", lambda e: e.tensor_scalar(out=fct[:, 0:4], in0=fct[:, 0:4], scalar1=1.0, scalar2=None,
                                                              op0=ALU.add), reads=[B_fct], writes=[B_fct])
                        for h in range(4):
                            hs = slice(h * 256, (h + 1) * 256)
                            S.op("dve", lambda e, h=h, hs=hs: e.tensor_scalar(out=acc[:, hs], in0=acc[:, hs],
                                                                              scalar1=fct[:, h:h + 1], scalar2=None,
                                                                              op0=ALU.mult), reads=[B_fct, B_acc],
                                 writes=[B_acc])
                            S.op("dve", lambda e, jr=jr, hs=hs, mj=mj: e.scalar_tensor_tensor(out=acc[:, hs],
                                                                                              in0=gath[:, jr, hs], scalar=mj,
                                                                                              in1=acc[:, hs], op0=ALU.mult,
                                                                                              op1=ALU.add),
                                 reads=[B_gath, B_acc, B_const], writes=[B_acc])
                    S.op("act", lambda e: e.activation(out=Sinit_bf[:], in_=acc[:], func=AF.Copy), reads=[B_acc],
                         writes=[B_Sinit])
                    S.op("dve", lambda e: e.tensor_scalar(out=kvacc[:], in0=gath[:, 0, 1028:1412], scalar1=flags[:, 0:1],
                                                          scalar2=None, op0=ALU.mult), reads=[B_gath, B_const],
                         writes=[B_kvacc])
                    for jr in range(1, 4):
                        S.op("dve", lambda e, jr=jr: e.scalar_tensor_tensor(out=kvacc[:], in0=gath[:, jr, 1028:1412],
                                                                            scalar=flags[:, jr:jr + 1], in1=kvacc[:],
                                                                            op0=ALU.mult, op1=ALU.add),
                             reads=[B_gath, B_kvacc, B_const], writes=[B_kvacc])
                    S.op("act", lambda e: e.activation(out=kprev[:], in_=kvacc[:, 0:256].rearrange("p (a b) -> p a b", a=2),
                                                       func=AF.Copy), reads=[B_kvacc], writes=[B_kvprev])
                    S.op("act", lambda e: e.activation(out=vprev[:], in_=kvacc[:, 256:384], func=AF.Copy), reads=[B_kvacc],
                         writes=[B_kvprev])
                    S.op("dve", lambda e: e.memset(sc[:, 30:31], 0.0), reads=[B_Sinit, B_kvprev],
                         writes=[B_gath, B_acc, B_kvacc, B_pT[0][0], B_pT[0][1], B_pT[1][0], B_pT[1][1], B_ya, B_ofp, B_merged,
                                 B_sa, B_sbb, B_dn])

                if stop == ("X", l):
                    break
                def outproj_chunk(gp, j, st_):
                    gsp = slice(gp * G, (gp + 1) * G)
                    if j % 2 == 0:
                        jj = j // 2
                        st_["w"] = wload(wout_d[l].rearrange("(k p) n -> p k n", p=128)[:, :, jj * 256:(jj + 1) * 256],
                                         lambda t: v3(t, 8, 256))
                    wo, Bwo = st_["w"]
                    c = j % 2
                    ps, Bp = proj_chunk(wo, Bwo, lambda t, k, c=c: v3(t, 8, 256)[:, k, c * 128:(c + 1) * 128], merged,
                                        B_merged)
                    S.op("dve", lambda e, ps=ps, j=j, gsp=gsp: e.tensor_tensor(out=xT[:, j, gsp], in0=ps[:],
                                                                               in1=xT[:, j, gsp], op=ALU.add),
                         reads=[Bp, B_x[gp]], writes=[B_x[gp]])

                pend_out = None
                for g in range(NG):
                    gs = slice(g * G, (g + 1) * G)
                    if g == 0:
                        rms_stats(l, g, sqr, B_sqr, rstd, B_rstd)
                    rms_scale(l, g, 0, hg, B_hg, rstd, B_rstd)
                    wqs = [wload(win_cols(l, C_QA + hp_ * 256, 256), lambda t: v3(t, 8, 256)) for hp_ in range(2)]
                    wkd, Bwkd = kdup_load(l)
                    if g > 0:
                        S.op("dve", lambda e: e.tensor_copy(out=kdup[:, :, 0:128], in_=kdup[:, :, G:G + 128]),
                             reads=[B_kdup], writes=[B_kdup])
                        S.op("dve", lambda e: e.tensor_copy(out=vaug[:, 0, :, :], in_=vaug[:, 4, :, :]), reads=[B_vaug],
                             writes=[B_vaug])
                    units = []
                    for c in range(4):
                        units.append((wqs[c // 2], (c % 2) * 128, qn[:, c, :], B_qn, qg8[:, 0:1]))
                    for hk in range(2):
                        units.append(((wkd, Bwkd), hk * 128, kdup[:, hk, 128:128 + G], B_kdup, spt[:, l, 17:18]))
                    prev_u = None
                    for u_ in units + [None]:
                        cur_u = None
                        if u_ is not None:
                            (wt_, Bwt_), co_, out_, Bout_, gain_ = u_
                            ps_, Bp_ = proj_chunk(wt_, Bwt_, lambda t, k, co_=co_: v3(t, 8, 256)[:, k, co_:co_ + 128], hg, B_hg)
                            cur_u = (ps_, Bp_, out_, Bout_, gain_)
                        if prev_u is not None:
                            qknorm(prev_u[0], prev_u[1], sqk, B_sqk, rsk, B_rsk, prev_u[2], prev_u[3], prev_u[4], G)
                        prev_u = cur_u
                    wv, Bwv = wload(win_cols(l, C_VA, 128), lambda t: v3(t, 8, 128))
                    ps, Bp = psum()
                    for blk in range(4):
                        for k in range(8):
                            S.op("pe", lambda e, k=k, blk=blk, ps=ps, wv=wv: e.matmul(
                                ps[:, blk * 128:(blk + 1) * 128], lhsT=hg[:, k, blk * 128:(blk + 1) * 128],
                                rhs=v3(wv, 8, 128)[:, k, :], start=(k == 0), stop=(k == 7)), reads=[Bwv, _bk(B_hg, k)], writes=[Bp])
                    S.op("act", lambda e, ps=ps: e.activation(
                        out=vaug[:, 1:5, :, 0:64], in_=ps[:].rearrange("p (b h d) -> p b h d", b=4, h=2), func=AF.Copy),
                         reads=[Bp], writes=[B_vaug])
                    if g == 0:
                        exchange_compute()
                        S.op("dve", lambda e: e.tensor_copy(out=kdup[:, :, 0:128], in_=kprev[:]), reads=[B_kvprev],
                             writes=[B_kdup])
                        for hk in range(2):
                            S.op("dve", lambda e, hk=hk: e.tensor_copy(out=vaug[:, 0, hk, 0:64],
                                                                        in_=vprev[:, hk * 64:(hk + 1) * 64]),
                                 reads=[B_kvprev], writes=[B_vaug])
                    if stop == ("B1", l):
                        raise _Stop()
                    def attn_stage1(i, hk, par):
                        psE, BpE = psum()
                        psO, BpO = psum()
                        for kb in range(2):
                            koff = i * 128 + kb * 128
                            for b in range(4):
                                hh = HORD[b]
                                chunk = 2 * hk + hh // 2
                                p0 = 64 * (hh % 2)
                                pst_, Bpt_ = (psE, BpE) if b < 2 else (psO, BpO)
                                cb = kb * 256 + (b % 2) * 128
                                S.op("pe", lambda e, pst_=pst_, cb=cb, chunk=chunk, p0=p0, koff=koff, hk=hk, i=i: e.matmul(
                                    pst_[:, cb:cb + 128], lhsT=kdup[p0:p0 + 64, hk, koff:koff + 128],
                                    rhs=qn[p0:p0 + 64, chunk, i * 128:(i + 1) * 128], start=True, stop=True),
                                     reads=[B_kdup, B_qn], writes=[Bpt_])
                        S.op("act", lambda e, psE=psE, par=par: e.activation(
                            out=pT[:, par, :, 0:256], in_=psE[:].rearrange("p (k n) -> p k n", k=2), func=AF.Exp),
                             reads=[BpE], writes=[B_pT[par][0], B_pT[par][1]])
                        S.op("act", lambda e, psO=psO, par=par: e.activation(
                            out=pT[:, par, :, 256:512], in_=psO[:].rearrange("p (k n) -> p k n", k=2), func=AF.Exp),
                             reads=[BpO], writes=[B_pT[par][0], B_pT[par][1]])
                        for kb in range(2):
                            if kb == 1:
                                mk = maskc
                            else:
                                mk = maskp0 if (g == 0 and i == 0) else maskp
                            S.op("dve", lambda e, kb=kb, mk=mk, par=par: e.tensor_tensor(
                                out=pT[:, par, kb, :], in0=pT[:, par, kb, :], in1=mk[:], op=ALU.mult),
                                 reads=[B_pT[par][kb], B_const], writes=[B_pT[par][kb]])

                    dnb = [(dn, B_dn), (sbb[0:64, :], B_sbb)]

                    def attn_stage2a(i, hk, par):
                        pso, Bpo = psum()
                        psd, Bpd = psum()
                        dn_, Bdn_ = dnb[par]
                        for kb in range(2):
                            S.op("pe", lambda e, pso=pso, kb=kb, hk=hk, i=i, par=par: e.matmul(
                                pso[:, :], lhsT=vaug[:, i + kb, hk, :], rhs=pT[:, par, kb, :], start=(kb == 0),
                                stop=(kb == 1)), reads=[B_vaug, B_pT[par][kb]], writes=[Bpo])
                        for kb in range(2):
                            S.op("pe", lambda e, psd=psd, kb=kb, par=par: e.matmul(
                                psd[:, :], lhsT=ones128[:], rhs=pT[:, par, kb, :], start=(kb == 0), stop=(kb == 1)),
                                 reads=[B_const, B_pT[par][kb]], writes=[Bpd])
                        S.op("dve", lambda e, psd=psd, hk=hk, dn_=dn_: e.tensor_tensor(out=dn_, in0=psd[0:64, :],
                                                                                       in1=esink[:, hk, :], op=ALU.add),
                             reads=[Bpd, B_lay], writes=[Bdn_])
                        return pso, Bpo

                    def attn_stage2b(i, hk, par, pso, Bpo):
                        dn_, Bdn_ = dnb[par]
                        S.op("act", lambda e, dn_=dn_: e.activation(out=dn_, in_=dn_, func=AF.Ln), reads=[Bdn_],
                             writes=[Bdn_])
                        S.op("act", lambda e, dn_=dn_: e.activation(out=dn_, in_=dn_, func=AF.Exp, scale=-1.0),
                             reads=[Bdn_], writes=[Bdn_])
                        S.op("dve", lambda e, pso=pso, hk=hk, i=i, dn_=dn_: e.tensor_tensor(
                            out=ya[:, hk * 4:(hk + 1) * 4, i * 128:(i + 1) * 128],
                            in0=pso[0:64, :].rearrange("p (h q) -> p h q", h=4),
                            in1=dn_.rearrange("p (h q) -> p h q", h=4), op=ALU.mult),
                             reads=[Bpo, Bdn_], writes=[B_ya])

                    its = [(i, hk) for i in range(4) for hk in range(2)]
                    attn_stage1(its[0][0], its[0][1], 0)
                    pend = None
                    for n_, (i, hk) in enumerate(its):
                        if n_ + 1 < len(its):
                            attn_stage1(its[n_ + 1][0], its[n_ + 1][1], (n_ + 1) % 2)
                        pso_, Bpo_ = attn_stage2a(i, hk, n_ % 2)
                        if pend is not None:
                            attn_stage2b(*pend)
                        pend = (i, hk, n_ % 2, pso_, Bpo_)
                        if pend_out is not None:
                            outproj_chunk(pend_out[0], n_, pend_out[1])
                    attn_stage2b(*pend)
                    pend_out = None
                    if stop == ("B2", l):
                        raise _Stop()
                    def gla_p1(hp):
                        for h in (2 * hp, 2 * hp + 1):
                            F_ = GF[h % 2]
                            wr, Bwr = wload(win_cols(l, C_RB + h * 256, 256), lambda t: v3(t, 8, 256))
                            for c in range(2):
                                psc, Bpc = psum()
                                ec = h * 256 + c * 128
                                S.op("pe", lambda e, psc=psc, ec=ec, h=h: e.matmul(psc[:], lhsT=Sinit_bf[:, ec:ec + 128],
                                                                                   rhs=qhat[:, h, gs], start=True, stop=True),
                                     reads=[B_Sinit, B_qhat[g]], writes=[Bpc])
                                S.op("dve", lambda e, psc=psc, c=c, h=h, F_=F_: e.tensor_tensor(
                                    out=F_["ofp"][:, c, :], in0=psc[:], in1=oloc[:, h * 2 + c, gs], op=ALU.add),
                                     reads=[Bpc, B_oloc[g]], writes=F_["Bofp"])
                                S.op("act", lambda e, c=c, F_=F_: e.activation(out=F_["sq"][:, c, :], in_=F_["ofp"][:, c, :],
                                                                               func=AF.Square),
                                     reads=F_["Bofp"], writes=F_["Bsq"][c])
                            for c in range(2):
                                psr, Bpr = proj_chunk(wr, Bwr, lambda t, k, c=c: v3(t, 8, 256)[:, k, c * 128:(c + 1) * 128],
                                                      hg, B_hg)
                                S.op("act", lambda e, c=c, psr=psr, F_=F_: e.activation(out=F_["sr"][:, c, :], in_=psr[:],
                                                                                        func=AF.Silu),
                                     reads=[Bpr], writes=F_["Bsr"])

                    def gla_p2(hp):
                        for h in (2 * hp, 2 * hp + 1):
                            F_ = GF[h % 2]
                            pss, Bpss = psum()
                            for c in range(2):
                                S.op("pe", lambda e, c=c, pss=pss, F_=F_: e.matmul(pss[:], lhsT=ones_dv[:],
                                                                                   rhs=F_["sq"][:, c, :], start=(c == 0),
                                                                                   stop=(c == 1)),
                                     reads=F_["Bsq"][c] + [B_const], writes=[Bpss])
                            S.op("act", lambda e, pss=pss, F_=F_: e.activation(out=F_["rs"][:], in_=pss[:], func=AF.Ln,
                                                                               bias=epsf[:, 0:1], scale=1.0),
                                 reads=[Bpss, B_const], writes=F_["Brs"])
                            S.op("act", lambda e, F_=F_: e.activation(out=F_["rs"][:], in_=F_["rs"][:], func=AF.Exp,
                                                                      scale=-0.5), reads=F_["Brs"], writes=F_["Brs"])
                            for c in range(2):
                                S.op("dve", lambda e, c=c, F_=F_: e.scalar_tensor_tensor(
                                    out=F_["ofp"][:, c, :], in0=F_["ofp"][:, c, :], scalar=spt[:, l, 22 + c:23 + c],
                                    in1=F_["rs"][:], op0=ALU.mult, op1=ALU.mult),
                                     reads=F_["Bofp"] + F_["Brs"] + [B_const], writes=F_["Bofp"])
                                S.op("dve", lambda e, c=c, h=h, F_=F_: e.tensor_tensor(
                                    out=oloc[:, h * 2 + c, gs], in0=F_["ofp"][:, c, :], in1=F_["sr"][:, c, :], op=ALU.mult),
                                     reads=F_["Bofp"] + F_["Bsr"], writes=[B_oloc[g]])
                    def merge_load(st2):
                        c2 = slice(st2 * 256, (st2 + 1) * 256)
                        wA, BwA = wload(wba_d[l].rearrange("(h p) n -> p h n", p=64)[:, :, c2],
                                        lambda t: t[0:64, :].rearrange("p (k n) -> p k n", k=8))
                        wGa, BwGa = wload(win_cols(l, C_GA + st2 * 256, 256), lambda t: v3(t, 8, 256))
                        wGb, BwGb = wload(win_cols(l, C_GB + st2 * 256, 256), lambda t: v3(t, 8, 256))
                        wB, BwB = wload(wbb_d[l].rearrange("(k p) n -> p k n", p=128)[:, :, c2], lambda t: v3(t, 8, 256))
                        return (wA, BwA, wGa, BwGa, wGb, BwGb, wB, BwB)

                    def merge_pre(st2, jj, W_):
                        if True:
                            (wA, BwA, wGa, BwGa, wGb, BwGb, wB, BwB) = W_
                            j = st2 * 2 + jj
                            cj = slice(jj * 128, (jj + 1) * 128)
                            psA, BpA = psum()
                            for hd in range(8):
                                S.op("pe", lambda e, psA=psA, hd=hd, wA=wA, cj=cj: e.matmul(
                                    psA[:], lhsT=wA[0:64, :].rearrange("p (k n) -> p k n", k=8)[:, (hd // 4) * 4 + HORD[hd % 4], cj],
                                    rhs=ya[:, hd, :], start=(hd == 0), stop=(hd == 7)),
                                     reads=[BwA, B_ya], writes=[BpA])
                            psGa, BpGa = proj_chunk(wGa, BwGa, lambda t, k, cj=cj: v3(t, 8, 256)[:, k, cj], hg, B_hg)
                            psGb, BpGb = proj_chunk(wGb, BwGb, lambda t, k, cj=cj: v3(t, 8, 256)[:, k, cj], hg, B_hg)
                            return (psA, BpA, psGa, BpGa, psGb, BpGb)

                    def merge_post(st2, jj, W_, P_):
                        if True:
                            (wA, BwA, wGa, BwGa, wGb, BwGb, wB, BwB) = W_
                            (psA, BpA, psGa, BpGa, psGb, BpGb) = P_
                            j = st2 * 2 + jj
                            cj = slice(jj * 128, (jj + 1) * 128)
                            S.op("act", lambda e, psGa=psGa: e.activation(out=sa[:], in_=psGa[:], func=AF.Sigmoid),
                                 reads=[BpGa], writes=[B_sa])
                            S.op("act", lambda e, psGb=psGb: e.activation(out=sbb[:], in_=psGb[:], func=AF.Sigmoid),
                                 reads=[BpGb], writes=[B_sbb])
                            S.op("dve", lambda e, psA=psA: e.tensor_tensor(out=sa[:], in0=psA[:], in1=sa[:], op=ALU.mult),
                                 reads=[BpA, B_sa], writes=[B_sa])
                            psB, BpB = psum()
                            for k in range(8):
                                S.op("pe", lambda e, psB=psB, k=k, wB=wB, cj=cj: e.matmul(
                                    psB[:], lhsT=v3(wB, 8, 256)[:, k, cj], rhs=oloc[:, k, gs], start=(k == 0), stop=(k == 7)),
                                     reads=[BwB, B_oloc[g]], writes=[BpB])
                            S.op("dve", lambda e, psB=psB: e.tensor_tensor(out=sbb[:], in0=psB[:], in1=sbb[:], op=ALU.mult),
                                 reads=[BpB, B_sbb], writes=[B_sbb])
                            S.op("dve", lambda e, j=j: e.tensor_tensor(out=merged[:, j, :], in0=sa[:], in1=sbb[:], op=ALU.add),
                                 reads=[B_sa, B_sbb], writes=[B_merged])


                    gla_p1(0)
                    gla_p2(0)
                    gla_p1(1)
                    W0_ = merge_load(0)
                    P0_ = merge_pre(0, 0, W0_)
                    P1_ = merge_pre(0, 1, W0_)
                    gla_p2(1)
                    merge_post(0, 0, W0_, P0_)
                    merge_post(0, 1, W0_, P1_)
                    if g + 1 < NG:
                        rms_stats(l, g + 1, sqr, B_sqr, rstd, B_rstd)
                    for st2 in range(1, 4):
                        W_ = merge_load(st2)
                        for jj in range(2):
                            merge_post(st2, jj, W_, merge_pre(st2, jj, W_))
                    if stop == ("B3", l):
                        raise _Stop()
                    if stop == ("B4", l):
                        raise _Stop()
                    if g == NG - 1:
                        st_ = {}
                        for j in range(8):
                            outproj_chunk(g, j, st_)
                    else:
                        pend_out = (g, {})
                if stop == ("mix", l):
                    break
                S.barrier()
                AR.reset()
                hgs_ = [(AR.get([8, G], BF16), [Buf("hgC0_%d" % k_) for k_ in range(8)]), (AR.get([8, G], BF16), [Buf("hgC1_%d" % k_) for k_ in range(8)])]
                sqr = AR.get([2, G], BF16); B_sqr = [Buf("sq0C"), Buf("sq1C")]
                rstd = AR.get([G], F32); B_rstd = Buf("rstdC")
                gt = AR.get([NJ, G], BF16); B_gt = Buf("gt")
                aext = AR.get([2, G + 2], F32); B_aext = [Buf("aext0"), Buf("aext1")]
                yv = AR.get([2, G], F32); B_yv = [Buf("yv0"), Buf("yv1")]
                msg2 = AR.get([16], F32); B_msg2 = Buf("msg2")
                gath2 = AR.get([4, 16], F32); B_gath2 = Buf("gath2")
                xh = AR.get([8, 2], F32); B_xh = Buf("xh")
                sq2 = AR.get([8, 2], BF16); B_sq2 = Buf("sq2")
                rs2 = AR.get([2], F32); B_rs2 = Buf("rs2")
                h2h1 = AR.get([8, 2], BF16); B_h2h1 = Buf("h2h1")
                B_cin2 = Buf("cin2"); B_cout2 = Buf("cout2")

                def halo_norm(src3, B_src, dst, B_dst):
                    S.op("act", lambda e: e.activation(out=sq2[:], in_=src3, func=AF.Square), reads=B_src, writes=[B_sq2])
                    ps, Bp = psum()
                    for k in range(8):
                        S.op("pe", lambda e, k=k, ps=ps: e.matmul(ps[:, 0:2], lhsT=ones_dm[:], rhs=sq2[:, k, :],
                                                                  start=(k == 0), stop=(k == 7)),
                             reads=[B_sq2, B_const], writes=[Bp])
                    rsqrt_eps(rs2[:], ps[:, 0:2], [Bp], B_rs2)
                    for k in range(8):
                        S.op("dve", lambda e, k=k: e.scalar_tensor_tensor(out=dst[:, k, :], in0=src3[:, k, :],
                                                                          scalar=spt[:, l, 8 + k:9 + k], in1=rs2[:],
                                                                          op0=ALU.mult, op1=ALU.mult),
                             reads=B_src + [B_rs2, B_const], writes=[B_dst])

                S.op("act", lambda e: e.activation(out=msg2[:].rearrange("p (k t) -> p k t", k=8),
                                                   in_=xT[:, :, TOK - 2:TOK], func=AF.Copy), reads=[B_x[NG - 1]],
                     writes=[B_msg2])
                S.dma("sp", lambda e: e.dma_start(out=cin2[l], in_=msg2[:]), "cin", reads=[B_msg2], writes=[B_cin2])
                S.cc(lambda e: e.collective_compute("AllGather", ALU.bypass, replica_groups=RG, ins=[cin2[l].opt()],
                                                    outs=[cout2[l].opt()]), "cc", reads=[B_cin2], writes=[B_cout2])
                S.dma("sp", lambda e: e.dma_start(out=gath2[:], in_=cout2[l].rearrange("(j p) n -> p j n", p=128)), "gath",
                      reads=[B_cout2], writes=[B_gath2])
                halo_norm(xT[:, :, G - 2:G], [B_x[0]], h2h1, B_h2h1)
                order_ = [1, 2, 3, 0]
                rmsnorm_group(l, order_[0], 8, hgs_[0][0], hgs_[0][1], sqr, B_sqr, rstd, B_rstd)
                for p_, g in enumerate(order_):
                    gs = slice(g * G, (g + 1) * G)
                    if g == 0:
                        xhf = xh[:].rearrange("p k t -> p (k t)")
                        S.op("dve", lambda e: e.tensor_scalar(out=xhf, in0=gath2[:, 0, :], scalar1=flags[:, 0:1],
                                                              scalar2=None, op0=ALU.mult), reads=[B_gath2, B_const],
                             writes=[B_xh])
                        for jr in range(1, 4):
                            S.op("dve", lambda e, jr=jr: e.scalar_tensor_tensor(out=xhf, in0=gath2[:, jr, :],
                                                                                scalar=flags[:, jr:jr + 1], in1=xhf,
                                                                                op0=ALU.mult, op1=ALU.add),
                                 reads=[B_gath2, B_xh, B_const], writes=[B_xh])
                        halo_norm(xh[:], [B_xh], h2halo, B_h2halo)
                    hg, B_hg = hgs_[p_ % 2]
                    for jp in range(NJ // 2):
                        wfa, Bwfa = wload(wfi_d[l].rearrange("(k p) n -> p k n", p=128)[:, :, jp * 256:(jp + 1) * 256],
                                          lambda t: v3(t, 8, 256))
                        wfu, Bwfu = wload(
                            wfi_d[l].rearrange("(k p) n -> p k n", p=128)[:, :, D_FF + jp * 256:D_FF + (jp + 1) * 256],
                            lambda t: v3(t, 8, 256))
                        for jj in range(2):
                            j = jp * 2 + jj
                            r = j % 2
                            cw = 46 + j * 3
                            psa, Bpa = proj_chunk(wfa, Bwfa, lambda t, k, jj=jj: v3(t, 8, 256)[:, k, jj * 128:(jj + 1) * 128],
                                                  hg, B_hg)
                            psu, Bpu = proj_chunk(wfu, Bwfu, lambda t, k, jj=jj: v3(t, 8, 256)[:, k, jj * 128:(jj + 1) * 128],
                                                  hg, B_hg)
                            if g in (0, 1):
                                hsrc, B_hsrc = (h2halo, B_h2halo) if g == 0 else (h2h1, B_h2h1)
                                psh, Bph = psum()
                                for k in range(8):
                                    S.op("pe", lambda e, k=k, psh=psh, jj=jj, wfa=wfa, hsrc=hsrc: e.matmul(
                                        psh[:, 0:2], lhsT=v3(wfa, 8, 256)[:, k, jj * 128:(jj + 1) * 128], rhs=hsrc[:, k, :],
                                        start=(k == 0), stop=(k == 7)), reads=[Bwfa, B_hsrc], writes=[Bph])
                                S.op("act", lambda e, psh=psh, r=r: e.activation(out=aext[:, r, 0:2], in_=psh[:, 0:2],
                                                                                 func=AF.Copy), reads=[Bph],
                                     writes=[B_aext[r]])
                            else:
                                S.op("act", lambda e, r=r, j=j: e.activation(out=aext[:, r, 0:2], in_=ahalo[:, j, :], func=AF.Copy),
                                     reads=[B_ahalo], writes=[B_aext[r]])
                            S.op("act", lambda e, psa=psa, r=r: e.activation(out=aext[:, r, 2:G + 2], in_=psa[:],
                                                                             func=AF.Copy), reads=[Bpa], writes=[B_aext[r]])
                            S.op("act", lambda e, r=r, j=j: e.activation(out=ahalo[:, j, :], in_=aext[:, r, G:G + 2], func=AF.Copy),
                                 reads=[B_aext[r]], writes=[B_ahalo])
                            S.op("dve", lambda e, r=r, cw=cw, j=j: e.tensor_scalar(
                                out=yv[:, r, :], in0=aext[:, r, 2:G + 2], scalar1=spt[:, l, cw + 2:cw + 3],
                                scalar2=spt[:, l, 24 + j:25 + j], op0=ALU.mult, op1=ALU.add),
                                 reads=[B_aext[r], B_const], writes=[B_yv[r]])
                            S.op("dve", lambda e, r=r, cw=cw: e.scalar_tensor_tensor(
                                out=yv[:, r, :], in0=aext[:, r, 1:G + 1], scalar=spt[:, l, cw + 1:cw + 2], in1=yv[:, r, :],
                                op0=ALU.mult, op1=ALU.add), reads=[B_aext[r], B_yv[r], B_const], writes=[B_yv[r]])
                            S.op("dve", lambda e, r=r, cw=cw: e.scalar_tensor_tensor(
                                out=yv[:, r, :], in0=aext[:, r, 0:G], scalar=spt[:, l, cw:cw + 1], in1=yv[:, r, :],
                                op0=ALU.mult, op1=ALU.add), reads=[B_aext[r], B_yv[r], B_const], writes=[B_yv[r]])
                            S.op("act", lambda e, r=r: e.activation(out=yv[:, r, :], in_=yv[:, r, :], func=AF.Silu),
                                 reads=[B_yv[r]], writes=[B_yv[r]])
                            S.op("dve", lambda e, r=r, j=j, psu=psu: e.tensor_tensor(out=gt[:, j, :], in0=psu[:],
                                                                                     in1=yv[:, r, :], op=ALU.mult),
                                 reads=[Bpu, B_yv[r]], writes=[B_gt])
                    for jp2 in range(4):
                        pss2 = [psum(), psum()]
                        for part in range(3):
                            k0 = part * 8
                            nk = min(8, NJ - k0)
                            wo, Bwo = wload(
                                wfo_d[l].rearrange("(c p) n -> p c n", p=128)[:, k0:k0 + nk, jp2 * 256:(jp2 + 1) * 256],
                                lambda t, nk=nk: v3(t, nk, 256))
                            for jj in range(2):
                                ps, Bp = pss2[jj]
                                for kk in range(nk):
                                    c = k0 + kk
                                    S.op("pe", lambda e, ps=ps, kk=kk, c=c, jj=jj, wo=wo, nk=nk: e.matmul(
                                        ps[:], lhsT=v3(wo, nk, 256)[:, kk, jj * 128:(jj + 1) * 128], rhs=gt[:, c, :],
                                        start=(c == 0), stop=(c == NJ - 1)), reads=[Bwo, B_gt], writes=[Bp])
                        for jj in range(2):
                            ps, Bp = pss2[jj]
                            jo = jp2 * 2 + jj
                            S.op("dve", lambda e, ps=ps, jo=jo: e.tensor_tensor(out=xT[:, jo, gs], in0=ps[:], in1=xT[:, jo, gs],
                                                                                op=ALU.add),
                                 reads=[Bp, B_x[g]], writes=[B_x[g]])
                        if l == DEPTH - 1 and stop is None:
                            if p_ == NG - 1:
                                jo0 = jp2 * 2
                                S.dma("sp", lambda e, jo0=jo0, gs=gs: e.dma_start(out=y_d[:, jo0:jo0 + 2, gs],
                                                                                 in_=xT[:, jo0:jo0 + 2, gs]),
                                      "out", reads=[B_x[g]])
                            elif jp2 == 3:
                                S.dma("sp", lambda e, gs=gs: e.dma_start(out=y_d[:, :, gs], in_=xT[:, :, gs]), "out",
                                      reads=[B_x[g]])
                            out_done[0] = True
                        if jp2 == 0 and p_ + 1 < NG:
                            rmsnorm_group(l, order_[p_ + 1], 8, hgs_[(p_ + 1) % 2][0], hgs_[(p_ + 1) % 2][1], sqr, B_sqr,
                                          rstd, B_rstd)
                if stop == ("layer", l):
                    break
        except _Stop:
            pass

        for g in ((1, 2, 3, 0) if not out_done[0] else ()):
            S.dma("sp", lambda e, g=g: e.dma_start(out=y_d[:, :, g * G:(g + 1) * G], in_=xT[:, :, g * G:(g + 1) * G]),
                  "out", reads=[B_x[g]])
        S.final_wait("sp")
        S.emit()
    return nc


def _layout_inputs(inputs):
    f32 = np.float32
    x = np.asarray(inputs["x"], f32)
    L = DEPTH
    sp = np.zeros((L, 128, NSP), f32)
    p = np.arange(128)
    for l in range(L):
        sp[l, :, 0:8] = np.asarray(inputs["ln_mix_g"][l], f32).reshape(8, 128).T
        sp[l, :, 8:16] = np.asarray(inputs["ln_ffn_g"][l], f32).reshape(8, 128).T
        sp[l, :, 16] = np.asarray(inputs["q_norm_g"][l], f32)[p % 64]
        sp[l, :, 17] = np.asarray(inputs["k_norm_g"][l], f32)[p % 64]
        sp[l, :, 18:22] = np.asarray(inputs["b_gk"][l], f32).reshape(4, 128).T
        sp[l, :, 22:24] = np.asarray(inputs["gla_norm_g"][l], f32).reshape(2, 128).T
        sp[l, :, 24:46] = np.asarray(inputs["conv_b"][l], f32).reshape(NJ, 128).T
        cw = np.asarray(inputs["conv_w"][l], f32)
        sp[l, :, 46:112] = cw.reshape(3, NJ, 128).transpose(2, 1, 0).reshape(128, NJ * 3)
        sp[l, :, 112:120] = np.asarray(inputs["sinks"][l], f32)[None, :]
    shared = {
        "sp": sp,
        "w_gk_up": np.ascontiguousarray(np.asarray(inputs["w_gk_up"], f32)),
        "w_in": np.ascontiguousarray(np.asarray(inputs["w_in"], f32)),
        "w_branch_a": np.ascontiguousarray(np.asarray(inputs["w_branch_a"], f32)),
        "w_branch_b": np.ascontiguousarray(np.asarray(inputs["w_branch_b"], f32)),
        "w_out": np.ascontiguousarray(np.asarray(inputs["w_out"], f32)),
        "w_ffn_in": np.ascontiguousarray(np.asarray(inputs["w_ffn_in"], f32)),
        "w_ffn_out": np.ascontiguousarray(np.asarray(inputs["w_ffn_out"], f32)),
    }
    in_maps = []
    for c in range(8):
        b, r = c // 4, c % 4
        xs = x[b, r * TOK:(r + 1) * TOK, :]
        xt = np.ascontiguousarray(xs.T.reshape(8, 128, TOK).transpose(1, 0, 2))
        fl = np.zeros((128, 8), f32)
        if r > 0:
            fl[:, r - 1] = 1.0
            fl[:, 7] = 1.0
        for j in range(3):
            fl[:, 4 + j] = 1.0 if j < r else 0.0
        m = dict(shared)
        m["xT"] = xt
        m["flags"] = fl
        in_maps.append(m)
    return in_maps


def _gather_out(results):
    out = np.zeros((2, 4 * TOK, D), np.float32)
    for c in range(8):
        b, r = c // 4, c % 4
        yt = np.asarray(results[c]["yT"])
        out[b, r * TOK:(r + 1) * TOK, :] = yt.transpose(2, 1, 0).reshape(TOK, D)
    return out


_NC_CACHE = {}


def kernel(**inputs):
    if "nc" not in _NC_CACHE:
        _NC_CACHE["nc"] = build()
    nc = _NC_CACHE["nc"]
    in_maps = _layout_inputs(inputs)
    res = run_bass_kernel_spmd(nc, in_maps, core_ids=list(range(8)))
    return _gather_out(res.results)
```

```python
import numpy as np
from contextlib import ExitStack
import concourse.bass as bass
import concourse.mybir as mybir
from concourse.bass_utils import run_bass_kernel_spmd

F32 = mybir.dt.float32
BF16 = mybir.dt.bfloat16
I32 = mybir.dt.int32
AF = mybir.ActivationFunctionType
ALU = mybir.AluOpType

D = 1024
TOK = 2048
G = 512
NG = TOK // G
DEPTH = 2
IN_W = 5904
D_FF = 2816
NJ = D_FF // 128
EPS = 1e-6
C_QA, C_KA, C_VA, C_QB, C_KB, C_VB, C_RB, C_GK, C_GA, C_GB = 0, 512, 640, 768, 1280, 1792, 2816, 3840, 3856, 4880
NSP = 120
HORD = (0, 2, 1, 3)
MSGW = 1024 + 4 + 256 + 128
EPOCH = 12000


import types


def _snap(fn):
    if fn is None or fn.__closure__ is None:
        return fn
    cells = tuple(types.CellType(c.cell_contents) for c in fn.__closure__)
    return types.FunctionType(fn.__code__, fn.__globals__, fn.__name__, fn.__defaults__, cells)


class _Stop(Exception):
    pass


def _bk(B, k):
    return B[k] if isinstance(B, list) else B


class Buf:
    __slots__ = ("name", "w", "r", "excl")

    def __init__(self, name, excl=False):
        self.name, self.w, self.r, self.excl = name, None, [], excl


class Sched:
    ENG = ("pe", "act", "dve", "pool", "sp")

    def __init__(self, nc, es):
        self.nc, self.es = nc, es
        self.prog = {e: [] for e in self.ENG}
        self.cnt = {e: 0 for e in self.ENG}
        self.nsem = 0
        self.sems = {e: self._newsem() for e in self.ENG}
        self.waited = {e: {} for e in self.ENG}
        self.dsem = {}
        self.last_tok = {e: None for e in self.ENG}
        self.dma_toks = {}

    def _newsem(self):
        self.nsem += 1
        return self.es.enter_context(self.nc.semaphore("s%d" % self.nsem))

    def _waits(self, eng, deps):
        waits = []
        for (sem, val, src) in deps:
            if src == eng == "pe":
                continue
            key = id(sem)
            if self.waited[eng].get(key, 0) >= val:
                continue
            self.waited[eng][key] = val
            waits.append((sem, val))
        return waits

    def _deps(self, reads, writes):
        deps = []
        for b in reads:
            if b.excl:
                deps += b.r
            if b.w is not None:
                deps.append(b.w)
        for b in writes:
            deps += b.r
            if b.w is not None:
                deps.append(b.w)
        return deps

    def _commit(self, tok, reads, writes):
        for b in reads:
            if b.excl:
                b.w, b.r = tok, []
            else:
                b.r.append(tok)
        for b in writes:
            b.w, b.r = tok, []

    def op(self, eng, fn, reads=(), writes=()):
        waits = self._waits(eng, self._deps(reads, writes))
        if self.cnt[eng] >= EPOCH:
            self.sems[eng] = self._newsem()
            self.cnt[eng] = 0
        self.cnt[eng] += 1
        sem = self.sems[eng]
        tok = (sem, self.cnt[eng], eng)
        self.prog[eng].append((waits, _snap(fn), sem, 1))
        self.last_tok[eng] = tok
        self._commit(tok, reads, writes)

    def dma(self, eng, fn, semname, reads=(), writes=()):
        waits = self._waits(eng, self._deps(reads, writes))
        if semname not in self.dsem:
            self.dsem[semname] = [self._newsem(), 0]
        s = self.dsem[semname]
        s[1] += 16
        tok = (s[0], s[1], "dma")
        self.prog[eng].append((waits, _snap(fn), s[0], 16))
        self.dma_toks[semname] = tok
        self._commit(tok, reads, writes)

    def cc(self, fn, semname, reads=(), writes=()):
        waits = self._waits("pool", self._deps(reads, writes))
        if semname not in self.dsem:
            self.dsem[semname] = [self._newsem(), 0]
        s = self.dsem[semname]
        s[1] += 1
        tok = (s[0], s[1], "dma")
        self.prog["pool"].append((waits, _snap(fn), s[0], None))
        self.dma_toks[semname] = tok
        self._commit(tok, reads, writes)

    def barrier(self, skip=()):
        toks = [t for t in self.last_tok.values() if t is not None] + \
               [t for n_, t in self.dma_toks.items() if n_ not in skip]
        for e in self.ENG:
            w = self._waits(e, [t for t in toks if t[2] != e])
            if w:
                self.prog[e].append((w, None, None, 0))

    def final_wait(self, eng):
        toks = [t for t in self.last_tok.values() if t is not None] + list(self.dma_toks.values())
        w = self._waits(eng, [t for t in toks if t[2] != eng])
        self.prog[eng].append((w, None, None, 0))

    def emit(self):
        nc = self.nc
        prog = self.prog

        def replay(e, lst):
            for waits, fn, sem, inc in lst:
                for (s, v) in waits:
                    e.wait_ge(s, v)
                if fn is None:
                    continue
                ins = fn(e)
                if inc is None:
                    ins.then_inc(sem)
                else:
                    ins.then_inc(sem, inc)

        with nc.Block() as block:
            @block.tensor
            def _(e):
                replay(e, prog["pe"])

            @block.scalar
            def _(e):
                replay(e, prog["act"])

            @block.vector
            def _(e):
                replay(e, prog["dve"])

            @block.gpsimd
            def _(e):
                replay(e, prog["pool"])

            @block.sync
            def _(e):
                replay(e, prog["sp"])


class Arena:
    def __init__(self, ap_f32, nwords):
        self.base, self.n, self.off = ap_f32, nwords, 0

    def reset(self):
        self.off = 0

    def view(self, off, shape, dtype, parts=128):
        save = self.off
        self.off = off
        ap = self.get(shape, dtype, parts)
        self.off = save
        return ap

    def get(self, shape, dtype, parts=128):
        n = int(np.prod(shape))
        words = n if dtype in (F32, I32) else (n + 1) // 2
        assert self.off + words <= self.n, ("arena overflow", self.off, words, self.n)
        ap = self.base[0:parts, self.off:self.off + words]
        self.off += words
        if dtype != F32:
            ap = ap.bitcast(dtype)
            if dtype == BF16 and n % 2:
                ap = ap[:, 0:n]
        if len(shape) == 2:
            ap = ap.rearrange("p (a b) -> p a b", a=shape[0])
        elif len(shape) == 3:
            ap = ap.rearrange("p (a b c) -> p a b c", a=shape[0], b=shape[1])
        return ap


def build(stop=None):
    nc = bass.Bass("TRN2", target_bir_lowering=False)
    dt = nc.dram_tensor
    x_d = dt("xT", [128, 8, TOK], F32, kind="ExternalInput").ap()
    flags_d = dt("flags", [128, 8], F32, kind="ExternalInput").ap()
    sp_d = dt("sp", [DEPTH, 128, NSP], F32, kind="ExternalInput").ap()
    wup_d = dt("w_gk_up", [DEPTH, 16, 512], F32, kind="ExternalInput").ap()
    win_d = dt("w_in", [DEPTH, D, IN_W], F32, kind="ExternalInput").ap()
    wba_d = dt("w_branch_a", [DEPTH, 512, D], F32, kind="ExternalInput").ap()
    wbb_d = dt("w_branch_b", [DEPTH, D, D], F32, kind="ExternalInput").ap()
    wout_d = dt("w_out", [DEPTH, D, D], F32, kind="ExternalInput").ap()
    wfi_d = dt("w_ffn_in", [DEPTH, D, 2 * D_FF], F32, kind="ExternalInput").ap()
    wfo_d = dt("w_ffn_out", [DEPTH, D_FF, D], F32, kind="ExternalInput").ap()
    y_d = dt("yT", [128, 8, TOK], F32, kind="ExternalOutput").ap()
    cin1 = [dt("cin1_%d" % l, [128, MSGW], F32, kind="Internal").ap() for l in range(DEPTH)]
    cout1 = [dt("cout1_%d" % l, [512, MSGW], F32, kind="Internal").ap() for l in range(DEPTH)]
    cin2 = [dt("cin2_%d" % l, [128, 16], F32, kind="Internal").ap() for l in range(DEPTH)]
    cout2 = [dt("cout2_%d" % l, [512, 16], F32, kind="Internal").ap() for l in range(DEPTH)]
    RG = [[0, 1, 2, 3], [4, 5, 6, 7]]

    with ExitStack() as es:
        S = Sched(nc, es)

        def sb(name, shape, dtype):
            return es.enter_context(nc.sbuf_tensor(name, shape, dtype))

        xT = sb("xTs", [128, 8, TOK], F32)
        oloc = sb("oloc", [128, 8, TOK], BF16)
        qhat = sb("qhat", [128, 4, TOK], BF16)
        B_x = [Buf("x%d" % g) for g in range(NG)]
        B_oloc = [Buf("oloc%d" % g) for g in range(NG)]
        B_qhat = [Buf("qhat%d" % g) for g in range(NG)]
        NSLOT = 6
        wsl = [sb("wslot%d" % i, [128, 2048], BF16) for i in range(NSLOT)]
        B_w = [Buf("w%d" % i) for i in range(NSLOT)]
        ident = sb("ident", [128, 128], BF16)
        ones_dm = sb("ones_dm", [128, 128], BF16)
        ones_dv = sb("ones_dv", [128, 128], BF16)
        bd64 = sb("bd64", [128, 128], BF16)
        ones64 = sb("ones64", [128, 64], BF16)
        ones128 = sb("ones128", [128, 128], BF16)
        maskc = sb("maskc", [128, 512], BF16)
        maskp = sb("maskp", [128, 512], BF16)
        maskp0 = sb("maskp0", [128, 512], BF16)
        onesf = sb("onesf", [128, 1], F32)
        zerof = sb("zerof", [128, 1], F32)
        epsf = sb("epsf", [128, 1], F32)
        flags = sb("flags_s", [128, 8], F32)
        spt = sb("spt", [128, DEPTH, NSP], F32)
        wup = sb("wup", [16, DEPTH, 512], BF16)
        nbgk = sb("nbgk", [128, 4], F32)
        qg8 = sb("qg8", [128, 1], F32)
        es8 = sb("es8", [128, 8], F32)
        esink = sb("esink", [64, 2, 512], F32)
        Sinit_bf = sb("Sinit_bf", [128, 1024], BF16)
        kprev = sb("kprev", [128, 2, 128], BF16)
        vprev = sb("vprev", [128, 128], BF16)
        Sst = sb("Sst", [128, 1024], F32)
        Cend = sb("Cend", [128, 4], F32)
        sc = sb("sc", [128, 64], F32)
        ahalo = sb("ahalo", [128, NJ, 2], F32)
        h2halo = sb("h2halo", [128, 8, 2], BF16)
        B_const = Buf("const")
        B_lay = Buf("layerconst")
        B_Sinit = Buf("Sinit")
        B_kvprev = Buf("kvprev")
        B_S = Buf("S")
        B_Cend = Buf("Cend")
        B_ahalo = Buf("ahalo")
        B_h2halo = Buf("h2halo")
        ARW = 13312
        arena_t = sb("arena", [128, ARW], F32)
        AR = Arena(arena_t, ARW)
        pst = [es.enter_context(nc.psum_tensor("ps%d" % i, [128, 512], F32)) for i in range(8)]
        B_ps = [Buf("ps%d" % i, excl=True) for i in range(8)]
        psrr = [0]

        def psum():
            i = psrr[0] % 8
            psrr[0] += 1
            return pst[i], B_ps[i]

        slot_rr = [0]

        def wload(src_ap, view_fn, parts=128):
            i = slot_rr[0] % NSLOT
            slot_rr[0] += 1
            dst = view_fn(wsl[i])
            S.dma("pool", lambda e, d=dst, s=src_ap: e.dma_start(out=d, in_=s), "w%d" % i, writes=[B_w[i]])
            return wsl[i], B_w[i]

        def wload_multi(pairs):
            i = slot_rr[0] % NSLOT
            slot_rr[0] += 1
            for (src_ap, view_fn) in pairs:
                dst = view_fn(wsl[i])
                S.dma("pool", lambda e, d=dst, s=src_ap: e.dma_start(out=d, in_=s), "w%d" % i, writes=[B_w[i]])
            return wsl[i], B_w[i]

        def v3(t, k, n):
            return t[:, 0:k * n].rearrange("p (k n) -> p k n", k=k)

        def win_cols(l, c0, n):
            return win_d[l].rearrange("(k p) n -> p k n", p=128)[:, :, c0:c0 + n]

        iot_i = AR.get([512], F32).bitcast(I32)
        iot_f = AR.get([512], F32)
        S.op("pool", lambda e: e.iota(iot_i[:], pattern=[[0, 4], [1, 128]], base=0, channel_multiplier=-1),
             writes=[B_const])
        S.op("dve", lambda e: e.tensor_copy(out=iot_f[:], in_=iot_i[:]), reads=[B_const], writes=[B_const])
        S.op("dve", lambda e: e.tensor_single_scalar(out=maskc[:], in_=iot_f[:], scalar=0.0, op=ALU.is_ge),
             writes=[B_const])
        S.op("dve", lambda e: e.tensor_single_scalar(out=maskp[:], in_=iot_f[:], scalar=0.0, op=ALU.is_lt),
             writes=[B_const])
        S.op("dve", lambda e: e.tensor_single_scalar(out=ident[:], in_=iot_f[:, 0:128], scalar=0.0, op=ALU.is_equal),
             writes=[B_const])
        S.op("dve", lambda e: e.memset(ones_dm[:], 1.0 / 1024.0), writes=[B_const])
        S.op("dve", lambda e: e.memset(ones_dv[:], 1.0 / 256.0), writes=[B_const])
        S.op("dve", lambda e: e.memset(bd64[:], 0.0), writes=[B_const])
        S.op("dve", lambda e: e.memset(bd64[0:64, 0:64], 1.0 / 64.0), writes=[B_const])
        S.op("dve", lambda e: e.memset(bd64[64:128, 64:128], 1.0 / 64.0), writes=[B_const])
        S.op("dve", lambda e: e.memset(ones64[:], 1.0), writes=[B_const])
        S.op("dve", lambda e: e.memset(ones128[:], 1.0), writes=[B_const])
        S.op("dve", lambda e: e.memset(onesf[:], 1.0), writes=[B_const])
        S.op("dve", lambda e: e.memset(zerof[:], 0.0), writes=[B_const])
        S.op("dve", lambda e: e.memset(epsf[:], EPS), writes=[B_const])

        def rsqrt_eps(out_ap, in_ap, reads, B_out):
            S.op("act", lambda e: e.activation(out=out_ap, in_=in_ap, func=AF.Ln, bias=epsf[0:out_ap.shape[0], 0:1],
                                               scale=1.0), reads=list(reads) + [B_const], writes=[B_out])
            S.op("act", lambda e: e.activation(out=out_ap, in_=out_ap, func=AF.Exp, scale=-0.5), reads=[B_out],
                 writes=[B_out])
        S.dma("sp", lambda e: e.dma_start(out=flags[:], in_=flags_d), "misc", writes=[B_const])
        S.dma("sp", lambda e: e.dma_start(out=spt[:], in_=sp_d.rearrange("l p n -> p l n")), "misc", writes=[B_const])
        S.dma("pool", lambda e: e.dma_start(out=wup[:], in_=wup_d.rearrange("l r n -> r l n")), "misc2",
              writes=[B_const])
        for g in range(NG):
            S.dma("sp", lambda e, g=g: e.dma_start(out=xT[:, :, g * G:(g + 1) * G], in_=x_d[:, :, g * G:(g + 1) * G]),
                  "xin%d" % g, writes=[B_x[g]])
        S.op("dve", lambda e: e.tensor_scalar(out=maskp0[:], in0=maskp[:], scalar1=flags[:, 7:8], scalar2=None,
                                              op0=ALU.mult), reads=[B_const], writes=[B_const])

        def rms_stats(l, g, sqr, B_sqr, rstd, B_rstd):
            ps, Bp = psum()
            for k in range(8):
                r = k % 2
                S.op("act", lambda e, k=k, r=r: e.activation(out=sqr[:, r, :], in_=xT[:, k, g * G:(g + 1) * G],
                                                             func=AF.Square),
                     reads=[B_x[g]], writes=[B_sqr[r]])
                S.op("pe", lambda e, k=k, r=r: e.matmul(ps[:], lhsT=ones_dm[:], rhs=sqr[:, r, :], start=(k == 0),
                                                        stop=(k == 7)),
                     reads=[B_sqr[r], B_const], writes=[Bp])
            rsqrt_eps(rstd[:], ps[:], [Bp], B_rstd)

        def rms_scale(l, g, gcol, hg, B_hg, rstd, B_rstd):
            for k in range(8):
                S.op("dve", lambda e, k=k: e.scalar_tensor_tensor(out=hg[:, k, :], in0=xT[:, k, g * G:(g + 1) * G],
                                                                  scalar=spt[:, l, gcol + k:gcol + k + 1], in1=rstd[:],
                                                                  op0=ALU.mult, op1=ALU.mult),
                     reads=[B_x[g], B_rstd, B_const], writes=[_bk(B_hg, k)])

        def rmsnorm_group(l, g, gcol, hg, B_hg, sqr, B_sqr, rstd, B_rstd):
            rms_stats(l, g, sqr, B_sqr, rstd, B_rstd)
            rms_scale(l, g, gcol, hg, B_hg, rstd, B_rstd)

        def proj_chunk(wt, Bw, kview_fn, hg, B_hg, nk=8):
            ps, Bp = psum()
            for k in range(nk):
                S.op("pe", lambda e, k=k: e.matmul(ps[:], lhsT=kview_fn(wt, k), rhs=hg[:, k, :], start=(k == 0),
                                                   stop=(k == nk - 1)),
                     reads=[Bw, _bk(B_hg, k)], writes=[Bp])
            return ps, Bp

        def qknorm(ps, Bp, sq, B_sq, rs, B_rs, out_ap, B_out, gain_ap, ncols):
            S.op("act", lambda e: e.activation(out=sq[:, 0:ncols], in_=ps[:, 0:ncols], func=AF.Square), reads=[Bp],
                 writes=[B_sq])
            ps2, Bp2 = psum()
            S.op("pe", lambda e: e.matmul(ps2[:, 0:ncols], lhsT=bd64[:], rhs=sq[:, 0:ncols], start=True, stop=True),
                 reads=[B_sq, B_const], writes=[Bp2])
            rsqrt_eps(rs[:, 0:ncols], ps2[:, 0:ncols], [Bp2], B_rs)
            S.op("dve", lambda e: e.scalar_tensor_tensor(out=out_ap, in0=ps[:, 0:ncols], scalar=gain_ap,
                                                         in1=rs[:, 0:ncols], op0=ALU.mult, op1=ALU.mult),
                 reads=[Bp, B_rs, B_lay, B_const], writes=[B_out])

        def kdup_load(l):
            cols = []
            for hk in range(2):
                for dup in range(2):
                    o = (hk * 2 + dup) * 64
                    cols.append((win_cols(l, C_KA + hk * 64, 64),
                                 lambda t, o=o: v3(t, 8, 256)[:, :, o:o + 64]))
            return wload_multi(cols)

        try:
            out_done = [False]
            for l in range(DEPTH if stop != "init" else 0):
                S.op("dve", lambda e: e.tensor_scalar(out=nbgk[:], in0=spt[:, l, 18:22], scalar1=-1.0, scalar2=None,
                                                      op0=ALU.mult), reads=[B_const], writes=[B_lay])
                S.op("dve", lambda e: e.tensor_scalar(out=qg8[:], in0=spt[:, l, 16:17], scalar1=0.125, scalar2=None,
                                                      op0=ALU.mult), reads=[B_const], writes=[B_lay])
                S.op("act", lambda e: e.activation(out=es8[:], in_=spt[:, l, 112:120], func=AF.Exp), reads=[B_const],
                     writes=[B_lay])
                for hk in range(2):
                    for hh in range(4):
                        S.op("dve", lambda e, hk=hk, hh=hh: e.tensor_copy(
                            out=esink[:, hk, hh * 128:(hh + 1) * 128],
                            in_=es8[0:64, hk * 4 + HORD[hh]:hk * 4 + HORD[hh] + 1].to_broadcast([64, 128])),
                             reads=[B_lay], writes=[B_lay])
                S.op("dve", lambda e: e.memset(Sst[:], 0.0), writes=[B_S])
                S.op("dve", lambda e: e.memset(Cend[:], 0.0), writes=[B_Cend])

                S.barrier()
                AR.reset()
                hg = AR.get([8, G], BF16); B_hg = [Buf("hgA%d" % k_) for k_ in range(8)]
                sqr = AR.get([2, G], BF16); B_sqr = [Buf("sq0"), Buf("sq1")]
                rstd = AR.get([G], F32); B_rstd = Buf("rstd")
                qt = AR.get([4, G], BF16); B_qt = Buf("qt")
                kt = AR.get([4, G], BF16); B_kt = Buf("kt")
                vtok = AR.get([4, 1024], BF16); B_vtok = Buf("vtok")
                lsp = AR.get([G], F32); B_lsp = Buf("lsp")
                Cc = AR.get([G], F32); B_Cc = Buf("Cc")
                E1 = AR.get([G], F32); B_E1 = Buf("E1")
                E2 = AR.get([G], F32); B_E2 = Buf("E2")
                gklr = AR.get([G], BF16); B_gklr = Buf("gklr")
                ATs = AR.get([2, G], BF16); B_AT = [Buf("AT0"), Buf("AT1")]
                ktok = AR.get([2, G], BF16); B_ktok = [Buf("ktok0"), Buf("ktok1")]
                U = AR.get([1024], F32); B_U = Buf("U")
                Ubf = AR.get([1024], BF16); B_Ubf = Buf("Ubf")
                dS = AR.get([4], F32); B_dS = Buf("dS")
                msg = AR.get([MSGW - 1024], F32); B_msg = Buf("msg")
                sqk = AR.get([G], BF16); B_sqk = Buf("sqk")
                rsk = AR.get([G], F32); B_rsk = Buf("rsk")

                for g in range(NG):
                    if g == 0:
                        rms_stats(l, g, sqr, B_sqr, rstd, B_rstd)
                        rms_scale(l, g, 0, hg, B_hg, rstd, B_rstd)
                    wt, Bw = wload(win_cols(l, C_GK, 128), lambda t: v3(t, 8, 128))
                    ps, Bp = psum()
                    for k in range(8):
                        S.op("pe", lambda e, k=k, wt=wt: e.matmul(ps[:, :], lhsT=v3(wt, 8, 128)[:, k, :], rhs=hg[:, k, :],
                                                                  start=(k == 0), stop=(k == 7)),
                             reads=[Bw, _bk(B_hg, k)], writes=[Bp])
                    S.op("act", lambda e, ps=ps: e.activation(out=gklr[0:16, :], in_=ps[0:16, :], func=AF.Copy),
                         reads=[Bp], writes=[B_gklr])
                    wqs = [wload(win_cols(l, C_QB + hp_ * 256, 256), lambda t: v3(t, 8, 256)) for hp_ in range(2)]
                    wks = [wload(win_cols(l, C_KB + hp_ * 256, 256), lambda t: v3(t, 8, 256)) for hp_ in range(2)]
                    def v_unit(q4, half, wv, Bwv):
                        ps, Bp = psum()
                        for bb in range(2):
                            blk = half * 2 + bb
                            for k in range(8):
                                S.op("pe", lambda e, k=k, blk=blk, bb=bb, ps=ps, wv=wv: e.matmul(
                                    ps[:, bb * 256:(bb + 1) * 256], lhsT=hg[:, k, blk * 128:(blk + 1) * 128],
                                    rhs=v3(wv, 8, 256)[:, k, :], start=(k == 0), stop=(k == 7)),
                                     reads=[Bwv, _bk(B_hg, k)], writes=[Bp])
                        S.op("act", lambda e, half=half, q4=q4, ps=ps: e.activation(
                            out=vtok[:, half * 2:half * 2 + 2, q4 * 256:(q4 + 1) * 256],
                            in_=ps[:].rearrange("p (b n) -> p b n", b=2), func=AF.Copy),
                             reads=[Bp], writes=[B_vtok])
                    for h in range(4):
                        psg, Bpg = psum()
                        S.op("pe", lambda e, h=h, psg=psg: e.matmul(psg[:], lhsT=wup[0:16, l, h * 128:(h + 1) * 128],
                                                                    rhs=gklr[0:16, :], start=True, stop=True),
                             reads=[B_gklr, B_const], writes=[Bpg])
                        S.op("act", lambda e, h=h, psg=psg: e.activation(out=lsp[:], in_=psg[:], func=AF.Exp,
                                                                         bias=nbgk[:, h:h + 1], scale=-1.0),
                             reads=[Bpg, B_lay], writes=[B_lsp])
                        S.op("act", lambda e: e.activation(out=lsp[:], in_=lsp[:], func=AF.Ln, bias=onesf[:, 0:1], scale=1.0),
                             reads=[B_lsp, B_const], writes=[B_lsp])
                        S.op("dve", lambda e, h=h: e.tensor_copy(out=sc[:, 0:1], in_=Cend[:, h:h + 1]), reads=[B_Cend],
                             writes=[B_dS])
                        S.op("dve", lambda e, h=h: e.tensor_tensor_scan(out=Cc[:], data0=onesf[:, 0:1].to_broadcast([128, G]),
                                                                        data1=lsp[:], initial=sc[:, 0:1], op0=ALU.mult,
                                                                        op1=ALU.add),
                             reads=[B_lsp, B_dS, B_const], writes=[B_Cc])
                        S.op("dve", lambda e, h=h: e.tensor_copy(out=Cend[:, h:h + 1], in_=Cc[:, G - 1:G]), reads=[B_Cc],
                             writes=[B_Cend])
                        S.op("dve", lambda e: e.tensor_scalar(out=sc[:, 1:2], in0=Cc[:, 255:256], scalar1=1.0 / 16.0,
                                                              scalar2=None, op0=ALU.mult), reads=[B_Cc], writes=[B_dS])
                        S.op("dve", lambda e: e.tensor_scalar(out=sc[:, 2:3], in0=Cc[:, 255:256], scalar1=-1.0 / 16.0,
                                                              scalar2=None, op0=ALU.mult), reads=[B_Cc], writes=[B_dS])
                        S.op("act", lambda e: e.activation(out=E1[:], in_=Cc[:], func=AF.Exp, bias=sc[:, 1:2],
                                                           scale=-1.0 / 16.0), reads=[B_Cc, B_dS], writes=[B_E1])
                        S.op("act", lambda e: e.activation(out=E2[:], in_=Cc[:], func=AF.Exp, bias=sc[:, 2:3],
                                                           scale=1.0 / 16.0), reads=[B_Cc, B_dS], writes=[B_E2])
                        S.op("act", lambda e: e.activation(out=sc[:, 3:4], in_=sc[:, 0:1], func=AF.Exp, bias=sc[:, 2:3],
                                                           scale=1.0 / 16.0), reads=[B_dS], writes=[B_dS])
                        S.op("act", lambda e: e.activation(out=sc[:, 4:5], in_=Cc[:, 255:256], func=AF.Exp,
                                                           scale=-1.0 / 16.0), reads=[B_Cc, B_dS], writes=[B_dS])
                        S.op("dve", lambda e, h=h: e.tensor_copy(out=dS[:, h:h + 1], in_=E1[:, G - 1:G]), reads=[B_E1],
                             writes=[B_dS])
                        wq, Bwq = wqs[h // 2]
                        wk, Bwk = wks[h // 2]
                        psq, Bpq = proj_chunk(wq, Bwq, lambda t, k, h=h: v3(t, 8, 256)[:, k, (h % 2) * 128:(h % 2 + 1) * 128], hg,
                                              B_hg)
                        S.op("dve", lambda e, h=h, psq=psq: e.scalar_tensor_tensor(out=qt[:, h, :], in0=psq[:],
                                                                                   scalar=128.0 ** -0.5, in1=E1[:],
                                                                                   op0=ALU.mult, op1=ALU.mult),
                             reads=[Bpq, B_E1], writes=[B_qt])
                        psk, Bpk = proj_chunk(wk, Bwk, lambda t, k, h=h: v3(t, 8, 256)[:, k, (h % 2) * 128:(h % 2 + 1) * 128], hg,
                                              B_hg)
                        S.op("dve", lambda e, h=h, psk=psk: e.tensor_tensor(out=kt[:, h, :], in0=psk[:], in1=E2[:],
                                                                            op=ALU.mult),
                             reads=[Bpk, B_E2], writes=[B_kt])
                        S.op("dve", lambda e, h=h: e.tensor_scalar(out=qhat[:, h, g * G:(g + 1) * G], in0=qt[:, h, :],
                                                                    scalar1=sc[:, 4:5], scalar2=None, op0=ALU.mult),
                             reads=[B_qt, B_dS], writes=[B_qhat[g]])
                        S.op("dve", lambda e, h=h: e.tensor_scalar(out=U[:, h * 256:(h + 1) * 256],
                                                                   in0=Sst[:, h * 256:(h + 1) * 256], scalar1=sc[:, 3:4],
                                                                   scalar2=None, op0=ALU.mult),
                             reads=[B_S, B_dS], writes=[B_U])
                    S.op("act", lambda e: e.activation(out=Ubf[:], in_=U[:], func=AF.Copy), reads=[B_U], writes=[B_Ubf])
                    for q4 in range(4):
                        wv, Bwv = wload(win_cols(l, C_VB + q4 * 256, 256), lambda t: v3(t, 8, 256))
                        for half in range(2):
                            v_unit(q4, half, wv, Bwv)
                    def blk_stage1(j, par):
                        tsl = slice(j * 128, (j + 1) * 128)
                        psA, BpA = psum()
                        for h in range(4):
                            S.op("pe", lambda e, h=h, psA=psA, tsl=tsl: e.matmul(psA[:, h * 128:(h + 1) * 128],
                                                                                 lhsT=kt[:, h, tsl], rhs=qt[:, h, tsl],
                                                                                 start=True, stop=True),
                                 reads=[B_kt, B_qt], writes=[BpA])
                        S.op("dve", lambda e, psA=psA, par=par: e.tensor_tensor(out=ATs[:, par, :], in0=psA[:], in1=maskc[:],
                                                                                op=ALU.mult),
                             reads=[BpA, B_const], writes=[B_AT[par]])
                        psT, BpT = psum()
                        psTb = psT[:, 0:256].bitcast(BF16)
                        for h in range(4):
                            S.op("pe", lambda e, h=h, psTb=psTb, tsl=tsl: e.transpose(psTb[:, h * 128:(h + 1) * 128],
                                                                                      kt[:, h, tsl], ident[:]),
                                 reads=[B_kt, B_const], writes=[BpT])
                        S.op("act", lambda e, psTb=psTb, par=par: e.activation(out=ktok[:, par, :], in_=psTb, func=AF.Copy),
                             reads=[BpT], writes=[B_ktok[par]])

                    def blk_stage2(j, par):
                        tsl = slice(j * 128, (j + 1) * 128)
                        kvs = []
                        for hp in range(2):
                            pskv, Bpkv = psum()
                            for hh in range(2):
                                h = hp * 2 + hh
                                S.op("pe", lambda e, pskv=pskv, hh=hh, h=h, j=j, par=par: e.matmul(
                                    pskv[:, hh * 256:(hh + 1) * 256], lhsT=ktok[:, par, h * 128:(h + 1) * 128],
                                    rhs=vtok[:, j, h * 256:(h + 1) * 256], start=True, stop=True),
                                     reads=[B_ktok[par], B_vtok], writes=[Bpkv])
                            kvs.append((pskv, Bpkv))
                        for hp in range(2):
                            pso, Bpo = psum()
                            for hh in range(2):
                                h = hp * 2 + hh
                                for c in range(2):
                                    oc = (hh * 2 + c) * 128
                                    ec = h * 256 + c * 128
                                    S.op("pe", lambda e, pso=pso, oc=oc, ec=ec, h=h, j=j, par=par: e.matmul(
                                        pso[:, oc:oc + 128], lhsT=vtok[:, j, ec:ec + 128],
                                        rhs=ATs[:, par, h * 128:(h + 1) * 128], start=True, stop=False),
                                         reads=[B_vtok, B_AT[par]], writes=[Bpo])
                                    S.op("pe", lambda e, pso=pso, oc=oc, ec=ec, h=h, tsl=tsl: e.matmul(
                                        pso[:, oc:oc + 128], lhsT=Ubf[:, ec:ec + 128], rhs=qt[:, h, tsl],
                                        start=False, stop=True), reads=[B_Ubf, B_qt], writes=[Bpo])
                            S.op("act", lambda e, pso=pso, hp=hp, j=j: e.activation(
                                out=oloc[:, hp * 4:hp * 4 + 4, g * G + j * 128:g * G + (j + 1) * 128],
                                in_=pso[:].rearrange("p (c t) -> p c t", c=4), func=AF.Copy),
                                 reads=[Bpo], writes=[B_oloc[g]])
                        for hp in range(2):
                            pskv, Bpkv = kvs[hp]
                            S.op("dve", lambda e, pskv=pskv, hp=hp: e.tensor_tensor(
                                out=U[:, hp * 512:(hp + 1) * 512], in0=pskv[:], in1=U[:, hp * 512:(hp + 1) * 512],
                                op=ALU.add), reads=[Bpkv, B_U], writes=[B_U])
                        S.op("act", lambda e: e.activation(out=Ubf[:], in_=U[:], func=AF.Copy), reads=[B_U], writes=[B_Ubf])

                    blk_stage1(0, 0)
                    for j in range(4):
                        if j + 1 < 4:
                            blk_stage1(j + 1, (j + 1) % 2)
                        blk_stage2(j, j % 2)
                        if j == 0 and g + 1 < NG:
                            rms_stats(l, g + 1, sqr, B_sqr, rstd, B_rstd)
                        if j == 1 and g + 1 < NG:
                            rms_scale(l, g + 1, 0, hg, B_hg, rstd, B_rstd)
                    for h in range(4):
                        S.op("dve", lambda e, h=h: e.tensor_scalar(out=Sst[:, h * 256:(h + 1) * 256],
                                                                   in0=U[:, h * 256:(h + 1) * 256], scalar1=dS[:, h:h + 1],
                                                                   scalar2=None, op0=ALU.mult),
                             reads=[B_U, B_dS], writes=[B_S])
                    if g == NG - 1:
                        wkd, Bwkd = kdup_load(l)
                        for hk in range(2):
                            ps, Bp = psum()
                            for k in range(8):
                                S.op("pe", lambda e, k=k, hk=hk, ps=ps, wkd=wkd: e.matmul(
                                    ps[:, 0:128], lhsT=v3(wkd, 8, 256)[:, k, hk * 128:(hk + 1) * 128],
                                    rhs=hg[:, k, 384:512], start=(k == 0), stop=(k == 7)), reads=[Bwkd, _bk(B_hg, k)], writes=[Bp])
                            qknorm(ps, Bp, sqk, B_sqk, rsk, B_rsk, msg[:, 4 + hk * 128:4 + (hk + 1) * 128], B_msg,
                                   spt[:, l, 17:18], 128)
                        wv, Bwv = wload(win_cols(l, C_VA, 128), lambda t: v3(t, 8, 128))
                        ps, Bp = psum()
                        for k in range(8):
                            S.op("pe", lambda e, k=k, ps=ps, wv=wv: e.matmul(ps[:, 0:128], lhsT=hg[:, k, 384:512],
                                                                             rhs=v3(wv, 8, 128)[:, k, :], start=(k == 0),
                                                                             stop=(k == 7)), reads=[Bwv, _bk(B_hg, k)], writes=[Bp])
                        S.op("act", lambda e, ps=ps: e.activation(out=msg[:, 260:388], in_=ps[:, 0:128], func=AF.Copy),
                             reads=[Bp], writes=[B_msg])
                if stop == ("A", l):
                    break
                S.op("act", lambda e: e.activation(out=msg[:, 0:4], in_=Cend[:], func=AF.Exp, scale=-1.0 / 16.0),
                     reads=[B_Cend], writes=[B_msg])
                B_cin = Buf("cin")
                B_cout = Buf("cout")
                S.dma("sp", lambda e: e.dma_start(out=cin1[l][:, 0:1024], in_=Sst[:]), "cin", reads=[B_S], writes=[B_cin])
                S.dma("sp", lambda e: e.dma_start(out=cin1[l][:, 1024:MSGW], in_=msg[:]), "cin", reads=[B_msg],
                      writes=[B_cin])
                S.cc(lambda e: e.collective_compute("AllGather", ALU.bypass, replica_groups=RG, ins=[cin1[l].opt()],
                                                    outs=[cout1[l].opt()]), "cc", reads=[B_cin], writes=[B_cout])
                S.barrier(skip=("cc",))
                AR.reset()
                hg = AR.get([8, G], BF16); B_hg = [Buf("hgB%d" % k_) for k_ in range(8)]
                sqr = AR.get([2, G], BF16); B_sqr = [Buf("sq0B"), Buf("sq1B")]
                rstd = AR.get([G], F32); B_rstd = Buf("rstdB")
                qn_off = AR.off
                qn = AR.get([4, G], BF16); B_qn = Buf("qn")
                kdup = AR.get([2, 128 + G], BF16); B_kdup = Buf("kdup")
                vaug = AR.get([5, 2, 128], BF16); B_vaug = Buf("vaug")
                pT_off = AR.off
                pT = AR.get([2, 2, G], BF16); B_pT = [[Buf("pT00"), Buf("pT01")], [Buf("pT10"), Buf("pT11")]]
                ya_off = AR.off
                ya = AR.get([8, G], BF16, parts=64); B_ya = Buf("ya")
                ya128 = AR.view(ya_off, [8, G], BF16)
                ofp = AR.get([2, G], F32); B_ofp = Buf("ofp")
                merged_off = AR.off
                merged = AR.get([8, G], BF16); B_merged = Buf("merged")
                sr = AR.view(merged_off, [2, G], BF16); B_sr = B_merged
                sa = AR.get([G], F32); B_sa = Buf("sa")
                sbb = AR.get([G], F32); B_sbb = Buf("sbb")
                sqk = sqr[:, 0, :]; B_sqk = B_sqr[0]
                rsk = rstd; B_rsk = B_rstd
                dn = AR.get([G], F32, parts=64); B_dn = Buf("dn")
                ofp2 = AR.view(qn_off, [2, G], F32)
                sr2 = AR.view(pT_off, [2, G], BF16)
                sqr2 = AR.view(pT_off + 512, [2, G], BF16)
                GF = [dict(ofp=ofp, Bofp=[B_ofp], sr=sr, Bsr=[B_sr], sq=sqr, Bsq=[[B_sqr[0]], [B_sqr[1]]], rs=rstd,
                           Brs=[B_rstd]),
                      dict(ofp=ofp2, Bofp=[B_qn], sr=sr2, Bsr=[B_pT[0][0], B_pT[0][1]],
                           sq=sqr2, Bsq=[[B_pT[1][0]], [B_pT[1][1]]], rs=sa, Brs=[B_sa])]

                S.op("dve", lambda e: e.memset(vaug[:], 0.0), writes=[B_vaug])
                XOFF = ARW - (4 * MSGW + 1024 + 384)
                assert XOFF >= pT_off
                gath = AR.view(XOFF, [4, MSGW], F32); B_gath = Buf("gath")
                acc = AR.view(XOFF + 4 * MSGW, [1024], F32); B_acc = Buf("acc")
                kvacc = AR.view(XOFF + 4 * MSGW + 1024, [384], F32); B_kvacc = Buf("kvacc")
                fct = sc[:, 8:24]; B_fct = Buf("fct")
                S.dma("sp", lambda e: e.dma_start(out=gath[:], in_=cout1[l].rearrange("(j p) n -> p j n", p=128)), "gath",
                      reads=[B_cout], writes=[B_gath])

                def exchange_compute():
                    S.op("dve", lambda e: e.memset(acc[:], 0.0), writes=[B_acc])
                    for jr in range(3):
                        mj = flags[:, 4 + jr:5 + jr]
                        S.op("dve", lambda e, jr=jr, mj=mj: e.tensor_scalar(out=fct[:, 0:4], in0=gath[:, jr, 1024:1028],
                                                                            scalar1=-1.0, scalar2=mj, op0=ALU.add,
                                                                            op1=ALU.mult), reads=[B_gath, B_const],
                             writes=[B_fct])
                        S.op("dve", lambda e: e.tensor_scalar(out=fct[:, 0:4], in0=fct[:, 0:4], scalar1=1.0, scalar2=None,
                                                              op0=ALU.add), reads=[B_fct], writes=[B_fct])
                        for h in range(4):
                            hs = slice(h * 256, (h + 1) * 256)
                            S.op("dve", lambda e, h=h, hs=hs: e.tensor_scalar(out=acc[:, hs], in0=acc[:, hs],
                                                                              scalar1=fct[:, h:h + 1], scalar2=None,
                                                                              op0=ALU.mult), reads=[B_fct, B_acc],
                                 writes=[B_acc])
                            S.op("dve", lambda e, jr=jr, hs=hs, mj=mj: e.scalar_tensor_tensor(out=acc[:, hs],
                                                                                              in0=gath[:, jr, hs], scalar=mj,
                                                                                              in1=acc[:, hs], op0=ALU.mult,
                                                                                              op1=ALU.add),
                                 reads=[B_gath, B_acc, B_const], writes=[B_acc])
                    S.op("act", lambda e: e.activation(out=Sinit_bf[:], in_=acc[:], func=AF.Copy), reads=[B_acc],
                         writes=[B_Sinit])
                    S.op("dve", lambda e: e.tensor_scalar(out=kvacc[:], in0=gath[:, 0, 1028:1412], scalar1=flags[:, 0:1],
                                                          scalar2=None, op0=ALU.mult), reads=[B_gath, B_const],
                         writes=[B_kvacc])
                    for jr in range(1, 4):
                        S.op("dve", lambda e, jr=jr: e.scalar_tensor_tensor(out=kvacc[:], in0=gath[:, jr, 1028:1412],
                                                                            scalar=flags[:, jr:jr + 1], in1=kvacc[:],
                                                                            op0=ALU.mult, op1=ALU.add),
                             reads=[B_gath, B_kvacc, B_const], writes=[B_kvacc])
                    S.op("act", lambda e: e.activation(out=kprev[:], in_=kvacc[:, 0:256].rearrange("p (a b) -> p a b", a=2),
                                                       func=AF.Copy), reads=[B_kvacc], writes=[B_kvprev])
                    S.op("act", lambda e: e.activation(out=vprev[:], in_=kvacc[:, 256:384], func=AF.Copy), reads=[B_kvacc],
                         writes=[B_kvprev])
                    S.op("dve", lambda e: e.memset(sc[:, 30:31], 0.0), reads=[B_Sinit, B_kvprev],
                         writes=[B_gath, B_acc, B_kvacc, B_pT[0][0], B_pT[0][1], B_pT[1][0], B_pT[1][1], B_ya, B_ofp, B_merged,
                                 B_sa, B_sbb, B_dn])

                if stop == ("X", l):
                    break
                def outproj_chunk(gp, j, st_):
                    gsp = slice(gp * G, (gp + 1) * G)
                    if j % 2 == 0:
                        jj = j // 2
                        st_["w"] = wload(wout_d[l].rearrange("(k p) n -> p k n", p=128)[:, :, jj * 256:(jj + 1) * 256],
                                         lambda t: v3(t, 8, 256))
                    wo, Bwo = st_["w"]
                    c = j % 2
                    ps, Bp = proj_chunk(wo, Bwo, lambda t, k, c=c: v3(t, 8, 256)[:, k, c * 128:(c + 1) * 128], merged,
                                        B_merged)
                    S.op("dve", lambda e, ps=ps, j=j, gsp=gsp: e.tensor_tensor(out=xT[:, j, gsp], in0=ps[:],
                                                                               in1=xT[:, j, gsp], op=ALU.add),
                         reads=[Bp, B_x[gp]], writes=[B_x[gp]])

                pend_out = None
                for g in range(NG):
                    gs = slice(g * G, (g + 1) * G)
                    if g == 0:
                        rms_stats(l, g, sqr, B_sqr, rstd, B_rstd)
                    rms_scale(l, g, 0, hg, B_hg, rstd, B_rstd)
                    wqs = [wload(win_cols(l, C_QA + hp_ * 256, 256), lambda t: v3(t, 8, 256)) for hp_ in range(2)]
                    wkd, Bwkd = kdup_load(l)
                    if g > 0:
                        S.op("dve", lambda e: e.tensor_copy(out=kdup[:, :, 0:128], in_=kdup[:, :, G:G + 128]),
                             reads=[B_kdup], writes=[B_kdup])
                        S.op("dve", lambda e: e.tensor_copy(out=vaug[:, 0, :, :], in_=vaug[:, 4, :, :]), reads=[B_vaug],
                             writes=[B_vaug])
                    units = []
                    for c in range(4):
                        units.append((wqs[c // 2], (c % 2) * 128, qn[:, c, :], B_qn, qg8[:, 0:1]))
                    for hk in range(2):
                        units.append(((wkd, Bwkd), hk * 128, kdup[:, hk, 128:128 + G], B_kdup, spt[:, l, 17:18]))
                    prev_u = None
                    for u_ in units + [None]:
                        cur_u = None
                        if u_ is not None:
                            (wt_, Bwt_), co_, out_, Bout_, gain_ = u_
                            ps_, Bp_ = proj_chunk(wt_, Bwt_, lambda t, k, co_=co_: v3(t, 8, 256)[:, k, co_:co_ + 128], hg, B_hg)
                            cur_u = (ps_, Bp_, out_, Bout_, gain_)
                        if prev_u is not None:
                            qknorm(prev_u[0], prev_u[1], sqk, B_sqk, rsk, B_rsk, prev_u[2], prev_u[3], prev_u[4], G)
                        prev_u = cur_u
                    wv, Bwv = wload(win_cols(l, C_VA, 128), lambda t: v3(t, 8, 128))
                    ps, Bp = psum()
                    for blk in range(4):
                        for k in range(8):
                            S.op("pe", lambda e, k=k, blk=blk, ps=ps, wv=wv: e.matmul(
                                ps[:, blk * 128:(blk + 1) * 128], lhsT=hg[:, k, blk * 128:(blk + 1) * 128],
                                rhs=v3(wv, 8, 128)[:, k, :], start=(k == 0), stop=(k == 7)), reads=[Bwv, _bk(B_hg, k)], writes=[Bp])
                    S.op("act", lambda e, ps=ps: e.activation(
                        out=vaug[:, 1:5, :, 0:64], in_=ps[:].rearrange("p (b h d) -> p b h d", b=4, h=2), func=AF.Copy),
                         reads=[Bp], writes=[B_vaug])
                    if g == 0:
                        exchange_compute()
                        S.op("dve", lambda e: e.tensor_copy(out=kdup[:, :, 0:128], in_=kprev[:]), reads=[B_kvprev],
                             writes=[B_kdup])
                        for hk in range(2):
                            S.op("dve", lambda e, hk=hk: e.tensor_copy(out=vaug[:, 0, hk, 0:64],
                                                                        in_=vprev[:, hk * 64:(hk + 1) * 64]),
                                 reads=[B_kvprev], writes=[B_vaug])
                    if stop == ("B1", l):
                        raise _Stop()
                    def attn_stage1(i, hk, par):
                        psE, BpE = psum()
                        psO, BpO = psum()
                        for kb in range(2):
                            koff = i * 128 + kb * 128
                            for b in range(4):
                                hh = HORD[b]
                                chunk = 2 * hk + hh // 2
                                p0 = 64 * (hh % 2)
                                pst_, Bpt_ = (psE, BpE) if b < 2 else (psO, BpO)
                                cb = kb * 256 + (b % 2) * 128
                                S.op("pe", lambda e, pst_=pst_, cb=cb, chunk=chunk, p0=p0, koff=koff, hk=hk, i=i: e.matmul(
                                    pst_[:, cb:cb + 128], lhsT=kdup[p0:p0 + 64, hk, koff:koff + 128],
                                    rhs=qn[p0:p0 + 64, chunk, i * 128:(i + 1) * 128], start=True, stop=True),
                                     reads=[B_kdup, B_qn], writes=[Bpt_])
                        S.op("act", lambda e, psE=psE, par=par: e.activation(
                            out=pT[:, par, :, 0:256], in_=psE[:].rearrange("p (k n) -> p k n", k=2), func=AF.Exp),
                             reads=[BpE], writes=[B_pT[par][0], B_pT[par][1]])
                        S.op("act", lambda e, psO=psO, par=par: e.activation(
                            out=pT[:, par, :, 256:512], in_=psO[:].rearrange("p (k n) -> p k n", k=2), func=AF.Exp),
                             reads=[BpO], writes=[B_pT[par][0], B_pT[par][1]])
                        for kb in range(2):
                            if kb == 1:
                                mk = maskc
                            else:
                                mk = maskp0 if (g == 0 and i == 0) else maskp
                            S.op("dve", lambda e, kb=kb, mk=mk, par=par: e.tensor_tensor(
                                out=pT[:, par, kb, :], in0=pT[:, par, kb, :], in1=mk[:], op=ALU.mult),
                                 reads=[B_pT[par][kb], B_const], writes=[B_pT[par][kb]])

                    dnb = [(dn, B_dn), (sbb[0:64, :], B_sbb)]

                    def attn_stage2a(i, hk, par):
                        pso, Bpo = psum()
                        psd, Bpd = psum()
                        dn_, Bdn_ = dnb[par]
                        for kb in range(2):
                            S.op("pe", lambda e, pso=pso, kb=kb, hk=hk, i=i, par=par: e.matmul(
                                pso[:, :], lhsT=vaug[:, i + kb, hk, :], rhs=pT[:, par, kb, :], start=(kb == 0),
                                stop=(kb == 1)), reads=[B_vaug, B_pT[par][kb]], writes=[Bpo])
                        for kb in range(2):
                            S.op("pe", lambda e, psd=psd, kb=kb, par=par: e.matmul(
                                psd[:, :], lhsT=ones128[:], rhs=pT[:, par, kb, :], start=(kb == 0), stop=(kb == 1)),
                                 reads=[B_const, B_pT[par][kb]], writes=[Bpd])
                        S.op("dve", lambda e, psd=psd, hk=hk, dn_=dn_: e.tensor_tensor(out=dn_, in0=psd[0:64, :],
                                                                                       in1=esink[:, hk, :], op=ALU.add),
                             reads=[Bpd, B_lay], writes=[Bdn_])
                        return pso, Bpo

                    def attn_stage2b(i, hk, par, pso, Bpo):
                        dn_, Bdn_ = dnb[par]
                        S.op("act", lambda e, dn_=dn_: e.activation(out=dn_, in_=dn_, func=AF.Ln), reads=[Bdn_],
                             writes=[Bdn_])
                        S.op("act", lambda e, dn_=dn_: e.activation(out=dn_, in_=dn_, func=AF.Exp, scale=-1.0),
                             reads=[Bdn_], writes=[Bdn_])
                        S.op("dve", lambda e, pso=pso, hk=hk, i=i, dn_=dn_: e.tensor_tensor(
                            out=ya[:, hk * 4:(hk + 1) * 4, i * 128:(i + 1) * 128],
                            in0=pso[0:64, :].rearrange("p (h q) -> p h q", h=4),
                            in1=dn_.rearrange("p (h q) -> p h q", h=4), op=ALU.mult),
                             reads=[Bpo, Bdn_], writes=[B_ya])

                    its = [(i, hk) for i in range(4) for hk in range(2)]
                    attn_stage1(its[0][0], its[0][1], 0)
                    pend = None
                    for n_, (i, hk) in enumerate(its):
                        if n_ + 1 < len(its):
                            attn_stage1(its[n_ + 1][0], its[n_ + 1][1], (n_ + 1) % 2)
                        pso_, Bpo_ = attn_stage2a(i, hk, n_ % 2)
                        if pend is not None:
                            attn_stage2b(*pend)
                        pend = (i, hk, n_ % 2, pso_, Bpo_)
                        if pend_out is not None:
                            outproj_chunk(pend_out[0], n_, pend_out[1])
                    attn_stage2b(*pend)
                    pend_out = None
                    S.dma("sp", lambda e: e.dma_start(
                        out=ya128[64:128, :, :].rearrange("p (m t) g -> p m t g", t=2)[:, :, 0, :],
                        in_=ya128[0:64, :, :].rearrange("p (m t) g -> p m t g", t=2)[:, :, 1, :]),
                          "yash", reads=[B_ya], writes=[B_ya])
                    if stop == ("B2", l):
                        raise _Stop()
                    def gla_p1(hp):
                        for h in (2 * hp, 2 * hp + 1):
                            F_ = GF[h % 2]
                            wr, Bwr = wload(win_cols(l, C_RB + h * 256, 256), lambda t: v3(t, 8, 256))
                            for c in range(2):
                                psc, Bpc = psum()
                                ec = h * 256 + c * 128
                                S.op("pe", lambda e, psc=psc, ec=ec, h=h: e.matmul(psc[:], lhsT=Sinit_bf[:, ec:ec + 128],
                                                                                   rhs=qhat[:, h, gs], start=True, stop=True),
                                     reads=[B_Sinit, B_qhat[g]], writes=[Bpc])
                                S.op("dve", lambda e, psc=psc, c=c, h=h, F_=F_: e.tensor_tensor(
                                    out=F_["ofp"][:, c, :], in0=psc[:], in1=oloc[:, h * 2 + c, gs], op=ALU.add),
                                     reads=[Bpc, B_oloc[g]], writes=F_["Bofp"])
                                S.op("act", lambda e, c=c, F_=F_: e.activation(out=F_["sq"][:, c, :], in_=F_["ofp"][:, c, :],
                                                                               func=AF.Square),
                                     reads=F_["Bofp"], writes=F_["Bsq"][c])
                            for c in range(2):
                                psr, Bpr = proj_chunk(wr, Bwr, lambda t, k, c=c: v3(t, 8, 256)[:, k, c * 128:(c + 1) * 128],
                                                      hg, B_hg)
                                S.op("act", lambda e, c=c, psr=psr, F_=F_: e.activation(out=F_["sr"][:, c, :], in_=psr[:],
                                                                                        func=AF.Silu),
                                     reads=[Bpr], writes=F_["Bsr"])

                    def gla_p2(hp):
                        for h in (2 * hp, 2 * hp + 1):
                            F_ = GF[h % 2]
                            pss, Bpss = psum()
                            for c in range(2):
                                S.op("pe", lambda e, c=c, pss=pss, F_=F_: e.matmul(pss[:], lhsT=ones_dv[:],
                                                                                   rhs=F_["sq"][:, c, :], start=(c == 0),
                                                                                   stop=(c == 1)),
                                     reads=F_["Bsq"][c] + [B_const], writes=[Bpss])
                            S.op("act", lambda e, pss=pss, F_=F_: e.activation(out=F_["rs"][:], in_=pss[:], func=AF.Ln,
                                                                               bias=epsf[:, 0:1], scale=1.0),
                                 reads=[Bpss, B_const], writes=F_["Brs"])
                            S.op("act", lambda e, F_=F_: e.activation(out=F_["rs"][:], in_=F_["rs"][:], func=AF.Exp,
                                                                      scale=-0.5), reads=F_["Brs"], writes=F_["Brs"])
                            for c in range(2):
                                S.op("dve", lambda e, c=c, F_=F_: e.scalar_tensor_tensor(
                                    out=F_["ofp"][:, c, :], in0=F_["ofp"][:, c, :], scalar=spt[:, l, 22 + c:23 + c],
                                    in1=F_["rs"][:], op0=ALU.mult, op1=ALU.mult),
                                     reads=F_["Bofp"] + F_["Brs"] + [B_const], writes=F_["Bofp"])
                                S.op("dve", lambda e, c=c, h=h, F_=F_: e.tensor_tensor(
                                    out=oloc[:, h * 2 + c, gs], in0=F_["ofp"][:, c, :], in1=F_["sr"][:, c, :], op=ALU.mult),
                                     reads=F_["Bofp"] + F_["Bsr"], writes=[B_oloc[g]])
                    def merge_load(st2):
                        c2 = slice(st2 * 256, (st2 + 1) * 256)
                        wA, BwA = wload_multi([
                            (wba_d[l][q_ * 256 + u_ * 128:q_ * 256 + u_ * 128 + 128, c2].rearrange("(r p) n -> p r n", p=64),
                             lambda t, q_=q_, u_=u_: t[u_ * 64:u_ * 64 + 64, 0:1024].rearrange("p (m n) -> p m n", m=4)[:, 2 * q_:2 * q_ + 2, :])
                            for q_ in range(2) for u_ in range(2)])
                        wGa, BwGa = wload(win_cols(l, C_GA + st2 * 256, 256), lambda t: v3(t, 8, 256))
                        wGb, BwGb = wload(win_cols(l, C_GB + st2 * 256, 256), lambda t: v3(t, 8, 256))
                        wB, BwB = wload(wbb_d[l].rearrange("(k p) n -> p k n", p=128)[:, :, c2], lambda t: v3(t, 8, 256))
                        return (wA, BwA, wGa, BwGa, wGb, BwGb, wB, BwB)

                    def merge_pre(st2, jj, W_):
                        if True:
                            (wA, BwA, wGa, BwGa, wGb, BwGb, wB, BwB) = W_
                            j = st2 * 2 + jj
                            cj = slice(jj * 128, (jj + 1) * 128)
                            psA, BpA = psum()
                            for m_ in range(4):
                                S.op("pe", lambda e, psA=psA, m_=m_, wA=wA, cj=cj: e.matmul(
                                    psA[:], lhsT=wA[:, 0:1024].rearrange("p (m n) -> p m n", m=4)[:, m_, cj],
                                    rhs=ya128[:, 2 * m_, :], start=(m_ == 0), stop=(m_ == 3)),
                                     reads=[BwA, B_ya], writes=[BpA])
                            psGa, BpGa = proj_chunk(wGa, BwGa, lambda t, k, cj=cj: v3(t, 8, 256)[:, k, cj], hg, B_hg)
                            psGb, BpGb = proj_chunk(wGb, BwGb, lambda t, k, cj=cj: v3(t, 8, 256)[:, k, cj], hg, B_hg)
                            return (psA, BpA, psGa, BpGa, psGb, BpGb)

                    def merge_post(st2, jj, W_, P_):
                        if True:
                            (wA, BwA, wGa, BwGa, wGb, BwGb, wB, BwB) = W_
                            (psA, BpA, psGa, BpGa, psGb, BpGb) = P_
                            j = st2 * 2 + jj
                            cj = slice(jj * 128, (jj + 1) * 128)
                            S.op("act", lambda e, psGa=psGa: e.activation(out=sa[:], in_=psGa[:], func=AF.Sigmoid),
                                 reads=[BpGa], writes=[B_sa])
                            S.op("act", lambda e, psGb=psGb: e.activation(out=sbb[:], in_=psGb[:], func=AF.Sigmoid),
                                 reads=[BpGb], writes=[B_sbb])
                            S.op("dve", lambda e, psA=psA: e.tensor_tensor(out=sa[:], in0=psA[:], in1=sa[:], op=ALU.mult),
                                 reads=[BpA, B_sa], writes=[B_sa])
                            psB, BpB = psum()
                            for k in range(8):
                                S.op("pe", lambda e, psB=psB, k=k, wB=wB, cj=cj: e.matmul(
                                    psB[:], lhsT=v3(wB, 8, 256)[:, k, cj], rhs=oloc[:, k, gs], start=(k == 0), stop=(k == 7)),
                                     reads=[BwB, B_oloc[g]], writes=[BpB])
                            S.op("dve", lambda e, psB=psB: e.tensor_tensor(out=sbb[:], in0=psB[:], in1=sbb[:], op=ALU.mult),
                                 reads=[BpB, B_sbb], writes=[B_sbb])
                            S.op("dve", lambda e, j=j: e.tensor_tensor(out=merged[:, j, :], in0=sa[:], in1=sbb[:], op=ALU.add),
                                 reads=[B_sa, B_sbb], writes=[B_merged])


                    gla_p1(0)
                    gla_p2(0)
                    gla_p1(1)
                    W0_ = merge_load(0)
                    P0_ = merge_pre(0, 0, W0_)
                    P1_ = merge_pre(0, 1, W0_)
                    gla_p2(1)
                    merge_post(0, 0, W0_, P0_)
                    merge_post(0, 1, W0_, P1_)
                    if g + 1 < NG:
                        rms_stats(l, g + 1, sqr, B_sqr, rstd, B_rstd)
                    for st2 in range(1, 4):
                        W_ = merge_load(st2)
                        for jj in range(2):
                            merge_post(st2, jj, W_, merge_pre(st2, jj, W_))
                    if stop == ("B3", l):
                        raise _Stop()
                    if stop == ("B4", l):
                        raise _Stop()
                    if g == NG - 1:
                        st_ = {}
                        for j in range(8):
                            outproj_chunk(g, j, st_)
                    else:
                        pend_out = (g, {})
                if stop == ("mix", l):
                    break
                S.barrier()
                AR.reset()
                hgs_ = [(AR.get([8, G], BF16), [Buf("hgC0_%d" % k_) for k_ in range(8)]), (AR.get([8, G], BF16), [Buf("hgC1_%d" % k_) for k_ in range(8)])]
                sqr = AR.get([2, G], BF16); B_sqr = [Buf("sq0C"), Buf("sq1C")]
                rstd = AR.get([G], F32); B_rstd = Buf("rstdC")
                gt = AR.get([NJ, G], BF16); B_gt = Buf("gt")
                aext = AR.get([2, G + 2], F32); B_aext = [Buf("aext0"), Buf("aext1")]
                yv = AR.get([2, G], F32); B_yv = [Buf("yv0"), Buf("yv1")]
                msg2 = AR.get([16], F32); B_msg2 = Buf("msg2")
                gath2 = AR.get([4, 16], F32); B_gath2 = Buf("gath2")
                xh = AR.get([8, 2], F32); B_xh = Buf("xh")
                sq2 = AR.get([8, 2], BF16); B_sq2 = Buf("sq2")
                rs2 = AR.get([2], F32); B_rs2 = Buf("rs2")
                h2h1 = AR.get([8, 2], BF16); B_h2h1 = Buf("h2h1")
                B_cin2 = Buf("cin2"); B_cout2 = Buf("cout2")

                def halo_norm(src3, B_src, dst, B_dst):
                    S.op("act", lambda e: e.activation(out=sq2[:], in_=src3, func=AF.Square), reads=B_src, writes=[B_sq2])
                    ps, Bp = psum()
                    for k in range(8):
                        S.op("pe", lambda e, k=k, ps=ps: e.matmul(ps[:, 0:2], lhsT=ones_dm[:], rhs=sq2[:, k, :],
                                                                  start=(k == 0), stop=(k == 7)),
                             reads=[B_sq2, B_const], writes=[Bp])
                    rsqrt_eps(rs2[:], ps[:, 0:2], [Bp], B_rs2)
                    for k in range(8):
                        S.op("dve", lambda e, k=k: e.scalar_tensor_tensor(out=dst[:, k, :], in0=src3[:, k, :],
                                                                          scalar=spt[:, l, 8 + k:9 + k], in1=rs2[:],
                                                                          op0=ALU.mult, op1=ALU.mult),
                             reads=B_src + [B_rs2, B_const], writes=[B_dst])

                S.op("act", lambda e: e.activation(out=msg2[:].rearrange("p (k t) -> p k t", k=8),
                                                   in_=xT[:, :, TOK - 2:TOK], func=AF.Copy), reads=[B_x[NG - 1]],
                     writes=[B_msg2])
                S.dma("sp", lambda e: e.dma_start(out=cin2[l], in_=msg2[:]), "cin", reads=[B_msg2], writes=[B_cin2])
                S.cc(lambda e: e.collective_compute("AllGather", ALU.bypass, replica_groups=RG, ins=[cin2[l].opt()],
                                                    outs=[cout2[l].opt()]), "cc", reads=[B_cin2], writes=[B_cout2])
                S.dma("sp", lambda e: e.dma_start(out=gath2[:], in_=cout2[l].rearrange("(j p) n -> p j n", p=128)), "gath",
                      reads=[B_cout2], writes=[B_gath2])
                halo_norm(xT[:, :, G - 2:G], [B_x[0]], h2h1, B_h2h1)
                order_ = [1, 2, 3, 0]
                rmsnorm_group(l, order_[0], 8, hgs_[0][0], hgs_[0][1], sqr, B_sqr, rstd, B_rstd)
                for p_, g in enumerate(order_):
                    gs = slice(g * G, (g + 1) * G)
                    if g == 0:
                        xhf = xh[:].rearrange("p k t -> p (k t)")
                        S.op("dve", lambda e: e.tensor_scalar(out=xhf, in0=gath2[:, 0, :], scalar1=flags[:, 0:1],
                                                              scalar2=None, op0=ALU.mult), reads=[B_gath2, B_const],
                             writes=[B_xh])
                        for jr in range(1, 4):
                            S.op("dve", lambda e, jr=jr: e.scalar_tensor_tensor(out=xhf, in0=gath2[:, jr, :],
                                                                                scalar=flags[:, jr:jr + 1], in1=xhf,
                                                                                op0=ALU.mult, op1=ALU.add),
                                 reads=[B_gath2, B_xh, B_const], writes=[B_xh])
                        halo_norm(xh[:], [B_xh], h2halo, B_h2halo)
                    hg, B_hg = hgs_[p_ % 2]
                    for jp in range(NJ // 2):
                        wfa, Bwfa = wload(wfi_d[l].rearrange("(k p) n -> p k n", p=128)[:, :, jp * 256:(jp + 1) * 256],
                                          lambda t: v3(t, 8, 256))
                        wfu, Bwfu = wload(
                            wfi_d[l].rearrange("(k p) n -> p k n", p=128)[:, :, D_FF + jp * 256:D_FF + (jp + 1) * 256],
                            lambda t: v3(t, 8, 256))
                        for jj in range(2):
                            j = jp * 2 + jj
                            r = j % 2
                            cw = 46 + j * 3
                            psa, Bpa = proj_chunk(wfa, Bwfa, lambda t, k, jj=jj: v3(t, 8, 256)[:, k, jj * 128:(jj + 1) * 128],
                                                  hg, B_hg)
                            psu, Bpu = proj_chunk(wfu, Bwfu, lambda t, k, jj=jj: v3(t, 8, 256)[:, k, jj * 128:(jj + 1) * 128],
                                                  hg, B_hg)
                            if g in (0, 1):
                                hsrc, B_hsrc = (h2halo, B_h2halo) if g == 0 else (h2h1, B_h2h1)
                                psh, Bph = psum()
                                for k in range(8):
                                    S.op("pe", lambda e, k=k, psh=psh, jj=jj, wfa=wfa, hsrc=hsrc: e.matmul(
                                        psh[:, 0:2], lhsT=v3(wfa, 8, 256)[:, k, jj * 128:(jj + 1) * 128], rhs=hsrc[:, k, :],
                                        start=(k == 0), stop=(k == 7)), reads=[Bwfa, B_hsrc], writes=[Bph])
                                S.op("act", lambda e, psh=psh, r=r: e.activation(out=aext[:, r, 0:2], in_=psh[:, 0:2],
                                                                                 func=AF.Copy), reads=[Bph],
                                     writes=[B_aext[r]])
                            else:
                                S.op("act", lambda e, r=r, j=j: e.activation(out=aext[:, r, 0:2], in_=ahalo[:, j, :], func=AF.Copy),
                                     reads=[B_ahalo], writes=[B_aext[r]])
                            S.op("act", lambda e, psa=psa, r=r: e.activation(out=aext[:, r, 2:G + 2], in_=psa[:],
                                                                             func=AF.Copy), reads=[Bpa], writes=[B_aext[r]])
                            S.op("act", lambda e, r=r, j=j: e.activation(out=ahalo[:, j, :], in_=aext[:, r, G:G + 2], func=AF.Copy),
                                 reads=[B_aext[r]], writes=[B_ahalo])
                            S.op("dve", lambda e, r=r, cw=cw, j=j: e.tensor_scalar(
                                out=yv[:, r, :], in0=aext[:, r, 2:G + 2], scalar1=spt[:, l, cw + 2:cw + 3],
                                scalar2=spt[:, l, 24 + j:25 + j], op0=ALU.mult, op1=ALU.add),
                                 reads=[B_aext[r], B_const], writes=[B_yv[r]])
                            S.op("dve", lambda e, r=r, cw=cw: e.scalar_tensor_tensor(
                                out=yv[:, r, :], in0=aext[:, r, 1:G + 1], scalar=spt[:, l, cw + 1:cw + 2], in1=yv[:, r, :],
                                op0=ALU.mult, op1=ALU.add), reads=[B_aext[r], B_yv[r], B_const], writes=[B_yv[r]])
                            S.op("dve", lambda e, r=r, cw=cw: e.scalar_tensor_tensor(
                                out=yv[:, r, :], in0=aext[:, r, 0:G], scalar=spt[:, l, cw:cw + 1], in1=yv[:, r, :],
                                op0=ALU.mult, op1=ALU.add), reads=[B_aext[r], B_yv[r], B_const], writes=[B_yv[r]])
                            S.op("act", lambda e, r=r: e.activation(out=yv[:, r, :], in_=yv[:, r, :], func=AF.Silu),
                                 reads=[B_yv[r]], writes=[B_yv[r]])
                            S.op("dve", lambda e, r=r, j=j, psu=psu: e.tensor_tensor(out=gt[:, j, :], in0=psu[:],
                                                                                     in1=yv[:, r, :], op=ALU.mult),
                                 reads=[Bpu, B_yv[r]], writes=[B_gt])
                    for jp2 in range(4):
                        pss2 = [psum(), psum()]
                        for part in range(3):
                            k0 = part * 8
                            nk = min(8, NJ - k0)
                            wo, Bwo = wload(
                                wfo_d[l].rearrange("(c p) n -> p c n", p=128)[:, k0:k0 + nk, jp2 * 256:(jp2 + 1) * 256],
                                lambda t, nk=nk: v3(t, nk, 256))
                            for jj in range(2):
                                ps, Bp = pss2[jj]
                                for kk in range(nk):
                                    c = k0 + kk
                                    S.op("pe", lambda e, ps=ps, kk=kk, c=c, jj=jj, wo=wo, nk=nk: e.matmul(
                                        ps[:], lhsT=v3(wo, nk, 256)[:, kk, jj * 128:(jj + 1) * 128], rhs=gt[:, c, :],
                                        start=(c == 0), stop=(c == NJ - 1)), reads=[Bwo, B_gt], writes=[Bp])
                        for jj in range(2):
                            ps, Bp = pss2[jj]
                            jo = jp2 * 2 + jj
                            S.op("dve", lambda e, ps=ps, jo=jo: e.tensor_tensor(out=xT[:, jo, gs], in0=ps[:], in1=xT[:, jo, gs],
                                                                                op=ALU.add),
                                 reads=[Bp, B_x[g]], writes=[B_x[g]])
                        if l == DEPTH - 1 and stop is None:
                            if p_ == NG - 1:
                                jo0 = jp2 * 2
                                S.dma("sp", lambda e, jo0=jo0, gs=gs: e.dma_start(out=y_d[:, jo0:jo0 + 2, gs],
                                                                                 in_=xT[:, jo0:jo0 + 2, gs]),
                                      "out", reads=[B_x[g]])
                            elif jp2 == 3:
                                S.dma("sp", lambda e, gs=gs: e.dma_start(out=y_d[:, :, gs], in_=xT[:, :, gs]), "out",
                                      reads=[B_x[g]])
                            out_done[0] = True
                        if jp2 == 0 and p_ + 1 < NG:
                            rmsnorm_group(l, order_[p_ + 1], 8, hgs_[(p_ + 1) % 2][0], hgs_[(p_ + 1) % 2][1], sqr, B_sqr,
                                          rstd, B_rstd)
                if stop == ("layer", l):
                    break
        except _Stop:
            pass

        for g in ((1, 2, 3, 0) if not out_done[0] else ()):
            S.dma("sp", lambda e, g=g: e.dma_start(out=y_d[:, :, g * G:(g + 1) * G], in_=xT[:, :, g * G:(g + 1) * G]),
                  "out", reads=[B_x[g]])
        S.final_wait("sp")
        S.emit()
    return nc


def _layout_inputs(inputs):
    f32 = np.float32
    x = np.asarray(inputs["x"], f32)
    L = DEPTH
    sp = np.zeros((L, 128, NSP), f32)
    p = np.arange(128)
    for l in range(L):
        sp[l, :, 0:8] = np.asarray(inputs["ln_mix_g"][l], f32).reshape(8, 128).T
        sp[l, :, 8:16] = np.asarray(inputs["ln_ffn_g"][l], f32).reshape(8, 128).T
        sp[l, :, 16] = np.asarray(inputs["q_norm_g"][l], f32)[p % 64]
        sp[l, :, 17] = np.asarray(inputs["k_norm_g"][l], f32)[p % 64]
        sp[l, :, 18:22] = np.asarray(inputs["b_gk"][l], f32).reshape(4, 128).T
        sp[l, :, 22:24] = np.asarray(inputs["gla_norm_g"][l], f32).reshape(2, 128).T
        sp[l, :, 24:46] = np.asarray(inputs["conv_b"][l], f32).reshape(NJ, 128).T
        cw = np.asarray(inputs["conv_w"][l], f32)
        sp[l, :, 46:112] = cw.reshape(3, NJ, 128).transpose(2, 1, 0).reshape(128, NJ * 3)
        sp[l, :, 112:120] = np.asarray(inputs["sinks"][l], f32)[None, :]
    shared = {
        "sp": sp,
        "w_gk_up": np.ascontiguousarray(np.asarray(inputs["w_gk_up"], f32)),
        "w_in": np.ascontiguousarray(np.asarray(inputs["w_in"], f32)),
        "w_branch_a": np.ascontiguousarray(np.asarray(inputs["w_branch_a"], f32)),
        "w_branch_b": np.ascontiguousarray(np.asarray(inputs["w_branch_b"], f32)),
        "w_out": np.ascontiguousarray(np.asarray(inputs["w_out"], f32)),
        "w_ffn_in": np.ascontiguousarray(np.asarray(inputs["w_ffn_in"], f32)),
        "w_ffn_out": np.ascontiguousarray(np.asarray(inputs["w_ffn_out"], f32)),
    }
    in_maps = []
    for c in range(8):
        b, r = c // 4, c % 4
        xs = x[b, r * TOK:(r + 1) * TOK, :]
        xt = np.ascontiguousarray(xs.T.reshape(8, 128, TOK).transpose(1, 0, 2))
        fl = np.zeros((128, 8), f32)
        if r > 0:
            fl[:, r - 1] = 1.0
            fl[:, 7] = 1.0
        for j in range(3):
            fl[:, 4 + j] = 1.0 if j < r else 0.0
        m = dict(shared)
        m["xT"] = xt
        m["flags"] = fl
        in_maps.append(m)
    return in_maps


def _gather_out(results):
    out = np.zeros((2, 4 * TOK, D), np.float32)
    for c in range(8):
        b, r = c // 4, c % 4
        yt = np.asarray(results[c]["yT"])
        out[b, r * TOK:(r + 1) * TOK, :] = yt.transpose(2, 1, 0).reshape(TOK, D)
    return out


_NC_CACHE = {}


def kernel(**inputs):
    if "nc" not in _NC_CACHE:
        _NC_CACHE["nc"] = build()
    nc = _NC_CACHE["nc"]
    in_maps = _layout_inputs(inputs)
    res = run_bass_kernel_spmd(nc, in_maps, core_ids=list(range(8)))
    return _gather_out(res.results)
```

```python
import numpy as np
from contextlib import ExitStack
import concourse.bass as bass
import concourse.mybir as mybir
from concourse.bass_utils import run_bass_kernel_spmd

F32 = mybir.dt.float32
BF16 = mybir.dt.bfloat16
I32 = mybir.dt.int32
AF = mybir.ActivationFunctionType
ALU = mybir.AluOpType

D = 1024
TOK = 2048
G = 512
NG = TOK // G
DEPTH = 2
IN_W = 5904
D_FF = 2816
NJ = D_FF // 128
EPS = 1e-6
C_QA, C_KA, C_VA, C_QB, C_KB, C_VB, C_RB, C_GK, C_GA, C_GB = 0, 512, 640, 768, 1280, 1792, 2816, 3840, 3856, 4880
NSP = 120
HORD = (0, 2, 1, 3)
MSGW = 1024 + 4 + 256 + 128
EPOCH = 12000


import types


def _snap(fn):
    if fn is None or fn.__closure__ is None:
        return fn
    cells = tuple(types.CellType(c.cell_contents) for c in fn.__closure__)
    return types.FunctionType(fn.__code__, fn.__globals__, fn.__name__, fn.__defaults__, cells)


class _Stop(Exception):
    pass


def _bk(B, k):
    return B[k] if isinstance(B, list) else B


class Buf:
    __slots__ = ("name", "w", "r", "excl")

    def __init__(self, name, excl=False):
        self.name, self.w, self.r, self.excl = name, None, [], excl


class Sched:
    ENG = ("pe", "act", "dve", "pool", "sp")

    def __init__(self, nc, es):
        self.nc, self.es = nc, es
        self.prog = {e: [] for e in self.ENG}
        self.cnt = {e: 0 for e in self.ENG}
        self.nsem = 0
        self.sems = {e: self._newsem() for e in self.ENG}
        self.waited = {e: {} for e in self.ENG}
        self.dsem = {}
        self.last_tok = {e: None for e in self.ENG}
        self.dma_toks = {}

    def _newsem(self):
        self.nsem += 1
        return self.es.enter_context(self.nc.semaphore("s%d" % self.nsem))

    def _waits(self, eng, deps):
        waits = []
        for (sem, val, src) in deps:
            if src == eng == "pe":
                continue
            key = id(sem)
            if self.waited[eng].get(key, 0) >= val:
                continue
            self.waited[eng][key] = val
            waits.append((sem, val))
        return waits

    def _deps(self, reads, writes):
        deps = []
        for b in reads:
            if b.excl:
                deps += b.r
            if b.w is not None:
                deps.append(b.w)
        for b in writes:
            deps += b.r
            if b.w is not None:
                deps.append(b.w)
        return deps

    def _commit(self, tok, reads, writes):
        for b in reads:
            if b.excl:
                b.w, b.r = tok, []
            else:
                b.r.append(tok)
        for b in writes:
            b.w, b.r = tok, []

    def op(self, eng, fn, reads=(), writes=()):
        waits = self._waits(eng, self._deps(reads, writes))
        if self.cnt[eng] >= EPOCH:
            self.sems[eng] = self._newsem()
            self.cnt[eng] = 0
        self.cnt[eng] += 1
        sem = self.sems[eng]
        tok = (sem, self.cnt[eng], eng)
        self.prog[eng].append((waits, _snap(fn), sem, 1))
        self.last_tok[eng] = tok
        self._commit(tok, reads, writes)

    def dma(self, eng, fn, semname, reads=(), writes=()):
        waits = self._waits(eng, self._deps(reads, writes))
        if semname not in self.dsem:
            self.dsem[semname] = [self._newsem(), 0]
        s = self.dsem[semname]
        s[1] += 16
        tok = (s[0], s[1], "dma")
        self.prog[eng].append((waits, _snap(fn), s[0], 16))
        self.dma_toks[semname] = tok
        self._commit(tok, reads, writes)

    def cc(self, fn, semname, reads=(), writes=()):
        waits = self._waits("pool", self._deps(reads, writes))
        if semname not in self.dsem:
            self.dsem[semname] = [self._newsem(), 0]
        s = self.dsem[semname]
        s[1] += 1
        tok = (s[0], s[1], "dma")
        self.prog["pool"].append((waits, _snap(fn), s[0], None))
        self.dma_toks[semname] = tok
        self._commit(tok, reads, writes)

    def barrier(self, skip=()):
        toks = [t for t in self.last_tok.values() if t is not None] + \
               [t for n_, t in self.dma_toks.items() if n_ not in skip]
        for e in self.ENG:
            w = self._waits(e, [t for t in toks if t[2] != e])
            if w:
                self.prog[e].append((w, None, None, 0))

    def final_wait(self, eng):
        toks = [t for t in self.last_tok.values() if t is not None] + list(self.dma_toks.values())
        w = self._waits(eng, [t for t in toks if t[2] != eng])
        self.prog[eng].append((w, None, None, 0))

    def emit(self):
        nc = self.nc
        prog = self.prog

        def replay(e, lst):
            for waits, fn, sem, inc in lst:
                for (s, v) in waits:
                    e.wait_ge(s, v)
                if fn is None:
                    continue
                ins = fn(e)
                if inc is None:
                    ins.then_inc(sem)
                else:
                    ins.then_inc(sem, inc)

        with nc.Block() as block:
            @block.tensor
            def _(e):
                replay(e, prog["pe"])

            @block.scalar
            def _(e):
                replay(e, prog["act"])

            @block.vector
            def _(e):
                replay(e, prog["dve"])

            @block.gpsimd
            def _(e):
                replay(e, prog["pool"])

            @block.sync
            def _(e):
                replay(e, prog["sp"])


class Arena:
    def __init__(self, ap_f32, nwords):
        self.base, self.n, self.off = ap_f32, nwords, 0

    def reset(self):
        self.off = 0

    def view(self, off, shape, dtype, parts=128):
        save = self.off
        self.off = off
        ap = self.get(shape, dtype, parts)
        self.off = save
        return ap

    def get(self, shape, dtype, parts=128):
        n = int(np.prod(shape))
        words = n if dtype in (F32, I32) else (n + 1) // 2
        assert self.off + words <= self.n, ("arena overflow", self.off, words, self.n)
        ap = self.base[0:parts, self.off:self.off + words]
        self.off += words
        if dtype != F32:
            ap = ap.bitcast(dtype)
            if dtype == BF16 and n % 2:
                ap = ap[:, 0:n]
        if len(shape) == 2:
            ap = ap.rearrange("p (a b) -> p a b", a=shape[0])
        elif len(shape) == 3:
            ap = ap.rearrange("p (a b c) -> p a b c", a=shape[0], b=shape[1])
        return ap


def build(stop=None):
    nc = bass.Bass("TRN2", target_bir_lowering=False)
    dt = nc.dram_tensor
    x_d = dt("xT", [128, 8, TOK], F32, kind="ExternalInput").ap()
    flags_d = dt("flags", [128, 8], F32, kind="ExternalInput").ap()
    sp_d = dt("sp", [DEPTH, 128, NSP], F32, kind="ExternalInput").ap()
    wup_d = dt("w_gk_up", [DEPTH, 16, 512], F32, kind="ExternalInput").ap()
    win_d = dt("w_in", [DEPTH, D, IN_W], F32, kind="ExternalInput").ap()
    wba_d = dt("w_branch_a", [DEPTH, 512, D], F32, kind="ExternalInput").ap()
    wbb_d = dt("w_branch_b", [DEPTH, D, D], F32, kind="ExternalInput").ap()
    wout_d = dt("w_out", [DEPTH, D, D], F32, kind="ExternalInput").ap()
    wfi_d = dt("w_ffn_in", [DEPTH, D, 2 * D_FF], F32, kind="ExternalInput").ap()
    wfo_d = dt("w_ffn_out", [DEPTH, D_FF, D], F32, kind="ExternalInput").ap()
    y_d = dt("yT", [128, 8, TOK], F32, kind="ExternalOutput").ap()
    cin1 = [dt("cin1_%d" % l, [128, MSGW], F32, kind="Internal").ap() for l in range(DEPTH)]
    cout1 = [dt("cout1_%d" % l, [512, MSGW], F32, kind="Internal").ap() for l in range(DEPTH)]
    cin2 = [dt("cin2_%d" % l, [128, 16], F32, kind="Internal").ap() for l in range(DEPTH)]
    cout2 = [dt("cout2_%d" % l, [512, 16], F32, kind="Internal").ap() for l in range(DEPTH)]
    RG = [[0, 1, 2, 3], [4, 5, 6, 7]]

    with ExitStack() as es:
        S = Sched(nc, es)

        def sb(name, shape, dtype):
            return es.enter_context(nc.sbuf_tensor(name, shape, dtype))

        xT = sb("xTs", [128, 8, TOK], F32)
        oloc = sb("oloc", [128, 8, TOK], BF16)
        qhat = sb("qhat", [128, 4, TOK], BF16)
        B_x = [Buf("x%d" % g) for g in range(NG)]
        B_oloc = [Buf("oloc%d" % g) for g in range(NG)]
        B_qhat = [Buf("qhat%d" % g) for g in range(NG)]
        NSLOT = 6
        wsl = [sb("wslot%d" % i, [128, 2048], BF16) for i in range(NSLOT)]
        B_w = [Buf("w%d" % i) for i in range(NSLOT)]
        ident = sb("ident", [128, 128], BF16)
        ones_dm = sb("ones_dm", [128, 128], BF16)
        ones_dv = sb("ones_dv", [128, 128], BF16)
        bd64 = sb("bd64", [128, 128], BF16)
        ones64 = sb("ones64", [128, 64], BF16)
        ones128 = sb("ones128", [128, 128], BF16)
        maskc = sb("maskc", [128, 512], BF16)
        maskp = sb("maskp", [128, 512], BF16)
        maskp0 = sb("maskp0", [128, 512], BF16)
        onesf = sb("onesf", [128, 1], F32)
        zerof = sb("zerof", [128, 1], F32)
        epsf = sb("epsf", [128, 1], F32)
        flags = sb("flags_s", [128, 8], F32)
        spt = sb("spt", [128, DEPTH, NSP], F32)
        wup = sb("wup", [16, DEPTH, 512], BF16)
        nbgk = sb("nbgk", [128, 4], F32)
        qg8 = sb("qg8", [128, 1], F32)
        es8 = sb("es8", [128, 8], F32)
        esink = sb("esink", [64, 2, 512], F32)
        Sinit_bf = sb("Sinit_bf", [128, 1024], BF16)
        kprev = sb("kprev", [128, 2, 128], BF16)
        vprev = sb("vprev", [128, 128], BF16)
        Sst = sb("Sst", [128, 1024], F32)
        Cend = sb("Cend", [128, 4], F32)
        sc = sb("sc", [128, 64], F32)
        ahalo = sb("ahalo", [128, NJ, 2], F32)
        h2halo = sb("h2halo", [128, 8, 2], BF16)
        B_const = Buf("const")
        B_lay = Buf("layerconst")
        B_Sinit = Buf("Sinit")
        B_kvprev = Buf("kvprev")
        B_S = Buf("S")
        B_Cend = Buf("Cend")
        B_ahalo = Buf("ahalo")
        B_h2halo = Buf("h2halo")
        ARW = 13312
        arena_t = sb("arena", [128, ARW], F32)
        AR = Arena(arena_t, ARW)
        pst = [es.enter_context(nc.psum_tensor("ps%d" % i, [128, 512], F32)) for i in range(8)]
        B_ps = [Buf("ps%d" % i, excl=True) for i in range(8)]
        psrr = [0]

        def psum():
            i = psrr[0] % 8
            psrr[0] += 1
            return pst[i], B_ps[i]

        slot_rr = [0]

        def wload(src_ap, view_fn, parts=128):
            i = slot_rr[0] % NSLOT
            slot_rr[0] += 1
            dst = view_fn(wsl[i])
            S.dma("pool", lambda e, d=dst, s=src_ap: e.dma_start(out=d, in_=s), "w%d" % i, writes=[B_w[i]])
            return wsl[i], B_w[i]

        def wload_multi(pairs):
            i = slot_rr[0] % NSLOT
            slot_rr[0] += 1
            for (src_ap, view_fn) in pairs:
                dst = view_fn(wsl[i])
                S.dma("pool", lambda e, d=dst, s=src_ap: e.dma_start(out=d, in_=s), "w%d" % i, writes=[B_w[i]])
            return wsl[i], B_w[i]

        def v3(t, k, n):
            return t[:, 0:k * n].rearrange("p (k n) -> p k n", k=k)

        def win_cols(l, c0, n):
            return win_d[l].rearrange("(k p) n -> p k n", p=128)[:, :, c0:c0 + n]

        iot_i = AR.get([512], F32).bitcast(I32)
        iot_f = AR.get([512], F32)
        S.op("pool", lambda e: e.iota(iot_i[:], pattern=[[0, 4], [1, 128]], base=0, channel_multiplier=-1),
             writes=[B_const])
        S.op("dve", lambda e: e.tensor_copy(out=iot_f[:], in_=iot_i[:]), reads=[B_const], writes=[B_const])
        S.op("dve", lambda e: e.tensor_single_scalar(out=maskc[:], in_=iot_f[:], scalar=0.0, op=ALU.is_ge),
             writes=[B_const])
        S.op("dve", lambda e: e.tensor_single_scalar(out=maskp[:], in_=iot_f[:], scalar=0.0, op=ALU.is_lt),
             writes=[B_const])
        S.op("dve", lambda e: e.tensor_single_scalar(out=ident[:], in_=iot_f[:, 0:128], scalar=0.0, op=ALU.is_equal),
             writes=[B_const])
        S.op("dve", lambda e: e.memset(ones_dm[:], 1.0 / 1024.0), writes=[B_const])
        S.op("dve", lambda e: e.memset(ones_dv[:], 1.0 / 256.0), writes=[B_const])
        S.op("dve", lambda e: e.memset(bd64[:], 0.0), writes=[B_const])
        S.op("dve", lambda e: e.memset(bd64[0:64, 0:64], 1.0 / 64.0), writes=[B_const])
        S.op("dve", lambda e: e.memset(bd64[64:128, 64:128], 1.0 / 64.0), writes=[B_const])
        S.op("dve", lambda e: e.memset(ones64[:], 1.0), writes=[B_const])
        S.op("dve", lambda e: e.memset(ones128[:], 1.0), writes=[B_const])
        S.op("dve", lambda e: e.memset(onesf[:], 1.0), writes=[B_const])
        S.op("dve", lambda e: e.memset(zerof[:], 0.0), writes=[B_const])
        S.op("dve", lambda e: e.memset(epsf[:], EPS), writes=[B_const])

        def rsqrt_eps(out_ap, in_ap, reads, B_out):
            S.op("act", lambda e: e.activation(out=out_ap, in_=in_ap, func=AF.Ln, bias=epsf[0:out_ap.shape[0], 0:1],
                                               scale=1.0), reads=list(reads) + [B_const], writes=[B_out])
            S.op("act", lambda e: e.activation(out=out_ap, in_=out_ap, func=AF.Exp, scale=-0.5), reads=[B_out],
                 writes=[B_out])
        S.dma("sp", lambda e: e.dma_start(out=flags[:], in_=flags_d), "misc", writes=[B_const])
        S.dma("sp", lambda e: e.dma_start(out=spt[:], in_=sp_d.rearrange("l p n -> p l n")), "misc", writes=[B_const])
        S.dma("pool", lambda e: e.dma_start(out=wup[:], in_=wup_d.rearrange("l r n -> r l n")), "misc2",
              writes=[B_const])
        for g in range(NG):
            S.dma("sp", lambda e, g=g: e.dma_start(out=xT[:, :, g * G:(g + 1) * G], in_=x_d[:, :, g * G:(g + 1) * G]),
                  "xin%d" % g, writes=[B_x[g]])
        S.op("dve", lambda e: e.tensor_scalar(out=maskp0[:], in0=maskp[:], scalar1=flags[:, 7:8], scalar2=None,
                                              op0=ALU.mult), reads=[B_const], writes=[B_const])

        def rms_stats(l, g, sqr, B_sqr, rstd, B_rstd):
            ps, Bp = psum()
            for k in range(8):
                r = k % 2
                S.op("act", lambda e, k=k, r=r: e.activation(out=sqr[:, r, :], in_=xT[:, k, g * G:(g + 1) * G],
                                                             func=AF.Square),
                     reads=[B_x[g]], writes=[B_sqr[r]])
                S.op("pe", lambda e, k=k, r=r: e.matmul(ps[:], lhsT=ones_dm[:], rhs=sqr[:, r, :], start=(k == 0),
                                                        stop=(k == 7)),
                     reads=[B_sqr[r], B_const], writes=[Bp])
            rsqrt_eps(rstd[:], ps[:], [Bp], B_rstd)

        def rms_scale(l, g, gcol, hg, B_hg, rstd, B_rstd):
            for k in range(8):
                S.op("dve", lambda e, k=k: e.scalar_tensor_tensor(out=hg[:, k, :], in0=xT[:, k, g * G:(g + 1) * G],
                                                                  scalar=spt[:, l, gcol + k:gcol + k + 1], in1=rstd[:],
                                                                  op0=ALU.mult, op1=ALU.mult),
                     reads=[B_x[g], B_rstd, B_const], writes=[_bk(B_hg, k)])

        def rmsnorm_group(l, g, gcol, hg, B_hg, sqr, B_sqr, rstd, B_rstd):
            rms_stats(l, g, sqr, B_sqr, rstd, B_rstd)
            rms_scale(l, g, gcol, hg, B_hg, rstd, B_rstd)

        def proj_chunk(wt, Bw, kview_fn, hg, B_hg, nk=8):
            ps, Bp = psum()
            for k in range(nk):
                S.op("pe", lambda e, k=k: e.matmul(ps[:], lhsT=kview_fn(wt, k), rhs=hg[:, k, :], start=(k == 0),
                                                   stop=(k == nk - 1)),
                     reads=[Bw, _bk(B_hg, k)], writes=[Bp])
            return ps, Bp

        def qknorm(ps, Bp, sq, B_sq, rs, B_rs, out_ap, B_out, gain_ap, ncols):
            S.op("act", lambda e: e.activation(out=sq[:, 0:ncols], in_=ps[:, 0:ncols], func=AF.Square), reads=[Bp],
                 writes=[B_sq])
            ps2, Bp2 = psum()
            S.op("pe", lambda e: e.matmul(ps2[:, 0:ncols], lhsT=bd64[:], rhs=sq[:, 0:ncols], start=True, stop=True),
                 reads=[B_sq, B_const], writes=[Bp2])
            rsqrt_eps(rs[:, 0:ncols], ps2[:, 0:ncols], [Bp2], B_rs)
            S.op("dve", lambda e: e.scalar_tensor_tensor(out=out_ap, in0=ps[:, 0:ncols], scalar=gain_ap,
                                                         in1=rs[:, 0:ncols], op0=ALU.mult, op1=ALU.mult),
                 reads=[Bp, B_rs, B_lay, B_const], writes=[B_out])

        def kdup_load(l):
            cols = []
            for hk in range(2):
                for dup in range(2):
                    o = (hk * 2 + dup) * 64
                    cols.append((win_cols(l, C_KA + hk * 64, 64),
                                 lambda t, o=o: v3(t, 8, 256)[:, :, o:o + 64]))
            return wload_multi(cols)

        try:
            out_done = [False]
            for l in range(DEPTH if stop != "init" else 0):
                S.op("dve", lambda e: e.tensor_scalar(out=nbgk[:], in0=spt[:, l, 18:22], scalar1=-1.0, scalar2=None,
                                                      op0=ALU.mult), reads=[B_const], writes=[B_lay])
                S.op("dve", lambda e: e.tensor_scalar(out=qg8[:], in0=spt[:, l, 16:17], scalar1=0.125, scalar2=None,
                                                      op0=ALU.mult), reads=[B_const], writes=[B_lay])
                S.op("act", lambda e: e.activation(out=es8[:], in_=spt[:, l, 112:120], func=AF.Exp), reads=[B_const],
                     writes=[B_lay])
                for hk in range(2):
                    for hh in range(4):
                        S.op("dve", lambda e, hk=hk, hh=hh: e.tensor_copy(
                            out=esink[:, hk, hh * 128:(hh + 1) * 128],
                            in_=es8[0:64, hk * 4 + HORD[hh]:hk * 4 + HORD[hh] + 1].to_broadcast([64, 128])),
                             reads=[B_lay], writes=[B_lay])
                S.op("dve", lambda e: e.memset(Sst[:], 0.0), writes=[B_S])
                S.op("dve", lambda e: e.memset(Cend[:], 0.0), writes=[B_Cend])

                S.barrier()
                AR.reset()
                hg = AR.get([8, G], BF16); B_hg = [Buf("hgA%d" % k_) for k_ in range(8)]
                sqr = AR.get([2, G], BF16); B_sqr = [Buf("sq0"), Buf("sq1")]
                rstd = AR.get([G], F32); B_rstd = Buf("rstd")
                qt = AR.get([4, G], BF16); B_qt = Buf("qt")
                kt = AR.get([4, G], BF16); B_kt = Buf("kt")
                vtok = AR.get([4, 1024], BF16); B_vtok = Buf("vtok")
                lsp = AR.get([G], F32); B_lsp = Buf("lsp")
                Cc = AR.get([G], F32); B_Cc = Buf("Cc")
                E1 = AR.get([G], F32); B_E1 = Buf("E1")
                E2 = AR.get([G], F32); B_E2 = Buf("E2")
                gklr = AR.get([G], BF16); B_gklr = Buf("gklr")
                ATs = AR.get([2, G], BF16); B_AT = [Buf("AT0"), Buf("AT1")]
                ktok = AR.get([2, G], BF16); B_ktok = [Buf("ktok0"), Buf("ktok1")]
                U = AR.get([1024], F32); B_U = Buf("U")
                Ubf = AR.get([1024], BF16); B_Ubf = Buf("Ubf")
                dS = AR.get([4], F32); B_dS = Buf("dS")
                msg = AR.get([MSGW - 1024], F32); B_msg = Buf("msg")
                sqk = AR.get([G], BF16); B_sqk = Buf("sqk")
                rsk = AR.get([G], F32); B_rsk = Buf("rsk")

                for g in range(NG):
                    if g == 0:
                        rms_stats(l, g, sqr, B_sqr, rstd, B_rstd)
                        rms_scale(l, g, 0, hg, B_hg, rstd, B_rstd)
                    wt, Bw = wload(win_cols(l, C_GK, 128), lambda t: v3(t, 8, 128))
                    ps, Bp = psum()
                    for k in range(8):
                        S.op("pe", lambda e, k=k, wt=wt: e.matmul(ps[:, :], lhsT=v3(wt, 8, 128)[:, k, :], rhs=hg[:, k, :],
                                                                  start=(k == 0), stop=(k == 7)),
                             reads=[Bw, _bk(B_hg, k)], writes=[Bp])
                    S.op("act", lambda e, ps=ps: e.activation(out=gklr[0:16, :], in_=ps[0:16, :], func=AF.Copy),
                         reads=[Bp], writes=[B_gklr])
                    wqs = [wload(win_cols(l, C_QB + hp_ * 256, 256), lambda t: v3(t, 8, 256)) for hp_ in range(2)]
                    wks = [wload(win_cols(l, C_KB + hp_ * 256, 256), lambda t: v3(t, 8, 256)) for hp_ in range(2)]
                    def v_unit(q4, half, wv, Bwv):
                        ps, Bp = psum()
                        for bb in range(2):
                            blk = half * 2 + bb
                            for k in range(8):
                                S.op("pe", lambda e, k=k, blk=blk, bb=bb, ps=ps, wv=wv: e.matmul(
                                    ps[:, bb * 256:(bb + 1) * 256], lhsT=hg[:, k, blk * 128:(blk + 1) * 128],
                                    rhs=v3(wv, 8, 256)[:, k, :], start=(k == 0), stop=(k == 7)),
                                     reads=[Bwv, _bk(B_hg, k)], writes=[Bp])
                        S.op("act", lambda e, half=half, q4=q4, ps=ps: e.activation(
                            out=vtok[:, half * 2:half * 2 + 2, q4 * 256:(q4 + 1) * 256],
                            in_=ps[:].rearrange("p (b n) -> p b n", b=2), func=AF.Copy),
                             reads=[Bp], writes=[B_vtok])
                    for h in range(4):
                        psg, Bpg = psum()
                        S.op("pe", lambda e, h=h, psg=psg: e.matmul(psg[:], lhsT=wup[0:16, l, h * 128:(h + 1) * 128],
                                                                    rhs=gklr[0:16, :], start=True, stop=True),
                             reads=[B_gklr, B_const], writes=[Bpg])
                        S.op("act", lambda e, h=h, psg=psg: e.activation(out=lsp[:], in_=psg[:], func=AF.Exp,
                                                                         bias=nbgk[:, h:h + 1], scale=-1.0),
                             reads=[Bpg, B_lay], writes=[B_lsp])
                        S.op("act", lambda e: e.activation(out=lsp[:], in_=lsp[:], func=AF.Ln, bias=onesf[:, 0:1], scale=1.0),
                             reads=[B_lsp, B_const], writes=[B_lsp])
                        S.op("dve", lambda e, h=h: e.tensor_copy(out=sc[:, 0:1], in_=Cend[:, h:h + 1]), reads=[B_Cend],
                             writes=[B_dS])
                        S.op("dve", lambda e, h=h: e.tensor_tensor_scan(out=Cc[:], data0=onesf[:, 0:1].to_broadcast([128, G]),
                                                                        data1=lsp[:], initial=sc[:, 0:1], op0=ALU.mult,
                                                                        op1=ALU.add),
                             reads=[B_lsp, B_dS, B_const], writes=[B_Cc])
                        S.op("dve", lambda e, h=h: e.tensor_copy(out=Cend[:, h:h + 1], in_=Cc[:, G - 1:G]), reads=[B_Cc],
                             writes=[B_Cend])
                        S.op("dve", lambda e: e.tensor_scalar(out=sc[:, 1:2], in0=Cc[:, 255:256], scalar1=1.0 / 16.0,
                                                              scalar2=None, op0=ALU.mult), reads=[B_Cc], writes=[B_dS])
                        S.op("dve", lambda e: e.tensor_scalar(out=sc[:, 2:3], in0=Cc[:, 255:256], scalar1=-1.0 / 16.0,
                                                              scalar2=None, op0=ALU.mult), reads=[B_Cc], writes=[B_dS])
                        S.op("act", lambda e: e.activation(out=E1[:], in_=Cc[:], func=AF.Exp, bias=sc[:, 1:2],
                                                           scale=-1.0 / 16.0), reads=[B_Cc, B_dS], writes=[B_E1])
                        S.op("act", lambda e: e.activation(out=E2[:], in_=Cc[:], func=AF.Exp, bias=sc[:, 2:3],
                                                           scale=1.0 / 16.0), reads=[B_Cc, B_dS], writes=[B_E2])
                        S.op("act", lambda e: e.activation(out=sc[:, 3:4], in_=sc[:, 0:1], func=AF.Exp, bias=sc[:, 2:3],
                                                           scale=1.0 / 16.0), reads=[B_dS], writes=[B_dS])
                        S.op("act", lambda e: e.activation(out=sc[:, 4:5], in_=Cc[:, 255:256], func=AF.Exp,
                                                           scale=-1.0 / 16.0), reads=[B_Cc, B_dS], writes=[B_dS])
                        S.op("dve", lambda e, h=h: e.tensor_copy(out=dS[:, h:h + 1], in_=E1[:, G - 1:G]), reads=[B_E1],
                             writes=[B_dS])
                        wq, Bwq = wqs[h // 2]
                        wk, Bwk = wks[h // 2]
                        psq, Bpq = proj_chunk(wq, Bwq, lambda t, k, h=h: v3(t, 8, 256)[:, k, (h % 2) * 128:(h % 2 + 1) * 128], hg,
                                              B_hg)
                        S.op("dve", lambda e, h=h, psq=psq: e.scalar_tensor_tensor(out=qt[:, h, :], in0=psq[:],
                                                                                   scalar=128.0 ** -0.5, in1=E1[:],
                                                                                   op0=ALU.mult, op1=ALU.mult),
                             reads=[Bpq, B_E1], writes=[B_qt])
                        psk, Bpk = proj_chunk(wk, Bwk, lambda t, k, h=h: v3(t, 8, 256)[:, k, (h % 2) * 128:(h % 2 + 1) * 128], hg,
                                              B_hg)
                        S.op("dve", lambda e, h=h, psk=psk: e.tensor_tensor(out=kt[:, h, :], in0=psk[:], in1=E2[:],
                                                                            op=ALU.mult),
                             reads=[Bpk, B_E2], writes=[B_kt])
                        S.op("dve", lambda e, h=h: e.tensor_scalar(out=qhat[:, h, g * G:(g + 1) * G], in0=qt[:, h, :],
                                                                    scalar1=sc[:, 4:5], scalar2=None, op0=ALU.mult),
                             reads=[B_qt, B_dS], writes=[B_qhat[g]])
                        S.op("dve", lambda e, h=h: e.tensor_scalar(out=U[:, h * 256:(h + 1) * 256],
                                                                   in0=Sst[:, h * 256:(h + 1) * 256], scalar1=sc[:, 3:4],
                                                                   scalar2=None, op0=ALU.mult),
                             reads=[B_S, B_dS], writes=[B_U])
                    S.op("act", lambda e: e.activation(out=Ubf[:], in_=U[:], func=AF.Copy), reads=[B_U], writes=[B_Ubf])
                    for q4 in range(4):
                        wv, Bwv = wload(win_cols(l, C_VB + q4 * 256, 256), lambda t: v3(t, 8, 256))
                        for half in range(2):
                            v_unit(q4, half, wv, Bwv)
                    def blk_stage1(j, par):
                        tsl = slice(j * 128, (j + 1) * 128)
                        psA, BpA = psum()
                        for h in range(4):
                            S.op("pe", lambda e, h=h, psA=psA, tsl=tsl: e.matmul(psA[:, h * 128:(h + 1) * 128],
                                                                                 lhsT=kt[:, h, tsl], rhs=qt[:, h, tsl],
                                                                                 start=True, stop=True),
                                 reads=[B_kt, B_qt], writes=[BpA])
                        S.op("dve", lambda e, psA=psA, par=par: e.tensor_tensor(out=ATs[:, par, :], in0=psA[:], in1=maskc[:],
                                                                                op=ALU.mult),
                             reads=[BpA, B_const], writes=[B_AT[par]])
                        psT, BpT = psum()
                        psTb = psT[:, 0:256].bitcast(BF16)
                        for h in range(4):
                            S.op("pe", lambda e, h=h, psTb=psTb, tsl=tsl: e.transpose(psTb[:, h * 128:(h + 1) * 128],
                                                                                      kt[:, h, tsl], ident[:]),
                                 reads=[B_kt, B_const], writes=[BpT])
                        S.op("act", lambda e, psTb=psTb, par=par: e.activation(out=ktok[:, par, :], in_=psTb, func=AF.Copy),
                             reads=[BpT], writes=[B_ktok[par]])

                    def blk_stage2(j, par):
                        tsl = slice(j * 128, (j + 1) * 128)
                        kvs = []
                        for hp in range(2):
                            pskv, Bpkv = psum()
                            for hh in range(2):
                                h = hp * 2 + hh
                                S.op("pe", lambda e, pskv=pskv, hh=hh, h=h, j=j, par=par: e.matmul(
                                    pskv[:, hh * 256:(hh + 1) * 256], lhsT=ktok[:, par, h * 128:(h + 1) * 128],
                                    rhs=vtok[:, j, h * 256:(h + 1) * 256], start=True, stop=True),
                                     reads=[B_ktok[par], B_vtok], writes=[Bpkv])
                            kvs.append((pskv, Bpkv))
                        for hp in range(2):
                            pso, Bpo = psum()
                            for hh in range(2):
                                h = hp * 2 + hh
                                for c in range(2):
                                    oc = (hh * 2 + c) * 128
                                    ec = h * 256 + c * 128
                                    S.op("pe", lambda e, pso=pso, oc=oc, ec=ec, h=h, j=j, par=par: e.matmul(
                                        pso[:, oc:oc + 128], lhsT=vtok[:, j, ec:ec + 128],
                                        rhs=ATs[:, par, h * 128:(h + 1) * 128], start=True, stop=False),
                                         reads=[B_vtok, B_AT[par]], writes=[Bpo])
                                    S.op("pe", lambda e, pso=pso, oc=oc, ec=ec, h=h, tsl=tsl: e.matmul(
                                        pso[:, oc:oc + 128], lhsT=Ubf[:, ec:ec + 128], rhs=qt[:, h, tsl],
                                        start=False, stop=True), reads=[B_Ubf, B_qt], writes=[Bpo])
                            S.op("act", lambda e, pso=pso, hp=hp, j=j: e.activation(
                                out=oloc[:, hp * 4:hp * 4 + 4, g * G + j * 128:g * G + (j + 1) * 128],
                                in_=pso[:].rearrange("p (c t) -> p c t", c=4), func=AF.Copy),
                                 reads=[Bpo], writes=[B_oloc[g]])
                        for hp in range(2):
                            pskv, Bpkv = kvs[hp]
                            S.op("dve", lambda e, pskv=pskv, hp=hp: e.tensor_tensor(
                                out=U[:, hp * 512:(hp + 1) * 512], in0=pskv[:], in1=U[:, hp * 512:(hp + 1) * 512],
                                op=ALU.add), reads=[Bpkv, B_U], writes=[B_U])
                        S.op("act", lambda e: e.activation(out=Ubf[:], in_=U[:], func=AF.Copy), reads=[B_U], writes=[B_Ubf])

                    blk_stage1(0, 0)
                    for j in range(4):
                        if j + 1 < 4:
                            blk_stage1(j + 1, (j + 1) % 2)
                        blk_stage2(j, j % 2)
                        if j == 0 and g + 1 < NG:
                            rms_stats(l, g + 1, sqr, B_sqr, rstd, B_rstd)
                        if j == 1 and g + 1 < NG:
                            rms_scale(l, g + 1, 0, hg, B_hg, rstd, B_rstd)
                    for h in range(4):
                        S.op("dve", lambda e, h=h: e.tensor_scalar(out=Sst[:, h * 256:(h + 1) * 256],
                                                                   in0=U[:, h * 256:(h + 1) * 256], scalar1=dS[:, h:h + 1],
                                                                   scalar2=None, op0=ALU.mult),
                             reads=[B_U, B_dS], writes=[B_S])
                    if g == NG - 1:
                        wkd, Bwkd = kdup_load(l)
                        for hk in range(2):
                            ps, Bp = psum()
                            for k in range(8):
                                S.op("pe", lambda e, k=k, hk=hk, ps=ps, wkd=wkd: e.matmul(
                                    ps[:, 0:128], lhsT=v3(wkd, 8, 256)[:, k, hk * 128:(hk + 1) * 128],
                                    rhs=hg[:, k, 384:512], start=(k == 0), stop=(k == 7)), reads=[Bwkd, _bk(B_hg, k)], writes=[Bp])
                            qknorm(ps, Bp, sqk, B_sqk, rsk, B_rsk, msg[:, 4 + hk * 128:4 + (hk + 1) * 128], B_msg,
                                   spt[:, l, 17:18], 128)
                        wv, Bwv = wload(win_cols(l, C_VA, 128), lambda t: v3(t, 8, 128))
                        ps, Bp = psum()
                        for k in range(8):
                            S.op("pe", lambda e, k=k, ps=ps, wv=wv: e.matmul(ps[:, 0:128], lhsT=hg[:, k, 384:512],
                                                                             rhs=v3(wv, 8, 128)[:, k, :], start=(k == 0),
                                                                             stop=(k == 7)), reads=[Bwv, _bk(B_hg, k)], writes=[Bp])
                        S.op("act", lambda e, ps=ps: e.activation(out=msg[:, 260:388], in_=ps[:, 0:128], func=AF.Copy),
                             reads=[Bp], writes=[B_msg])
                if stop == ("A", l):
                    break
                S.op("act", lambda e: e.activation(out=msg[:, 0:4], in_=Cend[:], func=AF.Exp, scale=-1.0 / 16.0),
                     reads=[B_Cend], writes=[B_msg])
                B_cin = Buf("cin")
                B_cout = Buf("cout")
                S.dma("sp", lambda e: e.dma_start(out=cin1[l][:, 0:1024], in_=Sst[:]), "cin", reads=[B_S], writes=[B_cin])
                S.dma("sp", lambda e: e.dma_start(out=cin1[l][:, 1024:MSGW], in_=msg[:]), "cin", reads=[B_msg],
                      writes=[B_cin])
                S.cc(lambda e: e.collective_compute("AllGather", ALU.bypass, replica_groups=RG, ins=[cin1[l].opt()],
                                                    outs=[cout1[l].opt()]), "cc", reads=[B_cin], writes=[B_cout])
                S.barrier(skip=("cc",))
                AR.reset()
                hg = AR.get([8, G], BF16); B_hg = [Buf("hgB%d" % k_) for k_ in range(8)]
                sqr = AR.get([2, G], BF16); B_sqr = [Buf("sq0B"), Buf("sq1B")]
                rstd = AR.get([G], F32); B_rstd = Buf("rstdB")
                qn_off = AR.off
                qn = AR.get([4, G], BF16); B_qn = Buf("qn")
                kdup = AR.get([2, 128 + G], BF16); B_kdup = Buf("kdup")
                vaug = AR.get([5, 2, 128], BF16); B_vaug = Buf("vaug")
                pT_off = AR.off
                pT = AR.get([2, 2, G], BF16); B_pT = [[Buf("pT00"), Buf("pT01")], [Buf("pT10"), Buf("pT11")]]
                ya_off = AR.off
                ya = AR.get([8, G], BF16, parts=64); B_ya = Buf("ya")
                ya128 = AR.view(ya_off, [8, G], BF16)
                ofp = AR.get([2, G], F32); B_ofp = Buf("ofp")
                merged_off = AR.off
                merged = AR.get([8, G], BF16); B_merged = Buf("merged")
                sr = AR.view(merged_off, [2, G], BF16); B_sr = B_merged
                sa = AR.get([G], F32); B_sa = Buf("sa")
                sbb = AR.get([G], F32); B_sbb = Buf("sbb")
                sqk = sqr[:, 0, :]; B_sqk = B_sqr[0]
                rsk = rstd; B_rsk = B_rstd
                dn = AR.get([G], F32, parts=64); B_dn = Buf("dn")
                ofp2 = AR.view(qn_off, [2, G], F32)
                sr2 = AR.view(pT_off, [2, G], BF16)
                sqr2 = AR.view(pT_off + 512, [2, G], BF16)
                GF = [dict(ofp=ofp, Bofp=[B_ofp], sr=sr, Bsr=[B_sr], sq=sqr, Bsq=[[B_sqr[0]], [B_sqr[1]]], rs=rstd,
                           Brs=[B_rstd]),
                      dict(ofp=ofp2, Bofp=[B_qn], sr=sr2, Bsr=[B_pT[0][0], B_pT[0][1]],
                           sq=sqr2, Bsq=[[B_pT[1][0]], [B_pT[1][1]]], rs=sa, Brs=[B_sa])]

                S.op("dve", lambda e: e.memset(vaug[:], 0.0), writes=[B_vaug])
                XOFF = ARW - (4 * MSGW + 1024 + 384)
                assert XOFF >= pT_off
                gath = AR.view(XOFF, [4, MSGW], F32); B_gath = Buf("gath")
                acc = AR.view(XOFF + 4 * MSGW, [1024], F32); B_acc = Buf("acc")
                kvacc = AR.view(XOFF + 4 * MSGW + 1024, [384], F32); B_kvacc = Buf("kvacc")
                fct = sc[:, 8:24]; B_fct = Buf("fct")
                S.dma("sp", lambda e: e.dma_start(out=gath[:, 0:3, :],
                                                  in_=cout1[l][0:384, :].rearrange("(j p) n -> p j n", p=128)), "gath",
                      reads=[B_cout], writes=[B_gath])

                def exchange_compute():
                    S.op("dve", lambda e: e.memset(acc[:], 0.0), writes=[B_acc])
                    for jr in range(3):
                        mj = flags[:, 4 + jr:5 + jr]
                        S.op("dve", lambda e, jr=jr, mj=mj: e.tensor_scalar(out=fct[:, 0:4], in0=gath[:, jr, 1024:1028],
                                                                            scalar1=-1.0, scalar2=mj, op0=ALU.add,
                                                                            op1=ALU.mult), reads=[B_gath, B_const],
                             writes=[B_fct])
                        S.op("dve", lambda e: e.tensor_scalar(out=fct[:, 0:4], in0=fct[:, 0:4], scalar1=1.0, scalar2=None,
                                                              op0=ALU.add), reads=[B_fct], writes=[B_fct])
                        for h in range(4):
                            hs = slice(h * 256, (h + 1) * 256)
                            S.op("dve", lambda e, h=h, hs=hs: e.tensor_scalar(out=acc[:, hs], in0=acc[:, hs],
                                                                              scalar1=fct[:, h:h + 1], scalar2=None,
                                                                              op0=ALU.mult), reads=[B_fct, B_acc],
                                 writes=[B_acc])
                            S.op("dve", lambda e, jr=jr, hs=hs, mj=mj: e.scalar_tensor_tensor(out=acc[:, hs],
                                                                                              in0=gath[:, jr, hs], scalar=mj,
                                                                                              in1=acc[:, hs], op0=ALU.mult,
                                                                                              op1=ALU.add),
                                 reads=[B_gath, B_acc, B_const], writes=[B_acc])
                    S.op("act", lambda e: e.activation(out=Sinit_bf[:], in_=acc[:], func=AF.Copy), reads=[B_acc],
                         writes=[B_Sinit])
                    S.op("dve", lambda e: e.tensor_scalar(out=kvacc[:], in0=gath[:, 0, 1028:1412], scalar1=flags[:, 0:1],
                                                          scalar2=None, op0=ALU.mult), reads=[B_gath, B_const],
                         writes=[B_kvacc])
                    for jr in range(1, 3):
                        S.op("dve", lambda e, jr=jr: e.scalar_tensor_tensor(out=kvacc[:], in0=gath[:, jr, 1028:1412],
                                                                            scalar=flags[:, jr:jr + 1], in1=kvacc[:],
                                                                            op0=ALU.mult, op1=ALU.add),
                             reads=[B_gath, B_kvacc, B_const], writes=[B_kvacc])
                    S.op("act", lambda e: e.activation(out=kprev[:], in_=kvacc[:, 0:256].rearrange("p (a b) -> p a b", a=2),
                                                       func=AF.Copy), reads=[B_kvacc], writes=[B_kvprev])
                    S.op("act", lambda e: e.activation(out=vprev[:], in_=kvacc[:, 256:384], func=AF.Copy), reads=[B_kvacc],
                         writes=[B_kvprev])
                    S.op("dve", lambda e: e.memset(sc[:, 30:31], 0.0), reads=[B_Sinit, B_kvprev],
                         writes=[B_gath, B_acc, B_kvacc, B_pT[0][0], B_pT[0][1], B_pT[1][0], B_pT[1][1], B_ya, B_ofp, B_merged,
                                 B_sa, B_sbb, B_dn])

                if stop == ("X", l):
                    break
                def outproj_chunk(gp, j, st_):
                    gsp = slice(gp * G, (gp + 1) * G)
                    if j % 2 == 0:
                        jj = j // 2
                        st_["w"] = wload(wout_d[l].rearrange("(k p) n -> p k n", p=128)[:, :, jj * 256:(jj + 1) * 256],
                                         lambda t: v3(t, 8, 256))
                    wo, Bwo = st_["w"]
                    c = j % 2
                    ps, Bp = proj_chunk(wo, Bwo, lambda t, k, c=c: v3(t, 8, 256)[:, k, c * 128:(c + 1) * 128], merged,
                                        B_merged)
                    S.op("dve", lambda e, ps=ps, j=j, gsp=gsp: e.tensor_tensor(out=xT[:, j, gsp], in0=ps[:],
                                                                               in1=xT[:, j, gsp], op=ALU.add),
                         reads=[Bp, B_x[gp]], writes=[B_x[gp]])

                pend_out = None
                for g in range(NG):
                    gs = slice(g * G, (g + 1) * G)
                    if g == 0:
                        rms_stats(l, g, sqr, B_sqr, rstd, B_rstd)
                    rms_scale(l, g, 0, hg, B_hg, rstd, B_rstd)
                    wqs = [wload(win_cols(l, C_QA + hp_ * 256, 256), lambda t: v3(t, 8, 256)) for hp_ in range(2)]
                    wkd, Bwkd = kdup_load(l)
                    if g > 0:
                        S.op("dve", lambda e: e.tensor_copy(out=kdup[:, :, 0:128], in_=kdup[:, :, G:G + 128]),
                             reads=[B_kdup], writes=[B_kdup])
                        S.op("dve", lambda e: e.tensor_copy(out=vaug[:, 0, :, :], in_=vaug[:, 4, :, :]), reads=[B_vaug],
                             writes=[B_vaug])
                    units = []
                    for c in range(4):
                        units.append((wqs[c // 2], (c % 2) * 128, qn[:, c, :], B_qn, qg8[:, 0:1]))
                    for hk in range(2):
                        units.append(((wkd, Bwkd), hk * 128, kdup[:, hk, 128:128 + G], B_kdup, spt[:, l, 17:18]))
                    prev_u = None
                    for u_ in units + [None]:
                        cur_u = None
                        if u_ is not None:
                            (wt_, Bwt_), co_, out_, Bout_, gain_ = u_
                            ps_, Bp_ = proj_chunk(wt_, Bwt_, lambda t, k, co_=co_: v3(t, 8, 256)[:, k, co_:co_ + 128], hg, B_hg)
                            cur_u = (ps_, Bp_, out_, Bout_, gain_)
                        if prev_u is not None:
                            qknorm(prev_u[0], prev_u[1], sqk, B_sqk, rsk, B_rsk, prev_u[2], prev_u[3], prev_u[4], G)
                        prev_u = cur_u
                    wv, Bwv = wload(win_cols(l, C_VA, 128), lambda t: v3(t, 8, 128))
                    ps, Bp = psum()
                    for blk in range(4):
                        for k in range(8):
                            S.op("pe", lambda e, k=k, blk=blk, ps=ps, wv=wv: e.matmul(
                                ps[:, blk * 128:(blk + 1) * 128], lhsT=hg[:, k, blk * 128:(blk + 1) * 128],
                                rhs=v3(wv, 8, 128)[:, k, :], start=(k == 0), stop=(k == 7)), reads=[Bwv, _bk(B_hg, k)], writes=[Bp])
                    S.op("act", lambda e, ps=ps: e.activation(
                        out=vaug[:, 1:5, :, 0:64], in_=ps[:].rearrange("p (b h d) -> p b h d", b=4, h=2), func=AF.Copy),
                         reads=[Bp], writes=[B_vaug])
                    if g == 0:
                        exchange_compute()
                        S.op("dve", lambda e: e.tensor_copy(out=kdup[:, :, 0:128], in_=kprev[:]), reads=[B_kvprev],
                             writes=[B_kdup])
                        for hk in range(2):
                            S.op("dve", lambda e, hk=hk: e.tensor_copy(out=vaug[:, 0, hk, 0:64],
                                                                        in_=vprev[:, hk * 64:(hk + 1) * 64]),
                                 reads=[B_kvprev], writes=[B_vaug])
                    if stop == ("B1", l):
                        raise _Stop()
                    def attn_stage1(i, hk, par):
                        psE, BpE = psum()
                        psO, BpO = psum()
                        for kb in range(2):
                            koff = i * 128 + kb * 128
                            for b in range(4):
                                hh = HORD[b]
                                chunk = 2 * hk + hh // 2
                                p0 = 64 * (hh % 2)
                                pst_, Bpt_ = (psE, BpE) if b < 2 else (psO, BpO)
                                cb = kb * 256 + (b % 2) * 128
                                S.op("pe", lambda e, pst_=pst_, cb=cb, chunk=chunk, p0=p0, koff=koff, hk=hk, i=i: e.matmul(
                                    pst_[:, cb:cb + 128], lhsT=kdup[p0:p0 + 64, hk, koff:koff + 128],
                                    rhs=qn[p0:p0 + 64, chunk, i * 128:(i + 1) * 128], start=True, stop=True),
                                     reads=[B_kdup, B_qn], writes=[Bpt_])
                        S.op("act", lambda e, psE=psE, par=par: e.activation(
                            out=pT[:, par, :, 0:256], in_=psE[:].rearrange("p (k n) -> p k n", k=2), func=AF.Exp),
                             reads=[BpE], writes=[B_pT[par][0], B_pT[par][1]])
                        S.op("act", lambda e, psO=psO, par=par: e.activation(
                            out=pT[:, par, :, 256:512], in_=psO[:].rearrange("p (k n) -> p k n", k=2), func=AF.Exp),
                             reads=[BpO], writes=[B_pT[par][0], B_pT[par][1]])
                        for kb in range(2):
                            if kb == 1:
                                mk = maskc
                            else:
                                mk = maskp0 if (g == 0 and i == 0) else maskp
                            S.op("dve", lambda e, kb=kb, mk=mk, par=par: e.tensor_tensor(
                                out=pT[:, par, kb, :], in0=pT[:, par, kb, :], in1=mk[:], op=ALU.mult),
                                 reads=[B_pT[par][kb], B_const], writes=[B_pT[par][kb]])

                    dnb = [(dn, B_dn), (sbb[0:64, :], B_sbb)]

                    def attn_stage2a(i, hk, par):
                        pso, Bpo = psum()
                        psd, Bpd = psum()
                        dn_, Bdn_ = dnb[par]
                        for kb in range(2):
                            S.op("pe", lambda e, pso=pso, kb=kb, hk=hk, i=i, par=par: e.matmul(
                                pso[:, :], lhsT=vaug[:, i + kb, hk, :], rhs=pT[:, par, kb, :], start=(kb == 0),
                                stop=(kb == 1)), reads=[B_vaug, B_pT[par][kb]], writes=[Bpo])
                        for kb in range(2):
                            S.op("pe", lambda e, psd=psd, kb=kb, par=par: e.matmul(
                                psd[:, :], lhsT=ones128[:], rhs=pT[:, par, kb, :], start=(kb == 0), stop=(kb == 1)),
                                 reads=[B_const, B_pT[par][kb]], writes=[Bpd])
                        S.op("dve", lambda e, psd=psd, hk=hk, dn_=dn_: e.tensor_tensor(out=dn_, in0=psd[0:64, :],
                                                                                       in1=esink[:, hk, :], op=ALU.add),
                             reads=[Bpd, B_lay], writes=[Bdn_])
                        return pso, Bpo

                    def attn_stage2b(i, hk, par, pso, Bpo):
                        dn_, Bdn_ = dnb[par]
                        S.op("act", lambda e, dn_=dn_: e.activation(out=dn_, in_=dn_, func=AF.Ln), reads=[Bdn_],
                             writes=[Bdn_])
                        S.op("act", lambda e, dn_=dn_: e.activation(out=dn_, in_=dn_, func=AF.Exp, scale=-1.0),
                             reads=[Bdn_], writes=[Bdn_])
                        S.op("dve", lambda e, pso=pso, hk=hk, i=i, dn_=dn_: e.tensor_tensor(
                            out=ya[:, hk * 4:(hk + 1) * 4, i * 128:(i + 1) * 128],
                            in0=pso[0:64, :].rearrange("p (h q) -> p h q", h=4),
                            in1=dn_.rearrange("p (h q) -> p h q", h=4), op=ALU.mult),
                             reads=[Bpo, Bdn_], writes=[B_ya])

                    its = [(i, hk) for i in range(4) for hk in range(2)]
                    attn_stage1(its[0][0], its[0][1], 0)
                    pend = None
                    for n_, (i, hk) in enumerate(its):
                        if n_ + 1 < len(its):
                            attn_stage1(its[n_ + 1][0], its[n_ + 1][1], (n_ + 1) % 2)
                        pso_, Bpo_ = attn_stage2a(i, hk, n_ % 2)
                        if pend is not None:
                            attn_stage2b(*pend)
                        pend = (i, hk, n_ % 2, pso_, Bpo_)
                        if pend_out is not None:
                            outproj_chunk(pend_out[0], n_, pend_out[1])
                    attn_stage2b(*pend)
                    pend_out = None
                    S.dma("sp", lambda e: e.dma_start(
                        out=ya128[64:128, :, :].rearrange("p (m t) g -> p m t g", t=2)[:, :, 0, :],
                        in_=ya128[0:64, :, :].rearrange("p (m t) g -> p m t g", t=2)[:, :, 1, :]),
                          "yash", reads=[B_ya], writes=[B_ya])
                    if stop == ("B2", l):
                        raise _Stop()
                    def gla_p1(hp):
                        for h in (2 * hp, 2 * hp + 1):
                            F_ = GF[h % 2]
                            wr, Bwr = wload(win_cols(l, C_RB + h * 256, 256), lambda t: v3(t, 8, 256))
                            for c in range(2):
                                psc, Bpc = psum()
                                ec = h * 256 + c * 128
                                S.op("pe", lambda e, psc=psc, ec=ec, h=h: e.matmul(psc[:], lhsT=Sinit_bf[:, ec:ec + 128],
                                                                                   rhs=qhat[:, h, gs], start=True, stop=True),
                                     reads=[B_Sinit, B_qhat[g]], writes=[Bpc])
                                S.op("dve", lambda e, psc=psc, c=c, h=h, F_=F_: e.tensor_tensor(
                                    out=F_["ofp"][:, c, :], in0=psc[:], in1=oloc[:, h * 2 + c, gs], op=ALU.add),
                                     reads=[Bpc, B_oloc[g]], writes=F_["Bofp"])
                                S.op("act", lambda e, c=c, F_=F_: e.activation(out=F_["sq"][:, c, :], in_=F_["ofp"][:, c, :],
                                                                               func=AF.Square),
                                     reads=F_["Bofp"], writes=F_["Bsq"][c])
                            for c in range(2):
                                psr, Bpr = proj_chunk(wr, Bwr, lambda t, k, c=c: v3(t, 8, 256)[:, k, c * 128:(c + 1) * 128],
                                                      hg, B_hg)
                                S.op("act", lambda e, c=c, psr=psr, F_=F_: e.activation(out=F_["sr"][:, c, :], in_=psr[:],
                                                                                        func=AF.Silu),
                                     reads=[Bpr], writes=F_["Bsr"])

                    def gla_p2(hp):
                        for h in (2 * hp, 2 * hp + 1):
                            F_ = GF[h % 2]
                            pss, Bpss = psum()
                            for c in range(2):
                                S.op("pe", lambda e, c=c, pss=pss, F_=F_: e.matmul(pss[:], lhsT=ones_dv[:],
                                                                                   rhs=F_["sq"][:, c, :], start=(c == 0),
                                                                                   stop=(c == 1)),
                                     reads=F_["Bsq"][c] + [B_const], writes=[Bpss])
                            S.op("act", lambda e, pss=pss, F_=F_: e.activation(out=F_["rs"][:], in_=pss[:], func=AF.Ln,
                                                                               bias=epsf[:, 0:1], scale=1.0),
                                 reads=[Bpss, B_const], writes=F_["Brs"])
                            S.op("act", lambda e, F_=F_: e.activation(out=F_["rs"][:], in_=F_["rs"][:], func=AF.Exp,
                                                                      scale=-0.5), reads=F_["Brs"], writes=F_["Brs"])
                            for c in range(2):
                                S.op("dve", lambda e, c=c, F_=F_: e.scalar_tensor_tensor(
                                    out=F_["ofp"][:, c, :], in0=F_["ofp"][:, c, :], scalar=spt[:, l, 22 + c:23 + c],
                                    in1=F_["rs"][:], op0=ALU.mult, op1=ALU.mult),
                                     reads=F_["Bofp"] + F_["Brs"] + [B_const], writes=F_["Bofp"])
                                S.op("dve", lambda e, c=c, h=h, F_=F_: e.tensor_tensor(
                                    out=oloc[:, h * 2 + c, gs], in0=F_["ofp"][:, c, :], in1=F_["sr"][:, c, :], op=ALU.mult),
                                     reads=F_["Bofp"] + F_["Bsr"], writes=[B_oloc[g]])
                    def merge_load(st2):
                        c2 = slice(st2 * 256, (st2 + 1) * 256)
                        wA, BwA = wload_multi([
                            (wba_d[l][q_ * 256 + u_ * 128:q_ * 256 + u_ * 128 + 128, c2].rearrange("(r p) n -> p r n", p=64),
                             lambda t, q_=q_, u_=u_: t[u_ * 64:u_ * 64 + 64, 0:1024].rearrange("p (m n) -> p m n", m=4)[:, 2 * q_:2 * q_ + 2, :])
                            for q_ in range(2) for u_ in range(2)])
                        wGa, BwGa = wload(win_cols(l, C_GA + st2 * 256, 256), lambda t: v3(t, 8, 256))
                        wGb, BwGb = wload(win_cols(l, C_GB + st2 * 256, 256), lambda t: v3(t, 8, 256))
                        wB, BwB = wload(wbb_d[l].rearrange("(k p) n -> p k n", p=128)[:, :, c2], lambda t: v3(t, 8, 256))
                        return (wA, BwA, wGa, BwGa, wGb, BwGb, wB, BwB)

                    def merge_pre(st2, jj, W_):
                        if True:
                            (wA, BwA, wGa, BwGa, wGb, BwGb, wB, BwB) = W_
                            j = st2 * 2 + jj
                            cj = slice(jj * 128, (jj + 1) * 128)
                            psA, BpA = psum()
                            for m_ in range(4):
                                S.op("pe", lambda e, psA=psA, m_=m_, wA=wA, cj=cj: e.matmul(
                                    psA[:], lhsT=wA[:, 0:1024].rearrange("p (m n) -> p m n", m=4)[:, m_, cj],
                                    rhs=ya128[:, 2 * m_, :], start=(m_ == 0), stop=(m_ == 3)),
                                     reads=[BwA, B_ya], writes=[BpA])
                            psGa, BpGa = proj_chunk(wGa, BwGa, lambda t, k, cj=cj: v3(t, 8, 256)[:, k, cj], hg, B_hg)
                            psGb, BpGb = proj_chunk(wGb, BwGb, lambda t, k, cj=cj: v3(t, 8, 256)[:, k, cj], hg, B_hg)
                            return (psA, BpA, psGa, BpGa, psGb, BpGb)

                    def merge_post(st2, jj, W_, P_):
                        if True:
                            (wA, BwA, wGa, BwGa, wGb, BwGb, wB, BwB) = W_
                            (psA, BpA, psGa, BpGa, psGb, BpGb) = P_
                            j = st2 * 2 + jj
                            cj = slice(jj * 128, (jj + 1) * 128)
                            S.op("act", lambda e, psGa=psGa: e.activation(out=sa[:], in_=psGa[:], func=AF.Sigmoid),
                                 reads=[BpGa], writes=[B_sa])
                            S.op("act", lambda e, psGb=psGb: e.activation(out=sbb[:], in_=psGb[:], func=AF.Sigmoid),
                                 reads=[BpGb], writes=[B_sbb])
                            S.op("dve", lambda e, psA=psA: e.tensor_tensor(out=sa[:], in0=psA[:], in1=sa[:], op=ALU.mult),
                                 reads=[BpA, B_sa], writes=[B_sa])
                            psB, BpB = psum()
                            for k in range(8):
                                S.op("pe", lambda e, psB=psB, k=k, wB=wB, cj=cj: e.matmul(
                                    psB[:], lhsT=v3(wB, 8, 256)[:, k, cj], rhs=oloc[:, k, gs], start=(k == 0), stop=(k == 7)),
                                     reads=[BwB, B_oloc[g]], writes=[BpB])
                            S.op("dve", lambda e, psB=psB: e.tensor_tensor(out=sbb[:], in0=psB[:], in1=sbb[:], op=ALU.mult),
                                 reads=[BpB, B_sbb], writes=[B_sbb])
                            S.op("dve", lambda e, j=j: e.tensor_tensor(out=merged[:, j, :], in0=sa[:], in1=sbb[:], op=ALU.add),
                                 reads=[B_sa, B_sbb], writes=[B_merged])


                    gla_p1(0)
                    gla_p2(0)
                    gla_p1(1)
                    W0_ = merge_load(0)
                    P0_ = merge_pre(0, 0, W0_)
                    P1_ = merge_pre(0, 1, W0_)
                    gla_p2(1)
                    merge_post(0, 0, W0_, P0_)
                    merge_post(0, 1, W0_, P1_)
                    if g + 1 < NG:
                        rms_stats(l, g + 1, sqr, B_sqr, rstd, B_rstd)
                    for st2 in range(1, 4):
                        W_ = merge_load(st2)
                        for jj in range(2):
                            merge_post(st2, jj, W_, merge_pre(st2, jj, W_))
                    if stop == ("B3", l):
                        raise _Stop()
                    if stop == ("B4", l):
                        raise _Stop()
                    if g == NG - 1:
                        st_ = {}
                        for j in range(8):
                            outproj_chunk(g, j, st_)
                    else:
                        pend_out = (g, {})
                if stop == ("mix", l):
                    break
                S.barrier()
                AR.reset()
                hgs_ = [(AR.get([8, G], BF16), [Buf("hgC0_%d" % k_) for k_ in range(8)]), (AR.get([8, G], BF16), [Buf("hgC1_%d" % k_) for k_ in range(8)])]
                sqr = AR.get([2, G], BF16); B_sqr = [Buf("sq0C"), Buf("sq1C")]
                rstd = AR.get([G], F32); B_rstd = Buf("rstdC")
                gt = AR.get([NJ, G], BF16); B_gt = Buf("gt")
                aext = AR.get([2, G + 2], F32); B_aext = [Buf("aext0"), Buf("aext1")]
                yv = AR.get([2, G], F32); B_yv = [Buf("yv0"), Buf("yv1")]
                msg2 = AR.get([16], F32); B_msg2 = Buf("msg2")
                gath2 = AR.get([4, 16], F32); B_gath2 = Buf("gath2")
                xh = AR.get([8, 2], F32); B_xh = Buf("xh")
                sq2 = AR.get([8, 2], BF16); B_sq2 = Buf("sq2")
                rs2 = AR.get([2], F32); B_rs2 = Buf("rs2")
                h2h1 = AR.get([8, 2], BF16); B_h2h1 = Buf("h2h1")
                B_cin2 = Buf("cin2"); B_cout2 = Buf("cout2")

                def halo_norm(src3, B_src, dst, B_dst):
                    S.op("act", lambda e: e.activation(out=sq2[:], in_=src3, func=AF.Square), reads=B_src, writes=[B_sq2])
                    ps, Bp = psum()
                    for k in range(8):
                        S.op("pe", lambda e, k=k, ps=ps: e.matmul(ps[:, 0:2], lhsT=ones_dm[:], rhs=sq2[:, k, :],
                                                                  start=(k == 0), stop=(k == 7)),
                             reads=[B_sq2, B_const], writes=[Bp])
                    rsqrt_eps(rs2[:], ps[:, 0:2], [Bp], B_rs2)
                    for k in range(8):
                        S.op("dve", lambda e, k=k: e.scalar_tensor_tensor(out=dst[:, k, :], in0=src3[:, k, :],
                                                                          scalar=spt[:, l, 8 + k:9 + k], in1=rs2[:],
                                                                          op0=ALU.mult, op1=ALU.mult),
                             reads=B_src + [B_rs2, B_const], writes=[B_dst])

                S.op("act", lambda e: e.activation(out=msg2[:].rearrange("p (k t) -> p k t", k=8),
                                                   in_=xT[:, :, TOK - 2:TOK], func=AF.Copy), reads=[B_x[NG - 1]],
                     writes=[B_msg2])
                S.dma("sp", lambda e: e.dma_start(out=cin2[l], in_=msg2[:]), "cin", reads=[B_msg2], writes=[B_cin2])
                S.cc(lambda e: e.collective_compute("AllGather", ALU.bypass, replica_groups=RG, ins=[cin2[l].opt()],
                                                    outs=[cout2[l].opt()]), "cc", reads=[B_cin2], writes=[B_cout2])
                S.dma("sp", lambda e: e.dma_start(out=gath2[:], in_=cout2[l].rearrange("(j p) n -> p j n", p=128)), "gath",
                      reads=[B_cout2], writes=[B_gath2])
                halo_norm(xT[:, :, G - 2:G], [B_x[0]], h2h1, B_h2h1)
                order_ = [1, 2, 3, 0]
                rmsnorm_group(l, order_[0], 8, hgs_[0][0], hgs_[0][1], sqr, B_sqr, rstd, B_rstd)
                for p_, g in enumerate(order_):
                    gs = slice(g * G, (g + 1) * G)
                    if g == 0:
                        xhf = xh[:].rearrange("p k t -> p (k t)")
                        S.op("dve", lambda e: e.tensor_scalar(out=xhf, in0=gath2[:, 0, :], scalar1=flags[:, 0:1],
                                                              scalar2=None, op0=ALU.mult), reads=[B_gath2, B_const],
                             writes=[B_xh])
                        for jr in range(1, 4):
                            S.op("dve", lambda e, jr=jr: e.scalar_tensor_tensor(out=xhf, in0=gath2[:, jr, :],
                                                                                scalar=flags[:, jr:jr + 1], in1=xhf,
                                                                                op0=ALU.mult, op1=ALU.add),
                                 reads=[B_gath2, B_xh, B_const], writes=[B_xh])
                        halo_norm(xh[:], [B_xh], h2halo, B_h2halo)
                    hg, B_hg = hgs_[p_ % 2]
                    for jp in range(NJ // 2):
                        wfa, Bwfa = wload(wfi_d[l].rearrange("(k p) n -> p k n", p=128)[:, :, jp * 256:(jp + 1) * 256],
                                          lambda t: v3(t, 8, 256))
                        wfu, Bwfu = wload(
                            wfi_d[l].rearrange("(k p) n -> p k n", p=128)[:, :, D_FF + jp * 256:D_FF + (jp + 1) * 256],
                            lambda t: v3(t, 8, 256))
                        for jj in range(2):
                            j = jp * 2 + jj
                            r = j % 2
                            cw = 46 + j * 3
                            psa, Bpa = proj_chunk(wfa, Bwfa, lambda t, k, jj=jj: v3(t, 8, 256)[:, k, jj * 128:(jj + 1) * 128],
                                                  hg, B_hg)
                            psu, Bpu = proj_chunk(wfu, Bwfu, lambda t, k, jj=jj: v3(t, 8, 256)[:, k, jj * 128:(jj + 1) * 128],
                                                  hg, B_hg)
                            if g in (0, 1):
                                hsrc, B_hsrc = (h2halo, B_h2halo) if g == 0 else (h2h1, B_h2h1)
                                psh, Bph = psum()
                                for k in range(8):
                                    S.op("pe", lambda e, k=k, psh=psh, jj=jj, wfa=wfa, hsrc=hsrc: e.matmul(
                                        psh[:, 0:2], lhsT=v3(wfa, 8, 256)[:, k, jj * 128:(jj + 1) * 128], rhs=hsrc[:, k, :],
                                        start=(k == 0), stop=(k == 7)), reads=[Bwfa, B_hsrc], writes=[Bph])
                                S.op("act", lambda e, psh=psh, r=r: e.activation(out=aext[:, r, 0:2], in_=psh[:, 0:2],
                                                                                 func=AF.Copy), reads=[Bph],
                                     writes=[B_aext[r]])
                            else:
                                S.op("act", lambda e, r=r, j=j: e.activation(out=aext[:, r, 0:2], in_=ahalo[:, j, :], func=AF.Copy),
                                     reads=[B_ahalo], writes=[B_aext[r]])
                            S.op("act", lambda e, psa=psa, r=r: e.activation(out=aext[:, r, 2:G + 2], in_=psa[:],
                                                                             func=AF.Copy), reads=[Bpa], writes=[B_aext[r]])
                            S.op("act", lambda e, r=r, j=j: e.activation(out=ahalo[:, j, :], in_=aext[:, r, G:G + 2], func=AF.Copy),
                                 reads=[B_aext[r]], writes=[B_ahalo])
                            S.op("dve", lambda e, r=r, cw=cw, j=j: e.tensor_scalar(
                                out=yv[:, r, :], in0=aext[:, r, 2:G + 2], scalar1=spt[:, l, cw + 2:cw + 3],
                                scalar2=spt[:, l, 24 + j:25 + j], op0=ALU.mult, op1=ALU.add),
                                 reads=[B_aext[r], B_const], writes=[B_yv[r]])
                            S.op("dve", lambda e, r=r, cw=cw: e.scalar_tensor_tensor(
                                out=yv[:, r, :], in0=aext[:, r, 1:G + 1], scalar=spt[:, l, cw + 1:cw + 2], in1=yv[:, r, :],
                                op0=ALU.mult, op1=ALU.add), reads=[B_aext[r], B_yv[r], B_const], writes=[B_yv[r]])
                            S.op("dve", lambda e, r=r, cw=cw: e.scalar_tensor_tensor(
                                out=yv[:, r, :], in0=aext[:, r, 0:G], scalar=spt[:, l, cw:cw + 1], in1=yv[:, r, :],
                                op0=ALU.mult, op1=ALU.add), reads=[B_aext[r], B_yv[r], B_const], writes=[B_yv[r]])
                            S.op("act", lambda e, r=r: e.activation(out=yv[:, r, :], in_=yv[:, r, :], func=AF.Silu),
                                 reads=[B_yv[r]], writes=[B_yv[r]])
                            S.op("dve", lambda e, r=r, j=j, psu=psu: e.tensor_tensor(out=gt[:, j, :], in0=psu[:],
                                                                                     in1=yv[:, r, :], op=ALU.mult),
                                 reads=[Bpu, B_yv[r]], writes=[B_gt])
                    for jp2 in range(4):
                        pss2 = [psum(), psum()]
                        for part in range(3):
                            k0 = part * 8
                            nk = min(8, NJ - k0)
                            wo, Bwo = wload(
                                wfo_d[l].rearrange("(c p) n -> p c n", p=128)[:, k0:k0 + nk, jp2 * 256:(jp2 + 1) * 256],
                                lambda t, nk=nk: v3(t, nk, 256))
                            for jj in range(2):
                                ps, Bp = pss2[jj]
                                for kk in range(nk):
                                    c = k0 + kk
                                    S.op("pe", lambda e, ps=ps, kk=kk, c=c, jj=jj, wo=wo, nk=nk: e.matmul(
                                        ps[:], lhsT=v3(wo, nk, 256)[:, kk, jj * 128:(jj + 1) * 128], rhs=gt[:, c, :],
                                        start=(c == 0), stop=(c == NJ - 1)), reads=[Bwo, B_gt], writes=[Bp])
                        for jj in range(2):
                            ps, Bp = pss2[jj]
                            jo = jp2 * 2 + jj
                            S.op("dve", lambda e, ps=ps, jo=jo: e.tensor_tensor(out=xT[:, jo, gs], in0=ps[:], in1=xT[:, jo, gs],
                                                                                op=ALU.add),
                                 reads=[Bp, B_x[g]], writes=[B_x[g]])
                        if l == DEPTH - 1 and stop is None:
                            if p_ == NG - 1:
                                jo0 = jp2 * 2
                                S.dma("sp", lambda e, jo0=jo0, gs=gs: e.dma_start(out=y_d[:, jo0:jo0 + 2, gs],
                                                                                 in_=xT[:, jo0:jo0 + 2, gs]),
                                      "out", reads=[B_x[g]])
                            elif jp2 == 3:
                                S.dma("sp", lambda e, gs=gs: e.dma_start(out=y_d[:, :, gs], in_=xT[:, :, gs]), "out",
                                      reads=[B_x[g]])
                            out_done[0] = True
                        if jp2 == 0 and p_ + 1 < NG:
                            rmsnorm_group(l, order_[p_ + 1], 8, hgs_[(p_ + 1) % 2][0], hgs_[(p_ + 1) % 2][1], sqr, B_sqr,
                                          rstd, B_rstd)
                if stop == ("layer", l):
                    break
        except _Stop:
            pass

        for g in ((1, 2, 3, 0) if not out_done[0] else ()):
            S.dma("sp", lambda e, g=g: e.dma_start(out=y_d[:, :, g * G:(g + 1) * G], in_=xT[:, :, g * G:(g + 1) * G]),
                  "out", reads=[B_x[g]])
        S.final_wait("sp")
        S.emit()
    return nc


def _layout_inputs(inputs):
    f32 = np.float32
    x = np.asarray(inputs["x"], f32)
    L = DEPTH
    sp = np.zeros((L, 128, NSP), f32)
    p = np.arange(128)
    for l in range(L):
        sp[l, :, 0:8] = np.asarray(inputs["ln_mix_g"][l], f32).reshape(8, 128).T
        sp[l, :, 8:16] = np.asarray(inputs["ln_ffn_g"][l], f32).reshape(8, 128).T
        sp[l, :, 16] = np.asarray(inputs["q_norm_g"][l], f32)[p % 64]
        sp[l, :, 17] = np.asarray(inputs["k_norm_g"][l], f32)[p % 64]
        sp[l, :, 18:22] = np.asarray(inputs["b_gk"][l], f32).reshape(4, 128).T
        sp[l, :, 22:24] = np.asarray(inputs["gla_norm_g"][l], f32).reshape(2, 128).T
        sp[l, :, 24:46] = np.asarray(inputs["conv_b"][l], f32).reshape(NJ, 128).T
        cw = np.asarray(inputs["conv_w"][l], f32)
        sp[l, :, 46:112] = cw.reshape(3, NJ, 128).transpose(2, 1, 0).reshape(128, NJ * 3)
        sp[l, :, 112:120] = np.asarray(inputs["sinks"][l], f32)[None, :]
    shared = {
        "sp": sp,
        "w_gk_up": np.ascontiguousarray(np.asarray(inputs["w_gk_up"], f32)),
        "w_in": np.ascontiguousarray(np.asarray(inputs["w_in"], f32)),
        "w_branch_a": np.ascontiguousarray(np.asarray(inputs["w_branch_a"], f32)),
        "w_branch_b": np.ascontiguousarray(np.asarray(inputs["w_branch_b"], f32)),
        "w_out": np.ascontiguousarray(np.asarray(inputs["w_out"], f32)),
        "w_ffn_in": np.ascontiguousarray(np.asarray(inputs["w_ffn_in"], f32)),
        "w_ffn_out": np.ascontiguousarray(np.asarray(inputs["w_ffn_out"], f32)),
    }
    in_maps = []
    for c in range(8):
        b, r = c // 4, c % 4
        xs = x[b, r * TOK:(r + 1) * TOK, :]
        xt = np.ascontiguousarray(xs.T.reshape(8, 128, TOK).transpose(1, 0, 2))
        fl = np.zeros((128, 8), f32)
        if r > 0:
            fl[:, r - 1] = 1.0
            fl[:, 7] = 1.0
        for j in range(3):
            fl[:, 4 + j] = 1.0 if j < r else 0.0
        m = dict(shared)
        m["xT"] = xt
        m["flags"] = fl
        in_maps.append(m)
    return in_maps


def _gather_out(results):
    out = np.zeros((2, 4 * TOK, D), np.float32)
    for c in range(8):
        b, r = c // 4, c % 4
        yt = np.asarray(results[c]["yT"])
        out[b, r * TOK:(r + 1) * TOK, :] = yt.transpose(2, 1, 0).reshape(TOK, D)
    return out


_NC_CACHE = {}


def kernel(**inputs):
    if "nc" not in _NC_CACHE:
        _NC_CACHE["nc"] = build()
    nc = _NC_CACHE["nc"]
    in_maps = _layout_inputs(inputs)
    res = run_bass_kernel_spmd(nc, in_maps, core_ids=list(range(8)))
    return _gather_out(res.results)
```

```python
import numpy as np
from contextlib import ExitStack
import concourse.bass as bass
import concourse.mybir as mybir
from concourse.bass_utils import run_bass_kernel_spmd

F32 = mybir.dt.float32
BF16 = mybir.dt.bfloat16
I32 = mybir.dt.int32
AF = mybir.ActivationFunctionType
ALU = mybir.AluOpType

D = 1024
TOK = 2048
G = 512
NG = TOK // G
DEPTH = 2
IN_W = 5904
D_FF = 2816
NJ = D_FF // 128
EPS = 1e-6
C_QA, C_KA, C_VA, C_QB, C_KB, C_VB, C_RB, C_GK, C_GA, C_GB = 0, 512, 640, 768, 1280, 1792, 2816, 3840, 3856, 4880
NSP = 120
HORD = (0, 2, 1, 3)
MSGW = 1024 + 4 + 256 + 128
EPOCH = 12000


import types


def _snap(fn):
    if fn is None or fn.__closure__ is None:
        return fn
    cells = tuple(types.CellType(c.cell_contents) for c in fn.__closure__)
    return types.FunctionType(fn.__code__, fn.__globals__, fn.__name__, fn.__defaults__, cells)


class _Stop(Exception):
    pass


def _bk(B, k):
    return B[k] if isinstance(B, list) else B


class Buf:
    __slots__ = ("name", "w", "r", "excl")

    def __init__(self, name, excl=False):
        self.name, self.w, self.r, self.excl = name, None, [], excl


class Sched:
    ENG = ("pe", "act", "dve", "pool", "sp")

    def __init__(self, nc, es):
        self.nc, self.es = nc, es
        self.prog = {e: [] for e in self.ENG}
        self.cnt = {e: 0 for e in self.ENG}
        self.nsem = 0
        self.sems = {e: self._newsem() for e in self.ENG}
        self.waited = {e: {} for e in self.ENG}
        self.dsem = {}
        self.last_tok = {e: None for e in self.ENG}
        self.dma_toks = {}

    def _newsem(self):
        self.nsem += 1
        return self.es.enter_context(self.nc.semaphore("s%d" % self.nsem))

    def _waits(self, eng, deps):
        waits = []
        for (sem, val, src) in deps:
            if src == eng == "pe":
                continue
            key = id(sem)
            if self.waited[eng].get(key, 0) >= val:
                continue
            self.waited[eng][key] = val
            waits.append((sem, val))
        return waits

    def _deps(self, reads, writes):
        deps = []
        for b in reads:
            if b.excl:
                deps += b.r
            if b.w is not None:
                deps.append(b.w)
        for b in writes:
            deps += b.r
            if b.w is not None:
                deps.append(b.w)
        return deps

    def _commit(self, tok, reads, writes):
        for b in reads:
            if b.excl:
                b.w, b.r = tok, []
            else:
                b.r.append(tok)
        for b in writes:
            b.w, b.r = tok, []

    def op(self, eng, fn, reads=(), writes=()):
        waits = self._waits(eng, self._deps(reads, writes))
        if self.cnt[eng] >= EPOCH:
            self.sems[eng] = self._newsem()
            self.cnt[eng] = 0
        self.cnt[eng] += 1
        sem = self.sems[eng]
        tok = (sem, self.cnt[eng], eng)
        self.prog[eng].append((waits, _snap(fn), sem, 1))
        self.last_tok[eng] = tok
        self._commit(tok, reads, writes)

    def dma(self, eng, fn, semname, reads=(), writes=()):
        waits = self._waits(eng, self._deps(reads, writes))
        if semname not in self.dsem:
            self.dsem[semname] = [self._newsem(), 0]
        s = self.dsem[semname]
        s[1] += 16
        tok = (s[0], s[1], "dma")
        self.prog[eng].append((waits, _snap(fn), s[0], 16))
        self.dma_toks[semname] = tok
        self._commit(tok, reads, writes)

    def cc(self, fn, semname, reads=(), writes=()):
        waits = self._waits("pool", self._deps(reads, writes))
        if semname not in self.dsem:
            self.dsem[semname] = [self._newsem(), 0]
        s = self.dsem[semname]
        s[1] += 1
        tok = (s[0], s[1], "dma")
        self.prog["pool"].append((waits, _snap(fn), s[0], None))
        self.dma_toks[semname] = tok
        self._commit(tok, reads, writes)

    def barrier(self, skip=()):
        toks = [t for t in self.last_tok.values() if t is not None] + \
               [t for n_, t in self.dma_toks.items() if n_ not in skip]
        for e in self.ENG:
            w = self._waits(e, [t for t in toks if t[2] != e])
            if w:
                self.prog[e].append((w, None, None, 0))

    def final_wait(self, eng):
        toks = [t for t in self.last_tok.values() if t is not None] + list(self.dma_toks.values())
        w = self._waits(eng, [t for t in toks if t[2] != eng])
        self.prog[eng].append((w, None, None, 0))

    def emit(self):
        nc = self.nc
        prog = self.prog

        def replay(e, lst):
            for waits, fn, sem, inc in lst:
                for (s, v) in waits:
                    e.wait_ge(s, v)
                if fn is None:
                    continue
                ins = fn(e)
                if inc is None:
                    ins.then_inc(sem)
                else:
                    ins.then_inc(sem, inc)

        with nc.Block() as block:
            @block.tensor
            def _(e):
                replay(e, prog["pe"])

            @block.scalar
            def _(e):
                replay(e, prog["act"])

            @block.vector
            def _(e):
                replay(e, prog["dve"])

            @block.gpsimd
            def _(e):
                replay(e, prog["pool"])

            @block.sync
            def _(e):
                replay(e, prog["sp"])


class Arena:
    def __init__(self, ap_f32, nwords):
        self.base, self.n, self.off = ap_f32, nwords, 0

    def reset(self):
        self.off = 0

    def view(self, off, shape, dtype, parts=128):
        save = self.off
        self.off = off
        ap = self.get(shape, dtype, parts)
        self.off = save
        return ap

    def get(self, shape, dtype, parts=128):
        n = int(np.prod(shape))
        words = n if dtype in (F32, I32) else (n + 1) // 2
        assert self.off + words <= self.n, ("arena overflow", self.off, words, self.n)
        ap = self.base[0:parts, self.off:self.off + words]
        self.off += words
        if dtype != F32:
            ap = ap.bitcast(dtype)
            if dtype == BF16 and n % 2:
                ap = ap[:, 0:n]
        if len(shape) == 2:
            ap = ap.rearrange("p (a b) -> p a b", a=shape[0])
        elif len(shape) == 3:
            ap = ap.rearrange("p (a b c) -> p a b c", a=shape[0], b=shape[1])
        return ap


def build(stop=None):
    nc = bass.Bass("TRN2", target_bir_lowering=False)
    dt = nc.dram_tensor
    x_d = dt("xT", [128, 8, TOK], F32, kind="ExternalInput").ap()
    flags_d = dt("flags", [128, 8], F32, kind="ExternalInput").ap()
    sp_d = dt("sp", [DEPTH, 128, NSP], F32, kind="ExternalInput").ap()
    wup_d = dt("w_gk_up", [DEPTH, 16, 512], F32, kind="ExternalInput").ap()
    win_d = dt("w_in", [DEPTH, D, IN_W], F32, kind="ExternalInput").ap()
    wba_d = dt("w_branch_a", [DEPTH, 512, D], F32, kind="ExternalInput").ap()
    wbb_d = dt("w_branch_b", [DEPTH, D, D], F32, kind="ExternalInput").ap()
    wout_d = dt("w_out", [DEPTH, D, D], F32, kind="ExternalInput").ap()
    wfi_d = dt("w_ffn_in", [DEPTH, D, 2 * D_FF], F32, kind="ExternalInput").ap()
    wfo_d = dt("w_ffn_out", [DEPTH, D_FF, D], F32, kind="ExternalInput").ap()
    y_d = dt("yT", [128, 8, TOK], F32, kind="ExternalOutput").ap()
    cin1 = [dt("cin1_%d" % l, [128, MSGW], F32, kind="Internal").ap() for l in range(DEPTH)]
    cout1 = [dt("cout1_%d" % l, [512, MSGW], F32, kind="Internal").ap() for l in range(DEPTH)]
    cin2 = [dt("cin2_%d" % l, [128, 16], F32, kind="Internal").ap() for l in range(DEPTH)]
    cout2 = [dt("cout2_%d" % l, [512, 16], F32, kind="Internal").ap() for l in range(DEPTH)]
    RG = [[0, 1, 2, 3], [4, 5, 6, 7]]

    with ExitStack() as es:
        S = Sched(nc, es)

        def sb(name, shape, dtype):
            return es.enter_context(nc.sbuf_tensor(name, shape, dtype))

        xT = sb("xTs", [128, 8, TOK], F32)
        oloc = sb("oloc", [128, 8, TOK], BF16)
        qhat = sb("qhat", [128, 4, TOK], BF16)
        B_x = [Buf("x%d" % g) for g in range(NG)]
        B_oloc = [Buf("oloc%d" % g) for g in range(NG)]
        B_qhat = [Buf("qhat%d" % g) for g in range(NG)]
        NSLOT = 6
        wsl = [sb("wslot%d" % i, [128, 2048], BF16) for i in range(NSLOT)]
        B_w = [Buf("w%d" % i) for i in range(NSLOT)]
        ident = sb("ident", [128, 128], BF16)
        ones_dm = sb("ones_dm", [128, 128], BF16)
        ones_dv = sb("ones_dv", [128, 128], BF16)
        bd64 = sb("bd64", [128, 128], BF16)
        ones64 = sb("ones64", [128, 64], BF16)
        ones128 = sb("ones128", [128, 128], BF16)
        maskc = sb("maskc", [128, 512], BF16)
        maskp = sb("maskp", [128, 512], BF16)
        maskp0 = sb("maskp0", [128, 512], BF16)
        onesf = sb("onesf", [128, 1], F32)
        zerof = sb("zerof", [128, 1], F32)
        epsf = sb("epsf", [128, 1], F32)
        flags = sb("flags_s", [128, 8], F32)
        spt = sb("spt", [128, DEPTH, NSP], F32)
        wup = sb("wup", [16, DEPTH, 512], BF16)
        nbgk = sb("nbgk", [128, 4], F32)
        qg8 = sb("qg8", [128, 1], F32)
        es8 = sb("es8", [128, 8], F32)
        esink = sb("esink", [64, 2, 512], F32)
        Sinit_bf = sb("Sinit_bf", [128, 1024], BF16)
        kprev = sb("kprev", [128, 2, 128], BF16)
        vprev = sb("vprev", [128, 128], BF16)
        Sst = sb("Sst", [128, 1024], F32)
        Cend = sb("Cend", [128, 4], F32)
        sc = sb("sc", [128, 64], F32)
        ahalo = sb("ahalo", [128, NJ, 2], F32)
        h2halo = sb("h2halo", [128, 8, 2], BF16)
        B_const = Buf("const")
        B_lay = Buf("layerconst")
        B_Sinit = Buf("Sinit")
        B_kvprev = Buf("kvprev")
        B_S = Buf("S")
        B_Cend = Buf("Cend")
        B_ahalo = Buf("ahalo")
        B_h2halo = Buf("h2halo")
        ARW = 13312
        arena_t = sb("arena", [128, ARW], F32)
        AR = Arena(arena_t, ARW)
        pst = [es.enter_context(nc.psum_tensor("ps%d" % i, [128, 512], F32)) for i in range(8)]
        B_ps = [Buf("ps%d" % i, excl=True) for i in range(8)]
        psrr = [0]

        def psum():
            i = psrr[0] % 8
            psrr[0] += 1
            return pst[i], B_ps[i]

        slot_rr = [0]

        def wload(src_ap, view_fn, parts=128):
            i = slot_rr[0] % NSLOT
            slot_rr[0] += 1
            dst = view_fn(wsl[i])
            S.dma("pool", lambda e, d=dst, s=src_ap: e.dma_start(out=d, in_=s), "w%d" % i, writes=[B_w[i]])
            return wsl[i], B_w[i]

        def wload_multi(pairs):
            i = slot_rr[0] % NSLOT
            slot_rr[0] += 1
            for (src_ap, view_fn) in pairs:
                dst = view_fn(wsl[i])
                S.dma("pool", lambda e, d=dst, s=src_ap: e.dma_start(out=d, in_=s), "w%d" % i, writes=[B_w[i]])
            return wsl[i], B_w[i]

        def v3(t, k, n):
            return t[:, 0:k * n].rearrange("p (k n) -> p k n", k=k)

        def win_cols(l, c0, n):
            return win_d[l].rearrange("(k p) n -> p k n", p=128)[:, :, c0:c0 + n]

        iot_i = AR.get([512], F32).bitcast(I32)
        iot_f = AR.get([512], F32)
        S.op("pool", lambda e: e.iota(iot_i[:], pattern=[[0, 4], [1, 128]], base=0, channel_multiplier=-1),
             writes=[B_const])
        S.op("dve", lambda e: e.tensor_copy(out=iot_f[:], in_=iot_i[:]), reads=[B_const], writes=[B_const])
        S.op("dve", lambda e: e.tensor_single_scalar(out=maskc[:], in_=iot_f[:], scalar=0.0, op=ALU.is_ge),
             writes=[B_const])
        S.op("dve", lambda e: e.tensor_single_scalar(out=maskp[:], in_=iot_f[:], scalar=0.0, op=ALU.is_lt),
             writes=[B_const])
        S.op("dve", lambda e: e.tensor_single_scalar(out=ident[:], in_=iot_f[:, 0:128], scalar=0.0, op=ALU.is_equal),
             writes=[B_const])
        S.op("dve", lambda e: e.memset(ones_dm[:], 1.0 / 1024.0), writes=[B_const])
        S.op("dve", lambda e: e.memset(ones_dv[:], 1.0 / 256.0), writes=[B_const])
        S.op("dve", lambda e: e.memset(bd64[:], 0.0), writes=[B_const])
        S.op("dve", lambda e: e.memset(bd64[0:64, 0:64], 1.0 / 64.0), writes=[B_const])
        S.op("dve", lambda e: e.memset(bd64[64:128, 64:128], 1.0 / 64.0), writes=[B_const])
        S.op("dve", lambda e: e.memset(ones64[:], 1.0), writes=[B_const])
        S.op("dve", lambda e: e.memset(ones128[:], 1.0), writes=[B_const])
        S.op("dve", lambda e: e.memset(onesf[:], 1.0), writes=[B_const])
        S.op("dve", lambda e: e.memset(zerof[:], 0.0), writes=[B_const])
        S.op("dve", lambda e: e.memset(epsf[:], EPS), writes=[B_const])

        def rsqrt_eps(out_ap, in_ap, reads, B_out):
            S.op("act", lambda e: e.activation(out=out_ap, in_=in_ap, func=AF.Ln, bias=epsf[0:out_ap.shape[0], 0:1],
                                               scale=1.0), reads=list(reads) + [B_const], writes=[B_out])
            S.op("act", lambda e: e.activation(out=out_ap, in_=out_ap, func=AF.Exp, scale=-0.5), reads=[B_out],
                 writes=[B_out])
        S.dma("sp", lambda e: e.dma_start(out=flags[:], in_=flags_d), "misc", writes=[B_const])
        S.dma("sp", lambda e: e.dma_start(out=spt[:], in_=sp_d.rearrange("l p n -> p l n")), "misc", writes=[B_const])
        S.dma("pool", lambda e: e.dma_start(out=wup[:], in_=wup_d.rearrange("l r n -> r l n")), "misc2",
              writes=[B_const])
        for g in range(NG):
            S.dma("sp", lambda e, g=g: e.dma_start(out=xT[:, :, g * G:(g + 1) * G], in_=x_d[:, :, g * G:(g + 1) * G]),
                  "xin%d" % g, writes=[B_x[g]])
        S.op("dve", lambda e: e.tensor_scalar(out=maskp0[:], in0=maskp[:], scalar1=flags[:, 7:8], scalar2=None,
                                              op0=ALU.mult), reads=[B_const], writes=[B_const])

        def rms_stats(l, g, sqr, B_sqr, rstd, B_rstd):
            ps, Bp = psum()
            for k in range(8):
                r = k % 2
                S.op("act", lambda e, k=k, r=r: e.activation(out=sqr[:, r, :], in_=xT[:, k, g * G:(g + 1) * G],
                                                             func=AF.Square),
                     reads=[B_x[g]], writes=[B_sqr[r]])
                S.op("pe", lambda e, k=k, r=r: e.matmul(ps[:], lhsT=ones_dm[:], rhs=sqr[:, r, :], start=(k == 0),
                                                        stop=(k == 7)),
                     reads=[B_sqr[r], B_const], writes=[Bp])
            rsqrt_eps(rstd[:], ps[:], [Bp], B_rstd)

        def rms_scale(l, g, gcol, hg, B_hg, rstd, B_rstd):
            for k in range(8):
                S.op("dve", lambda e, k=k: e.scalar_tensor_tensor(out=hg[:, k, :], in0=xT[:, k, g * G:(g + 1) * G],
                                                                  scalar=spt[:, l, gcol + k:gcol + k + 1], in1=rstd[:],
                                                                  op0=ALU.mult, op1=ALU.mult),
                     reads=[B_x[g], B_rstd, B_const], writes=[_bk(B_hg, k)])

        def rmsnorm_group(l, g, gcol, hg, B_hg, sqr, B_sqr, rstd, B_rstd):
            rms_stats(l, g, sqr, B_sqr, rstd, B_rstd)
            rms_scale(l, g, gcol, hg, B_hg, rstd, B_rstd)

        def proj_chunk(wt, Bw, kview_fn, hg, B_hg, nk=8):
            ps, Bp = psum()
            for k in range(nk):
                S.op("pe", lambda e, k=k: e.matmul(ps[:], lhsT=kview_fn(wt, k), rhs=hg[:, k, :], start=(k == 0),
                                                   stop=(k == nk - 1)),
                     reads=[Bw, _bk(B_hg, k)], writes=[Bp])
            return ps, Bp

        def qknorm(ps, Bp, sq, B_sq, rs, B_rs, out_ap, B_out, gain_ap, ncols):
            S.op("act", lambda e: e.activation(out=sq[:, 0:ncols], in_=ps[:, 0:ncols], func=AF.Square), reads=[Bp],
                 writes=[B_sq])
            ps2, Bp2 = psum()
            S.op("pe", lambda e: e.matmul(ps2[:, 0:ncols], lhsT=bd64[:], rhs=sq[:, 0:ncols], start=True, stop=True),
                 reads=[B_sq, B_const], writes=[Bp2])
            rsqrt_eps(rs[:, 0:ncols], ps2[:, 0:ncols], [Bp2], B_rs)
            S.op("dve", lambda e: e.scalar_tensor_tensor(out=out_ap, in0=ps[:, 0:ncols], scalar=gain_ap,
                                                         in1=rs[:, 0:ncols], op0=ALU.mult, op1=ALU.mult),
                 reads=[Bp, B_rs, B_lay, B_const], writes=[B_out])

        def kdup_load(l):
            cols = []
            for hk in range(2):
                for dup in range(2):
                    o = (hk * 2 + dup) * 64
                    cols.append((win_cols(l, C_KA + hk * 64, 64),
                                 lambda t, o=o: v3(t, 8, 256)[:, :, o:o + 64]))
            return wload_multi(cols)

        try:
            out_done = [False]
            for l in range(DEPTH if stop != "init" else 0):
                S.op("dve", lambda e: e.tensor_scalar(out=nbgk[:], in0=spt[:, l, 18:22], scalar1=-1.0, scalar2=None,
                                                      op0=ALU.mult), reads=[B_const], writes=[B_lay])
                S.op("dve", lambda e: e.tensor_scalar(out=qg8[:], in0=spt[:, l, 16:17], scalar1=0.125, scalar2=None,
                                                      op0=ALU.mult), reads=[B_const], writes=[B_lay])
                S.op("act", lambda e: e.activation(out=es8[:], in_=spt[:, l, 112:120], func=AF.Exp), reads=[B_const],
                     writes=[B_lay])
                for hk in range(2):
                    for hh in range(4):
                        S.op("dve", lambda e, hk=hk, hh=hh: e.tensor_copy(
                            out=esink[:, hk, hh * 128:(hh + 1) * 128],
                            in_=es8[0:64, hk * 4 + HORD[hh]:hk * 4 + HORD[hh] + 1].to_broadcast([64, 128])),
                             reads=[B_lay], writes=[B_lay])
                S.op("dve", lambda e: e.memset(Sst[:], 0.0), writes=[B_S])
                S.op("dve", lambda e: e.memset(Cend[:], 0.0), writes=[B_Cend])

                S.barrier()
                AR.reset()
                hg = AR.get([8, G], BF16); B_hg = [Buf("hgA%d" % k_) for k_ in range(8)]
                sqr = AR.get([2, G], BF16); B_sqr = [Buf("sq0"), Buf("sq1")]
                rstd = AR.get([G], F32); B_rstd = Buf("rstd")
                qt = AR.get([4, G], BF16); B_qt = Buf("qt")
                kt = AR.get([4, G], BF16); B_kt = Buf("kt")
                vtok = AR.get([4, 1024], BF16); B_vtok = Buf("vtok")
                lsp = AR.get([G], F32); B_lsp = Buf("lsp")
                Cc = AR.get([G], F32); B_Cc = Buf("Cc")
                E1 = AR.get([G], F32); B_E1 = Buf("E1")
                E2 = AR.get([G], F32); B_E2 = Buf("E2")
                gklr = AR.get([G], BF16); B_gklr = Buf("gklr")
                ATs = AR.get([2, G], BF16); B_AT = [Buf("AT0"), Buf("AT1")]
                ktok = AR.get([2, G], BF16); B_ktok = [Buf("ktok0"), Buf("ktok1")]
                U = AR.get([1024], F32); B_U = [Buf("U0"), Buf("U1")]
                Ubf = AR.get([1024], BF16); B_Ubf = [Buf("Ubf0"), Buf("Ubf1")]
                dS = AR.get([4], F32); B_dS = Buf("dS")
                msg = AR.get([MSGW - 1024], F32); B_msg = Buf("msg")
                sqk = AR.get([G], BF16); B_sqk = Buf("sqk")
                rsk = AR.get([G], F32); B_rsk = Buf("rsk")

                for g in range(NG):
                    if g == 0:
                        rms_stats(l, g, sqr, B_sqr, rstd, B_rstd)
                        rms_scale(l, g, 0, hg, B_hg, rstd, B_rstd)
                    wt, Bw = wload(win_cols(l, C_GK, 128), lambda t: v3(t, 8, 128))
                    ps, Bp = psum()
                    for k in range(8):
                        S.op("pe", lambda e, k=k, wt=wt: e.matmul(ps[:, :], lhsT=v3(wt, 8, 128)[:, k, :], rhs=hg[:, k, :],
                                                                  start=(k == 0), stop=(k == 7)),
                             reads=[Bw, _bk(B_hg, k)], writes=[Bp])
                    S.op("act", lambda e, ps=ps: e.activation(out=gklr[0:16, :], in_=ps[0:16, :], func=AF.Copy),
                         reads=[Bp], writes=[B_gklr])
                    wqs = [wload(win_cols(l, C_QB + hp_ * 256, 256), lambda t: v3(t, 8, 256)) for hp_ in range(2)]
                    wks = [wload(win_cols(l, C_KB + hp_ * 256, 256), lambda t: v3(t, 8, 256)) for hp_ in range(2)]
                    def v_unit(q4, half, wv, Bwv):
                        ps, Bp = psum()
                        for bb in range(2):
                            blk = half * 2 + bb
                            for k in range(8):
                                S.op("pe", lambda e, k=k, blk=blk, bb=bb, ps=ps, wv=wv: e.matmul(
                                    ps[:, bb * 256:(bb + 1) * 256], lhsT=hg[:, k, blk * 128:(blk + 1) * 128],
                                    rhs=v3(wv, 8, 256)[:, k, :], start=(k == 0), stop=(k == 7)),
                                     reads=[Bwv, _bk(B_hg, k)], writes=[Bp])
                        S.op("act", lambda e, half=half, q4=q4, ps=ps: e.activation(
                            out=vtok[:, half * 2:half * 2 + 2, q4 * 256:(q4 + 1) * 256],
                            in_=ps[:].rearrange("p (b n) -> p b n", b=2), func=AF.Copy),
                             reads=[Bp], writes=[B_vtok])
                    for h in range(4):
                        psg, Bpg = psum()
                        S.op("pe", lambda e, h=h, psg=psg: e.matmul(psg[:], lhsT=wup[0:16, l, h * 128:(h + 1) * 128],
                                                                    rhs=gklr[0:16, :], start=True, stop=True),
                             reads=[B_gklr, B_const], writes=[Bpg])
                        S.op("act", lambda e, h=h, psg=psg: e.activation(out=lsp[:], in_=psg[:], func=AF.Exp,
                                                                         bias=nbgk[:, h:h + 1], scale=-1.0),
                             reads=[Bpg, B_lay], writes=[B_lsp])
                        S.op("act", lambda e: e.activation(out=lsp[:], in_=lsp[:], func=AF.Ln, bias=onesf[:, 0:1], scale=1.0),
                             reads=[B_lsp, B_const], writes=[B_lsp])
                        S.op("dve", lambda e, h=h: e.tensor_copy(out=sc[:, 0:1], in_=Cend[:, h:h + 1]), reads=[B_Cend],
                             writes=[B_dS])
                        S.op("dve", lambda e, h=h: e.tensor_tensor_scan(out=Cc[:], data0=onesf[:, 0:1].to_broadcast([128, G]),
                                                                        data1=lsp[:], initial=sc[:, 0:1], op0=ALU.mult,
                                                                        op1=ALU.add),
                             reads=[B_lsp, B_dS, B_const], writes=[B_Cc])
                        S.op("dve", lambda e, h=h: e.tensor_copy(out=Cend[:, h:h + 1], in_=Cc[:, G - 1:G]), reads=[B_Cc],
                             writes=[B_Cend])
                        S.op("dve", lambda e: e.tensor_scalar(out=sc[:, 1:2], in0=Cc[:, 255:256], scalar1=1.0 / 16.0,
                                                              scalar2=None, op0=ALU.mult), reads=[B_Cc], writes=[B_dS])
                        S.op("dve", lambda e: e.tensor_scalar(out=sc[:, 2:3], in0=Cc[:, 255:256], scalar1=-1.0 / 16.0,
                                                              scalar2=None, op0=ALU.mult), reads=[B_Cc], writes=[B_dS])
                        S.op("act", lambda e: e.activation(out=E1[:], in_=Cc[:], func=AF.Exp, bias=sc[:, 1:2],
                                                           scale=-1.0 / 16.0), reads=[B_Cc, B_dS], writes=[B_E1])
                        S.op("act", lambda e: e.activation(out=E2[:], in_=Cc[:], func=AF.Exp, bias=sc[:, 2:3],
                                                           scale=1.0 / 16.0), reads=[B_Cc, B_dS], writes=[B_E2])
                        S.op("act", lambda e: e.activation(out=sc[:, 3:4], in_=sc[:, 0:1], func=AF.Exp, bias=sc[:, 2:3],
                                                           scale=1.0 / 16.0), reads=[B_dS], writes=[B_dS])
                        S.op("act", lambda e: e.activation(out=sc[:, 4:5], in_=Cc[:, 255:256], func=AF.Exp,
                                                           scale=-1.0 / 16.0), reads=[B_Cc, B_dS], writes=[B_dS])
                        S.op("dve", lambda e, h=h: e.tensor_copy(out=dS[:, h:h + 1], in_=E1[:, G - 1:G]), reads=[B_E1],
                             writes=[B_dS])
                        wq, Bwq = wqs[h // 2]
                        wk, Bwk = wks[h // 2]
                        psq, Bpq = proj_chunk(wq, Bwq, lambda t, k, h=h: v3(t, 8, 256)[:, k, (h % 2) * 128:(h % 2 + 1) * 128], hg,
                                              B_hg)
                        S.op("dve", lambda e, h=h, psq=psq: e.scalar_tensor_tensor(out=qt[:, h, :], in0=psq[:],
                                                                                   scalar=128.0 ** -0.5, in1=E1[:],
                                                                                   op0=ALU.mult, op1=ALU.mult),
                             reads=[Bpq, B_E1], writes=[B_qt])
                        psk, Bpk = proj_chunk(wk, Bwk, lambda t, k, h=h: v3(t, 8, 256)[:, k, (h % 2) * 128:(h % 2 + 1) * 128], hg,
                                              B_hg)
                        S.op("dve", lambda e, h=h, psk=psk: e.tensor_tensor(out=kt[:, h, :], in0=psk[:], in1=E2[:],
                                                                            op=ALU.mult),
                             reads=[Bpk, B_E2], writes=[B_kt])
                        S.op("dve", lambda e, h=h: e.tensor_scalar(out=qhat[:, h, g * G:(g + 1) * G], in0=qt[:, h, :],
                                                                    scalar1=sc[:, 4:5], scalar2=None, op0=ALU.mult),
                             reads=[B_qt, B_dS], writes=[B_qhat[g]])
                        S.op("dve", lambda e, h=h: e.tensor_scalar(out=U[:, h * 256:(h + 1) * 256],
                                                                   in0=Sst[:, h * 256:(h + 1) * 256], scalar1=sc[:, 3:4],
                                                                   scalar2=None, op0=ALU.mult),
                             reads=[B_S, B_dS], writes=[B_U[h // 2]])
                    for hp in range(2):
                        S.op("act", lambda e, hp=hp: e.activation(out=Ubf[:, hp * 512:(hp + 1) * 512],
                                                                  in_=U[:, hp * 512:(hp + 1) * 512], func=AF.Copy),
                             reads=[B_U[hp]], writes=[B_Ubf[hp]])
                    for q4 in range(4):
                        wv, Bwv = wload(win_cols(l, C_VB + q4 * 256, 256), lambda t: v3(t, 8, 256))
                        for half in range(2):
                            v_unit(q4, half, wv, Bwv)
                    def blk_stage1(j, par):
                        tsl = slice(j * 128, (j + 1) * 128)
                        psA, BpA = psum()
                        for h in range(4):
                            S.op("pe", lambda e, h=h, psA=psA, tsl=tsl: e.matmul(psA[:, h * 128:(h + 1) * 128],
                                                                                 lhsT=kt[:, h, tsl], rhs=qt[:, h, tsl],
                                                                                 start=True, stop=True),
                                 reads=[B_kt, B_qt], writes=[BpA])
                        S.op("dve", lambda e, psA=psA, par=par: e.tensor_tensor(out=ATs[:, par, :], in0=psA[:], in1=maskc[:],
                                                                                op=ALU.mult),
                             reads=[BpA, B_const], writes=[B_AT[par]])
                        psT, BpT = psum()
                        psTb = psT[:, 0:256].bitcast(BF16)
                        for h in range(4):
                            S.op("pe", lambda e, h=h, psTb=psTb, tsl=tsl: e.transpose(psTb[:, h * 128:(h + 1) * 128],
                                                                                      kt[:, h, tsl], ident[:]),
                                 reads=[B_kt, B_const], writes=[BpT])
                        S.op("act", lambda e, psTb=psTb, par=par: e.activation(out=ktok[:, par, :], in_=psTb, func=AF.Copy),
                             reads=[BpT], writes=[B_ktok[par]])

                    def blk_stage2(j, par):
                        tsl = slice(j * 128, (j + 1) * 128)
                        kvs = []
                        for hp in range(2):
                            pskv, Bpkv = psum()
                            for hh in range(2):
                                h = hp * 2 + hh
                                S.op("pe", lambda e, pskv=pskv, hh=hh, h=h, j=j, par=par: e.matmul(
                                    pskv[:, hh * 256:(hh + 1) * 256], lhsT=ktok[:, par, h * 128:(h + 1) * 128],
                                    rhs=vtok[:, j, h * 256:(h + 1) * 256], start=True, stop=True),
                                     reads=[B_ktok[par], B_vtok], writes=[Bpkv])
                            kvs.append((pskv, Bpkv))
                        for hp in range(2):
                            pso, Bpo = psum()
                            for hh in range(2):
                                h = hp * 2 + hh
                                for c in range(2):
                                    oc = (hh * 2 + c) * 128
                                    ec = h * 256 + c * 128
                                    S.op("pe", lambda e, pso=pso, oc=oc, ec=ec, h=h, j=j, par=par: e.matmul(
                                        pso[:, oc:oc + 128], lhsT=vtok[:, j, ec:ec + 128],
                                        rhs=ATs[:, par, h * 128:(h + 1) * 128], start=True, stop=False),
                                         reads=[B_vtok, B_AT[par]], writes=[Bpo])
                                    S.op("pe", lambda e, pso=pso, oc=oc, ec=ec, h=h, tsl=tsl: e.matmul(
                                        pso[:, oc:oc + 128], lhsT=Ubf[:, ec:ec + 128], rhs=qt[:, h, tsl],
                                        start=False, stop=True), reads=[B_Ubf[hp], B_qt], writes=[Bpo])
                            S.op("act", lambda e, pso=pso, hp=hp, j=j: e.activation(
                                out=oloc[:, hp * 4:hp * 4 + 4, g * G + j * 128:g * G + (j + 1) * 128],
                                in_=pso[:].rearrange("p (c t) -> p c t", c=4), func=AF.Copy),
                                 reads=[Bpo], writes=[B_oloc[g]])
                        for hp in range(2):
                            pskv, Bpkv = kvs[hp]
                            S.op("dve", lambda e, pskv=pskv, hp=hp: e.tensor_tensor(
                                out=U[:, hp * 512:(hp + 1) * 512], in0=pskv[:], in1=U[:, hp * 512:(hp + 1) * 512],
                                op=ALU.add), reads=[Bpkv, B_U[hp]], writes=[B_U[hp]])
                            S.op("act", lambda e, hp=hp: e.activation(out=Ubf[:, hp * 512:(hp + 1) * 512],
                                                                      in_=U[:, hp * 512:(hp + 1) * 512], func=AF.Copy),
                                 reads=[B_U[hp]], writes=[B_Ubf[hp]])

                    blk_stage1(0, 0)
                    for j in range(4):
                        if j + 1 < 4:
                            blk_stage1(j + 1, (j + 1) % 2)
                        blk_stage2(j, j % 2)
                        if j == 0 and g + 1 < NG:
                            rms_stats(l, g + 1, sqr, B_sqr, rstd, B_rstd)
                        if j == 1 and g + 1 < NG:
                            rms_scale(l, g + 1, 0, hg, B_hg, rstd, B_rstd)
                    for h in range(4):
                        S.op("dve", lambda e, h=h: e.tensor_scalar(out=Sst[:, h * 256:(h + 1) * 256],
                                                                   in0=U[:, h * 256:(h + 1) * 256], scalar1=dS[:, h:h + 1],
                                                                   scalar2=None, op0=ALU.mult),
                             reads=[B_U[h // 2], B_dS], writes=[B_S])
                    if g == NG - 1:
                        wkd, Bwkd = kdup_load(l)
                        for hk in range(2):
                            ps, Bp = psum()
                            for k in range(8):
                                S.op("pe", lambda e, k=k, hk=hk, ps=ps, wkd=wkd: e.matmul(
                                    ps[:, 0:128], lhsT=v3(wkd, 8, 256)[:, k, hk * 128:(hk + 1) * 128],
                                    rhs=hg[:, k, 384:512], start=(k == 0), stop=(k == 7)), reads=[Bwkd, _bk(B_hg, k)], writes=[Bp])
                            qknorm(ps, Bp, sqk, B_sqk, rsk, B_rsk, msg[:, 4 + hk * 128:4 + (hk + 1) * 128], B_msg,
                                   spt[:, l, 17:18], 128)
                        wv, Bwv = wload(win_cols(l, C_VA, 128), lambda t: v3(t, 8, 128))
                        ps, Bp = psum()
                        for k in range(8):
                            S.op("pe", lambda e, k=k, ps=ps, wv=wv: e.matmul(ps[:, 0:128], lhsT=hg[:, k, 384:512],
                                                                             rhs=v3(wv, 8, 128)[:, k, :], start=(k == 0),
                                                                             stop=(k == 7)), reads=[Bwv, _bk(B_hg, k)], writes=[Bp])
                        S.op("act", lambda e, ps=ps: e.activation(out=msg[:, 260:388], in_=ps[:, 0:128], func=AF.Copy),
                             reads=[Bp], writes=[B_msg])
                if stop == ("A", l):
                    break
                S.op("act", lambda e: e.activation(out=msg[:, 0:4], in_=Cend[:], func=AF.Exp, scale=-1.0 / 16.0),
                     reads=[B_Cend], writes=[B_msg])
                B_cin = Buf("cin")
                B_cout = Buf("cout")
                S.dma("sp", lambda e: e.dma_start(out=cin1[l][:, 0:1024], in_=Sst[:]), "cin", reads=[B_S], writes=[B_cin])
                S.dma("sp", lambda e: e.dma_start(out=cin1[l][:, 1024:MSGW], in_=msg[:]), "cin", reads=[B_msg],
                      writes=[B_cin])
                S.cc(lambda e: e.collective_compute("AllGather", ALU.bypass, replica_groups=RG, ins=[cin1[l].opt()],
                                                    outs=[cout1[l].opt()]), "cc", reads=[B_cin], writes=[B_cout])
                S.barrier(skip=("cc",))
                AR.reset()
                hg = AR.get([8, G], BF16); B_hg = [Buf("hgB%d" % k_) for k_ in range(8)]
                sqr = AR.get([2, G], BF16); B_sqr = [Buf("sq0B"), Buf("sq1B")]
                rstd = AR.get([G], F32); B_rstd = Buf("rstdB")
                qn_off = AR.off
                qn = AR.get([4, G], BF16); B_qn = Buf("qn")
                kdup = AR.get([2, 128 + G], BF16); B_kdup = Buf("kdup")
                vaug = AR.get([5, 2, 128], BF16); B_vaug = Buf("vaug")
                pT_off = AR.off
                pT = AR.get([2, 2, G], BF16); B_pT = [[Buf("pT00"), Buf("pT01")], [Buf("pT10"), Buf("pT11")]]
                ya_off = AR.off
                ya = AR.get([8, G], BF16, parts=64); B_ya = Buf("ya")
                ya128 = AR.view(ya_off, [8, G], BF16)
                ofp = AR.get([2, G], F32); B_ofp = Buf("ofp")
                merged_off = AR.off
                merged = AR.get([8, G], BF16); B_merged = Buf("merged")
                sr = AR.view(merged_off, [2, G], BF16); B_sr = B_merged
                sa = AR.get([G], F32); B_sa = Buf("sa")
                sbb = AR.get([G], F32); B_sbb = Buf("sbb")
                sqk = sqr[:, 0, :]; B_sqk = B_sqr[0]
                rsk = rstd; B_rsk = B_rstd
                dn = AR.get([G], F32, parts=64); B_dn = Buf("dn")
                ofp2 = AR.view(qn_off, [2, G], F32)
                sr2 = AR.view(pT_off, [2, G], BF16)
                sqr2 = AR.view(pT_off + 512, [2, G], BF16)
                GF = [dict(ofp=ofp, Bofp=[B_ofp], sr=sr, Bsr=[B_sr], sq=sqr, Bsq=[[B_sqr[0]], [B_sqr[1]]], rs=rstd,
                           Brs=[B_rstd]),
                      dict(ofp=ofp2, Bofp=[B_qn], sr=sr2, Bsr=[B_pT[0][0], B_pT[0][1]],
                           sq=sqr2, Bsq=[[B_pT[1][0]], [B_pT[1][1]]], rs=sa, Brs=[B_sa])]

                S.op("dve", lambda e: e.memset(vaug[:], 0.0), writes=[B_vaug])
                XOFF = ARW - (4 * MSGW + 1024 + 384)
                assert XOFF >= pT_off
                gath = AR.view(XOFF, [4, MSGW], F32); B_gath = Buf("gath")
                acc = AR.view(XOFF + 4 * MSGW, [1024], F32); B_acc = Buf("acc")
                kvacc = AR.view(XOFF + 4 * MSGW + 1024, [384], F32); B_kvacc = Buf("kvacc")
                fct = sc[:, 8:24]; B_fct = Buf("fct")
                S.dma("sp", lambda e: e.dma_start(out=gath[:, 0:3, :],
                                                  in_=cout1[l][0:384, :].rearrange("(j p) n -> p j n", p=128)), "gath",
                      reads=[B_cout], writes=[B_gath])

                def exchange_compute():
                    S.op("dve", lambda e: e.memset(acc[:], 0.0), writes=[B_acc])
                    for jr in range(3):
                        mj = flags[:, 4 + jr:5 + jr]
                        S.op("dve", lambda e, jr=jr, mj=mj: e.tensor_scalar(out=fct[:, 0:4], in0=gath[:, jr, 1024:1028],
                                                                            scalar1=-1.0, scalar2=mj, op0=ALU.add,
                                                                            op1=ALU.mult), reads=[B_gath, B_const],
                             writes=[B_fct])
                        S.op("dve", lambda e: e.tensor_scalar(out=fct[:, 0:4], in0=fct[:, 0:4], scalar1=1.0, scalar2=None,
                                                              op0=ALU.add), reads=[B_fct], writes=[B_fct])
                        for h in range(4):
                            hs = slice(h * 256, (h + 1) * 256)
                            S.op("dve", lambda e, h=h, hs=hs: e.tensor_scalar(out=acc[:, hs], in0=acc[:, hs],
                                                                              scalar1=fct[:, h:h + 1], scalar2=None,
                                                                              op0=ALU.mult), reads=[B_fct, B_acc],
                                 writes=[B_acc])
                            S.op("dve", lambda e, jr=jr, hs=hs, mj=mj: e.scalar_tensor_tensor(out=acc[:, hs],
                                                                                              in0=gath[:, jr, hs], scalar=mj,
                                                                                              in1=acc[:, hs], op0=ALU.mult,
                                                                                              op1=ALU.add),
                                 reads=[B_gath, B_acc, B_const], writes=[B_acc])
                    S.op("act", lambda e: e.activation(out=Sinit_bf[:], in_=acc[:], func=AF.Copy), reads=[B_acc],
                         writes=[B_Sinit])
                    S.op("dve", lambda e: e.tensor_scalar(out=kvacc[:], in0=gath[:, 0, 1028:1412], scalar1=flags[:, 0:1],
                                                          scalar2=None, op0=ALU.mult), reads=[B_gath, B_const],
                         writes=[B_kvacc])
                    for jr in range(1, 3):
                        S.op("dve", lambda e, jr=jr: e.scalar_tensor_tensor(out=kvacc[:], in0=gath[:, jr, 1028:1412],
                                                                            scalar=flags[:, jr:jr + 1], in1=kvacc[:],
                                                                            op0=ALU.mult, op1=ALU.add),
                             reads=[B_gath, B_kvacc, B_const], writes=[B_kvacc])
                    S.op("act", lambda e: e.activation(out=kprev[:], in_=kvacc[:, 0:256].rearrange("p (a b) -> p a b", a=2),
                                                       func=AF.Copy), reads=[B_kvacc], writes=[B_kvprev])
                    S.op("act", lambda e: e.activation(out=vprev[:], in_=kvacc[:, 256:384], func=AF.Copy), reads=[B_kvacc],
                         writes=[B_kvprev])
                    S.op("dve", lambda e: e.memset(sc[:, 30:31], 0.0), reads=[B_Sinit, B_kvprev],
                         writes=[B_gath, B_acc, B_kvacc, B_pT[0][0], B_pT[0][1], B_pT[1][0], B_pT[1][1], B_ya, B_ofp, B_merged,
                                 B_sa, B_sbb, B_dn])

                if stop == ("X", l):
                    break
                def outproj_chunk(gp, j, st_):
                    gsp = slice(gp * G, (gp + 1) * G)
                    if j % 2 == 0:
                        jj = j // 2
                        st_["w"] = wload(wout_d[l].rearrange("(k p) n -> p k n", p=128)[:, :, jj * 256:(jj + 1) * 256],
                                         lambda t: v3(t, 8, 256))
                    wo, Bwo = st_["w"]
                    c = j % 2
                    ps, Bp = proj_chunk(wo, Bwo, lambda t, k, c=c: v3(t, 8, 256)[:, k, c * 128:(c + 1) * 128], merged,
                                        B_merged)
                    S.op("dve", lambda e, ps=ps, j=j, gsp=gsp: e.tensor_tensor(out=xT[:, j, gsp], in0=ps[:],
                                                                               in1=xT[:, j, gsp], op=ALU.add),
                         reads=[Bp, B_x[gp]], writes=[B_x[gp]])

                pend_out = None
                for g in range(NG):
                    gs = slice(g * G, (g + 1) * G)
                    if g == 0:
                        rms_stats(l, g, sqr, B_sqr, rstd, B_rstd)
                    rms_scale(l, g, 0, hg, B_hg, rstd, B_rstd)
                    wqs = [wload(win_cols(l, C_QA + hp_ * 256, 256), lambda t: v3(t, 8, 256)) for hp_ in range(2)]
                    wkd, Bwkd = kdup_load(l)
                    if g > 0:
                        S.op("dve", lambda e: e.tensor_copy(out=kdup[:, :, 0:128], in_=kdup[:, :, G:G + 128]),
                             reads=[B_kdup], writes=[B_kdup])
                        S.op("dve", lambda e: e.tensor_copy(out=vaug[:, 0, :, :], in_=vaug[:, 4, :, :]), reads=[B_vaug],
                             writes=[B_vaug])
                    units = []
                    for c in range(4):
                        units.append((wqs[c // 2], (c % 2) * 128, qn[:, c, :], B_qn, qg8[:, 0:1]))
                    for hk in range(2):
                        units.append(((wkd, Bwkd), hk * 128, kdup[:, hk, 128:128 + G], B_kdup, spt[:, l, 17:18]))
                    prev_u = None
                    for u_ in units + [None]:
                        cur_u = None
                        if u_ is not None:
                            (wt_, Bwt_), co_, out_, Bout_, gain_ = u_
                            ps_, Bp_ = proj_chunk(wt_, Bwt_, lambda t, k, co_=co_: v3(t, 8, 256)[:, k, co_:co_ + 128], hg, B_hg)
                            cur_u = (ps_, Bp_, out_, Bout_, gain_)
                        if prev_u is not None:
                            qknorm(prev_u[0], prev_u[1], sqk, B_sqk, rsk, B_rsk, prev_u[2], prev_u[3], prev_u[4], G)
                        prev_u = cur_u
                    wv, Bwv = wload(win_cols(l, C_VA, 128), lambda t: v3(t, 8, 128))
                    ps, Bp = psum()
                    for blk in range(4):
                        for k in range(8):
                            S.op("pe", lambda e, k=k, blk=blk, ps=ps, wv=wv: e.matmul(
                                ps[:, blk * 128:(blk + 1) * 128], lhsT=hg[:, k, blk * 128:(blk + 1) * 128],
                                rhs=v3(wv, 8, 128)[:, k, :], start=(k == 0), stop=(k == 7)), reads=[Bwv, _bk(B_hg, k)], writes=[Bp])
                    S.op("act", lambda e, ps=ps: e.activation(
                        out=vaug[:, 1:5, :, 0:64], in_=ps[:].rearrange("p (b h d) -> p b h d", b=4, h=2), func=AF.Copy),
                         reads=[Bp], writes=[B_vaug])
                    if g == 0:
                        exchange_compute()
                        S.op("dve", lambda e: e.tensor_copy(out=kdup[:, :, 0:128], in_=kprev[:]), reads=[B_kvprev],
                             writes=[B_kdup])
                        for hk in range(2):
                            S.op("dve", lambda e, hk=hk: e.tensor_copy(out=vaug[:, 0, hk, 0:64],
                                                                        in_=vprev[:, hk * 64:(hk + 1) * 64]),
                                 reads=[B_kvprev], writes=[B_vaug])
                    if stop == ("B1", l):
                        raise _Stop()
                    def attn_stage1(i, hk, par):
                        psE, BpE = psum()
                        psO, BpO = psum()
                        for kb in range(2):
                            koff = i * 128 + kb * 128
                            for b in range(4):
                                hh = HORD[b]
                                chunk = 2 * hk + hh // 2
                                p0 = 64 * (hh % 2)
                                pst_, Bpt_ = (psE, BpE) if b < 2 else (psO, BpO)
                                cb = kb * 256 + (b % 2) * 128
                                S.op("pe", lambda e, pst_=pst_, cb=cb, chunk=chunk, p0=p0, koff=koff, hk=hk, i=i: e.matmul(
                                    pst_[:, cb:cb + 128], lhsT=kdup[p0:p0 + 64, hk, koff:koff + 128],
                                    rhs=qn[p0:p0 + 64, chunk, i * 128:(i + 1) * 128], start=True, stop=True),
                                     reads=[B_kdup, B_qn], writes=[Bpt_])
                        S.op("act", lambda e, psE=psE, par=par: e.activation(
                            out=pT[:, par, :, 0:256], in_=psE[:].rearrange("p (k n) -> p k n", k=2), func=AF.Exp),
                             reads=[BpE], writes=[B_pT[par][0], B_pT[par][1]])
                        S.op("act", lambda e, psO=psO, par=par: e.activation(
                            out=pT[:, par, :, 256:512], in_=psO[:].rearrange("p (k n) -> p k n", k=2), func=AF.Exp),
                             reads=[BpO], writes=[B_pT[par][0], B_pT[par][1]])
                        for kb in range(2):
                            if kb == 1:
                                mk = maskc
                            else:
                                mk = maskp0 if (g == 0 and i == 0) else maskp
                            S.op("dve", lambda e, kb=kb, mk=mk, par=par: e.tensor_tensor(
                                out=pT[:, par, kb, :], in0=pT[:, par, kb, :], in1=mk[:], op=ALU.mult),
                                 reads=[B_pT[par][kb], B_const], writes=[B_pT[par][kb]])

                    dnb = [(dn, B_dn), (sbb[0:64, :], B_sbb)]

                    def attn_stage2a(i, hk, par):
                        pso, Bpo = psum()
                        psd, Bpd = psum()
                        dn_, Bdn_ = dnb[par]
                        for kb in range(2):
                            S.op("pe", lambda e, pso=pso, kb=kb, hk=hk, i=i, par=par: e.matmul(
                                pso[:, :], lhsT=vaug[:, i + kb, hk, :], rhs=pT[:, par, kb, :], start=(kb == 0),
                                stop=(kb == 1)), reads=[B_vaug, B_pT[par][kb]], writes=[Bpo])
                        for kb in range(2):
                            S.op("pe", lambda e, psd=psd, kb=kb, par=par: e.matmul(
                                psd[:, :], lhsT=ones128[:], rhs=pT[:, par, kb, :], start=(kb == 0), stop=(kb == 1)),
                                 reads=[B_const, B_pT[par][kb]], writes=[Bpd])
                        S.op("dve", lambda e, psd=psd, hk=hk, dn_=dn_: e.tensor_tensor(out=dn_, in0=psd[0:64, :],
                                                                                       in1=esink[:, hk, :], op=ALU.add),
                             reads=[Bpd, B_lay], writes=[Bdn_])
                        return pso, Bpo

                    def attn_stage2b(i, hk, par, pso, Bpo):
                        dn_, Bdn_ = dnb[par]
                        S.op("act", lambda e, dn_=dn_: e.activation(out=dn_, in_=dn_, func=AF.Ln), reads=[Bdn_],
                             writes=[Bdn_])
                        S.op("act", lambda e, dn_=dn_: e.activation(out=dn_, in_=dn_, func=AF.Exp, scale=-1.0),
                             reads=[Bdn_], writes=[Bdn_])
                        S.op("dve", lambda e, pso=pso, hk=hk, i=i, dn_=dn_: e.tensor_tensor(
                            out=ya[:, hk * 4:(hk + 1) * 4, i * 128:(i + 1) * 128],
                            in0=pso[0:64, :].rearrange("p (h q) -> p h q", h=4),
                            in1=dn_.rearrange("p (h q) -> p h q", h=4), op=ALU.mult),
                             reads=[Bpo, Bdn_], writes=[B_ya])

                    its = [(i, hk) for i in range(4) for hk in range(2)]
                    attn_stage1(its[0][0], its[0][1], 0)
                    pend = None
                    for n_, (i, hk) in enumerate(its):
                        if n_ + 1 < len(its):
                            attn_stage1(its[n_ + 1][0], its[n_ + 1][1], (n_ + 1) % 2)
                        pso_, Bpo_ = attn_stage2a(i, hk, n_ % 2)
                        if pend is not None:
                            attn_stage2b(*pend)
                        pend = (i, hk, n_ % 2, pso_, Bpo_)
                        if pend_out is not None:
                            outproj_chunk(pend_out[0], n_, pend_out[1])
                    attn_stage2b(*pend)
                    pend_out = None
                    S.dma("sp", lambda e: e.dma_start(
                        out=ya128[64:128, :, :].rearrange("p (m t) g -> p m t g", t=2)[:, :, 0, :],
                        in_=ya128[0:64, :, :].rearrange("p (m t) g -> p m t g", t=2)[:, :, 1, :]),
                          "yash", reads=[B_ya], writes=[B_ya])
                    if stop == ("B2", l):
                        raise _Stop()
                    def gla_p1(hp):
                        for h in (2 * hp, 2 * hp + 1):
                            F_ = GF[h % 2]
                            wr, Bwr = wload(win_cols(l, C_RB + h * 256, 256), lambda t: v3(t, 8, 256))
                            for c in range(2):
                                psc, Bpc = psum()
                                ec = h * 256 + c * 128
                                S.op("pe", lambda e, psc=psc, ec=ec, h=h: e.matmul(psc[:], lhsT=Sinit_bf[:, ec:ec + 128],
                                                                                   rhs=qhat[:, h, gs], start=True, stop=True),
                                     reads=[B_Sinit, B_qhat[g]], writes=[Bpc])
                                S.op("dve", lambda e, psc=psc, c=c, h=h, F_=F_: e.tensor_tensor(
                                    out=F_["ofp"][:, c, :], in0=psc[:], in1=oloc[:, h * 2 + c, gs], op=ALU.add),
                                     reads=[Bpc, B_oloc[g]], writes=F_["Bofp"])
                                S.op("act", lambda e, c=c, F_=F_: e.activation(out=F_["sq"][:, c, :], in_=F_["ofp"][:, c, :],
                                                                               func=AF.Square),
                                     reads=F_["Bofp"], writes=F_["Bsq"][c])
                            for c in range(2):
                                psr, Bpr = proj_chunk(wr, Bwr, lambda t, k, c=c: v3(t, 8, 256)[:, k, c * 128:(c + 1) * 128],
                                                      hg, B_hg)
                                S.op("act", lambda e, c=c, psr=psr, F_=F_: e.activation(out=F_["sr"][:, c, :], in_=psr[:],
                                                                                        func=AF.Silu),
                                     reads=[Bpr], writes=F_["Bsr"])

                    def gla_p2(hp):
                        for h in (2 * hp, 2 * hp + 1):
                            F_ = GF[h % 2]
                            pss, Bpss = psum()
                            for c in range(2):
                                S.op("pe", lambda e, c=c, pss=pss, F_=F_: e.matmul(pss[:], lhsT=ones_dv[:],
                                                                                   rhs=F_["sq"][:, c, :], start=(c == 0),
                                                                                   stop=(c == 1)),
                                     reads=F_["Bsq"][c] + [B_const], writes=[Bpss])
                            S.op("act", lambda e, pss=pss, F_=F_: e.activation(out=F_["rs"][:], in_=pss[:], func=AF.Ln,
                                                                               bias=epsf[:, 0:1], scale=1.0),
                                 reads=[Bpss, B_const], writes=F_["Brs"])
                            S.op("act", lambda e, F_=F_: e.activation(out=F_["rs"][:], in_=F_["rs"][:], func=AF.Exp,
                                                                      scale=-0.5), reads=F_["Brs"], writes=F_["Brs"])
                            for c in range(2):
                                S.op("dve", lambda e, c=c, F_=F_: e.scalar_tensor_tensor(
                                    out=F_["ofp"][:, c, :], in0=F_["ofp"][:, c, :], scalar=spt[:, l, 22 + c:23 + c],
                                    in1=F_["rs"][:], op0=ALU.mult, op1=ALU.mult),
                                     reads=F_["Bofp"] + F_["Brs"] + [B_const], writes=F_["Bofp"])
                                S.op("dve", lambda e, c=c, h=h, F_=F_: e.tensor_tensor(
                                    out=oloc[:, h * 2 + c, gs], in0=F_["ofp"][:, c, :], in1=F_["sr"][:, c, :], op=ALU.mult),
                                     reads=F_["Bofp"] + F_["Bsr"], writes=[B_oloc[g]])
                    def merge_load(st2):
                        c2 = slice(st2 * 256, (st2 + 1) * 256)
                        wA, BwA = wload_multi([
                            (wba_d[l][q_ * 256 + u_ * 128:q_ * 256 + u_ * 128 + 128, c2].rearrange("(r p) n -> p r n", p=64),
                             lambda t, q_=q_, u_=u_: t[u_ * 64:u_ * 64 + 64, 0:1024].rearrange("p (m n) -> p m n", m=4)[:, 2 * q_:2 * q_ + 2, :])
                            for q_ in range(2) for u_ in range(2)])
                        wGa, BwGa = wload(win_cols(l, C_GA + st2 * 256, 256), lambda t: v3(t, 8, 256))
                        wGb, BwGb = wload(win_cols(l, C_GB + st2 * 256, 256), lambda t: v3(t, 8, 256))
                        wB, BwB = wload(wbb_d[l].rearrange("(k p) n -> p k n", p=128)[:, :, c2], lambda t: v3(t, 8, 256))
                        return (wA, BwA, wGa, BwGa, wGb, BwGb, wB, BwB)

                    def merge_pre(st2, jj, W_):
                        if True:
                            (wA, BwA, wGa, BwGa, wGb, BwGb, wB, BwB) = W_
                            j = st2 * 2 + jj
                            cj = slice(jj * 128, (jj + 1) * 128)
                            psA, BpA = psum()
                            for m_ in range(4):
                                S.op("pe", lambda e, psA=psA, m_=m_, wA=wA, cj=cj: e.matmul(
                                    psA[:], lhsT=wA[:, 0:1024].rearrange("p (m n) -> p m n", m=4)[:, m_, cj],
                                    rhs=ya128[:, 2 * m_, :], start=(m_ == 0), stop=(m_ == 3)),
                                     reads=[BwA, B_ya], writes=[BpA])
                            psGa, BpGa = proj_chunk(wGa, BwGa, lambda t, k, cj=cj: v3(t, 8, 256)[:, k, cj], hg, B_hg)
                            psGb, BpGb = proj_chunk(wGb, BwGb, lambda t, k, cj=cj: v3(t, 8, 256)[:, k, cj], hg, B_hg)
                            return (psA, BpA, psGa, BpGa, psGb, BpGb)

                    def merge_post(st2, jj, W_, P_):
                        if True:
                            (wA, BwA, wGa, BwGa, wGb, BwGb, wB, BwB) = W_
                            (psA, BpA, psGa, BpGa, psGb, BpGb) = P_
                            j = st2 * 2 + jj
                            cj = slice(jj * 128, (jj + 1) * 128)
                            S.op("act", lambda e, psGa=psGa: e.activation(out=sa[:], in_=psGa[:], func=AF.Sigmoid),
                                 reads=[BpGa], writes=[B_sa])
                            S.op("act", lambda e, psGb=psGb: e.activation(out=sbb[:], in_=psGb[:], func=AF.Sigmoid),
                                 reads=[BpGb], writes=[B_sbb])
                            S.op("dve", lambda e, psA=psA: e.tensor_tensor(out=sa[:], in0=psA[:], in1=sa[:], op=ALU.mult),
                                 reads=[BpA, B_sa], writes=[B_sa])
                            psB, BpB = psum()
                            for k in range(8):
                                S.op("pe", lambda e, psB=psB, k=k, wB=wB, cj=cj: e.matmul(
                                    psB[:], lhsT=v3(wB, 8, 256)[:, k, cj], rhs=oloc[:, k, gs], start=(k == 0), stop=(k == 7)),
                                     reads=[BwB, B_oloc[g]], writes=[BpB])
                            S.op("dve", lambda e, psB=psB: e.tensor_tensor(out=sbb[:], in0=psB[:], in1=sbb[:], op=ALU.mult),
                                 reads=[BpB, B_sbb], writes=[B_sbb])
                            S.op("dve", lambda e, j=j: e.tensor_tensor(out=merged[:, j, :], in0=sa[:], in1=sbb[:], op=ALU.add),
                                 reads=[B_sa, B_sbb], writes=[B_merged])


                    gla_p1(0)
                    gla_p2(0)
                    gla_p1(1)
                    W0_ = merge_load(0)
                    P0_ = merge_pre(0, 0, W0_)
                    P1_ = merge_pre(0, 1, W0_)
                    gla_p2(1)
                    merge_post(0, 0, W0_, P0_)
                    merge_post(0, 1, W0_, P1_)
                    if g + 1 < NG:
                        rms_stats(l, g + 1, sqr, B_sqr, rstd, B_rstd)
                    for st2 in range(1, 4):
                        W_ = merge_load(st2)
                        for jj in range(2):
                            merge_post(st2, jj, W_, merge_pre(st2, jj, W_))
                    if stop == ("B3", l):
                        raise _Stop()
                    if stop == ("B4", l):
                        raise _Stop()
                    if g == NG - 1:
                        st_ = {}
                        for j in range(8):
                            outproj_chunk(g, j, st_)
                    else:
                        pend_out = (g, {})
                if stop == ("mix", l):
                    break
                S.barrier()
                AR.reset()
                hgs_ = [(AR.get([8, G], BF16), [Buf("hgC0_%d" % k_) for k_ in range(8)]), (AR.get([8, G], BF16), [Buf("hgC1_%d" % k_) for k_ in range(8)])]
                sqr = AR.get([2, G], BF16); B_sqr = [Buf("sq0C"), Buf("sq1C")]
                rstd = AR.get([G], F32); B_rstd = Buf("rstdC")
                gt = AR.get([NJ, G], BF16); B_gt = Buf("gt")
                aext = AR.get([2, G + 2], F32); B_aext = [Buf("aext0"), Buf("aext1")]
                yv = AR.get([2, G], F32); B_yv = [Buf("yv0"), Buf("yv1")]
                msg2 = AR.get([16], F32); B_msg2 = Buf("msg2")
                gath2 = AR.get([4, 16], F32); B_gath2 = Buf("gath2")
                xh = AR.get([8, 2], F32); B_xh = Buf("xh")
                sq2 = AR.get([8, 2], BF16); B_sq2 = Buf("sq2")
                rs2 = AR.get([2], F32); B_rs2 = Buf("rs2")
                h2h1 = AR.get([8, 2], BF16); B_h2h1 = Buf("h2h1")
                B_cin2 = Buf("cin2"); B_cout2 = Buf("cout2")

                def halo_norm(src3, B_src, dst, B_dst):
                    S.op("act", lambda e: e.activation(out=sq2[:], in_=src3, func=AF.Square), reads=B_src, writes=[B_sq2])
                    ps, Bp = psum()
                    for k in range(8):
                        S.op("pe", lambda e, k=k, ps=ps: e.matmul(ps[:, 0:2], lhsT=ones_dm[:], rhs=sq2[:, k, :],
                                                                  start=(k == 0), stop=(k == 7)),
                             reads=[B_sq2, B_const], writes=[Bp])
                    rsqrt_eps(rs2[:], ps[:, 0:2], [Bp], B_rs2)
                    for k in range(8):
                        S.op("dve", lambda e, k=k: e.scalar_tensor_tensor(out=dst[:, k, :], in0=src3[:, k, :],
                                                                          scalar=spt[:, l, 8 + k:9 + k], in1=rs2[:],
                                                                          op0=ALU.mult, op1=ALU.mult),
                             reads=B_src + [B_rs2, B_const], writes=[B_dst])

                S.op("act", lambda e: e.activation(out=msg2[:].rearrange("p (k t) -> p k t", k=8),
                                                   in_=xT[:, :, TOK - 2:TOK], func=AF.Copy), reads=[B_x[NG - 1]],
                     writes=[B_msg2])
                S.dma("sp", lambda e: e.dma_start(out=cin2[l], in_=msg2[:]), "cin", reads=[B_msg2], writes=[B_cin2])
                S.cc(lambda e: e.collective_compute("AllGather", ALU.bypass, replica_groups=RG, ins=[cin2[l].opt()],
                                                    outs=[cout2[l].opt()]), "cc", reads=[B_cin2], writes=[B_cout2])
                S.dma("sp", lambda e: e.dma_start(out=gath2[:], in_=cout2[l].rearrange("(j p) n -> p j n", p=128)), "gath",
                      reads=[B_cout2], writes=[B_gath2])
                halo_norm(xT[:, :, G - 2:G], [B_x[0]], h2h1, B_h2h1)
                order_ = [1, 2, 3, 0]
                rmsnorm_group(l, order_[0], 8, hgs_[0][0], hgs_[0][1], sqr, B_sqr, rstd, B_rstd)
                for p_, g in enumerate(order_):
                    gs = slice(g * G, (g + 1) * G)
                    if g == 0:
                        xhf = xh[:].rearrange("p k t -> p (k t)")
                        S.op("dve", lambda e: e.tensor_scalar(out=xhf, in0=gath2[:, 0, :], scalar1=flags[:, 0:1],
                                                              scalar2=None, op0=ALU.mult), reads=[B_gath2, B_const],
                             writes=[B_xh])
                        for jr in range(1, 4):
                            S.op("dve", lambda e, jr=jr: e.scalar_tensor_tensor(out=xhf, in0=gath2[:, jr, :],
                                                                                scalar=flags[:, jr:jr + 1], in1=xhf,
                                                                                op0=ALU.mult, op1=ALU.add),
                                 reads=[B_gath2, B_xh, B_const], writes=[B_xh])
                        halo_norm(xh[:], [B_xh], h2halo, B_h2halo)
                    hg, B_hg = hgs_[p_ % 2]
                    for jp in range(NJ // 2):
                        wfa, Bwfa = wload(wfi_d[l].rearrange("(k p) n -> p k n", p=128)[:, :, jp * 256:(jp + 1) * 256],
                                          lambda t: v3(t, 8, 256))
                        wfu, Bwfu = wload(
                            wfi_d[l].rearrange("(k p) n -> p k n", p=128)[:, :, D_FF + jp * 256:D_FF + (jp + 1) * 256],
                            lambda t: v3(t, 8, 256))
                        for jj in range(2):
                            j = jp * 2 + jj
                            r = j % 2
                            cw = 46 + j * 3
                            psa, Bpa = proj_chunk(wfa, Bwfa, lambda t, k, jj=jj: v3(t, 8, 256)[:, k, jj * 128:(jj + 1) * 128],
                                                  hg, B_hg)
                            psu, Bpu = proj_chunk(wfu, Bwfu, lambda t, k, jj=jj: v3(t, 8, 256)[:, k, jj * 128:(jj + 1) * 128],
                                                  hg, B_hg)
                            if g in (0, 1):
                                hsrc, B_hsrc = (h2halo, B_h2halo) if g == 0 else (h2h1, B_h2h1)
                                psh, Bph = psum()
                                for k in range(8):
                                    S.op("pe", lambda e, k=k, psh=psh, jj=jj, wfa=wfa, hsrc=hsrc: e.matmul(
                                        psh[:, 0:2], lhsT=v3(wfa, 8, 256)[:, k, jj * 128:(jj + 1) * 128], rhs=hsrc[:, k, :],
                                        start=(k == 0), stop=(k == 7)), reads=[Bwfa, B_hsrc], writes=[Bph])
                                S.op("act", lambda e, psh=psh, r=r: e.activation(out=aext[:, r, 0:2], in_=psh[:, 0:2],
                                                                                 func=AF.Copy), reads=[Bph],
                                     writes=[B_aext[r]])
                            else:
                                S.op("act", lambda e, r=r, j=j: e.activation(out=aext[:, r, 0:2], in_=ahalo[:, j, :], func=AF.Copy),
                                     reads=[B_ahalo], writes=[B_aext[r]])
                            S.op("act", lambda e, psa=psa, r=r: e.activation(out=aext[:, r, 2:G + 2], in_=psa[:],
                                                                             func=AF.Copy), reads=[Bpa], writes=[B_aext[r]])
                            S.op("act", lambda e, r=r, j=j: e.activation(out=ahalo[:, j, :], in_=aext[:, r, G:G + 2], func=AF.Copy),
                                 reads=[B_aext[r]], writes=[B_ahalo])
                            S.op("dve", lambda e, r=r, cw=cw, j=j: e.tensor_scalar(
                                out=yv[:, r, :], in0=aext[:, r, 2:G + 2], scalar1=spt[:, l, cw + 2:cw + 3],
                                scalar2=spt[:, l, 24 + j:25 + j], op0=ALU.mult, op1=ALU.add),
                                 reads=[B_aext[r], B_const], writes=[B_yv[r]])
                            S.op("dve", lambda e, r=r, cw=cw: e.scalar_tensor_tensor(
                                out=yv[:, r, :], in0=aext[:, r, 1:G + 1], scalar=spt[:, l, cw + 1:cw + 2], in1=yv[:, r, :],
                                op0=ALU.mult, op1=ALU.add), reads=[B_aext[r], B_yv[r], B_const], writes=[B_yv[r]])
                            S.op("dve", lambda e, r=r, cw=cw: e.scalar_tensor_tensor(
                                out=yv[:, r, :], in0=aext[:, r, 0:G], scalar=spt[:, l, cw:cw + 1], in1=yv[:, r, :],
                                op0=ALU.mult, op1=ALU.add), reads=[B_aext[r], B_yv[r], B_const], writes=[B_yv[r]])
                            S.op("act", lambda e, r=r: e.activation(out=yv[:, r, :], in_=yv[:, r, :], func=AF.Silu),
                                 reads=[B_yv[r]], writes=[B_yv[r]])
                            S.op("dve", lambda e, r=r, j=j, psu=psu: e.tensor_tensor(out=gt[:, j, :], in0=psu[:],
                                                                                     in1=yv[:, r, :], op=ALU.mult),
                                 reads=[Bpu, B_yv[r]], writes=[B_gt])
                    for jp2 in range(4):
                        pss2 = [psum(), psum()]
                        for part in range(3):
                            k0 = part * 8
                            nk = min(8, NJ - k0)
                            wo, Bwo = wload(
                                wfo_d[l].rearrange("(c p) n -> p c n", p=128)[:, k0:k0 + nk, jp2 * 256:(jp2 + 1) * 256],
                                lambda t, nk=nk: v3(t, nk, 256))
                            for jj in range(2):
                                ps, Bp = pss2[jj]
                                for kk in range(nk):
                                    c = k0 + kk
                                    S.op("pe", lambda e, ps=ps, kk=kk, c=c, jj=jj, wo=wo, nk=nk: e.matmul(
                                        ps[:], lhsT=v3(wo, nk, 256)[:, kk, jj * 128:(jj + 1) * 128], rhs=gt[:, c, :],
                                        start=(c == 0), stop=(c == NJ - 1)), reads=[Bwo, B_gt], writes=[Bp])
                        for jj in range(2):
                            ps, Bp = pss2[jj]
                            jo = jp2 * 2 + jj
                            S.op("dve", lambda e, ps=ps, jo=jo: e.tensor_tensor(out=xT[:, jo, gs], in0=ps[:], in1=xT[:, jo, gs],
                                                                                op=ALU.add),
                                 reads=[Bp, B_x[g]], writes=[B_x[g]])
                        if l == DEPTH - 1 and stop is None:
                            if p_ == NG - 1:
                                jo0 = jp2 * 2
                                S.dma("sp", lambda e, jo0=jo0, gs=gs: e.dma_start(out=y_d[:, jo0:jo0 + 2, gs],
                                                                                 in_=xT[:, jo0:jo0 + 2, gs]),
                                      "out", reads=[B_x[g]])
                            elif jp2 == 3:
                                S.dma("sp", lambda e, gs=gs: e.dma_start(out=y_d[:, :, gs], in_=xT[:, :, gs]), "out",
                                      reads=[B_x[g]])
                            out_done[0] = True
                        if jp2 == 0 and p_ + 1 < NG:
                            rmsnorm_group(l, order_[p_ + 1], 8, hgs_[(p_ + 1) % 2][0], hgs_[(p_ + 1) % 2][1], sqr, B_sqr,
                                          rstd, B_rstd)
                if stop == ("layer", l):
                    break
        except _Stop:
            pass

        for g in ((1, 2, 3, 0) if not out_done[0] else ()):
            S.dma("sp", lambda e, g=g: e.dma_start(out=y_d[:, :, g * G:(g + 1) * G], in_=xT[:, :, g * G:(g + 1) * G]),
                  "out", reads=[B_x[g]])
        S.final_wait("sp")
        S.emit()
    return nc


def _layout_inputs(inputs):
    f32 = np.float32
    x = np.asarray(inputs["x"], f32)
    L = DEPTH
    sp = np.zeros((L, 128, NSP), f32)
    p = np.arange(128)
    for l in range(L):
        sp[l, :, 0:8] = np.asarray(inputs["ln_mix_g"][l], f32).reshape(8, 128).T
        sp[l, :, 8:16] = np.asarray(inputs["ln_ffn_g"][l], f32).reshape(8, 128).T
        sp[l, :, 16] = np.asarray(inputs["q_norm_g"][l], f32)[p % 64]
        sp[l, :, 17] = np.asarray(inputs["k_norm_g"][l], f32)[p % 64]
        sp[l, :, 18:22] = np.asarray(inputs["b_gk"][l], f32).reshape(4, 128).T
        sp[l, :, 22:24] = np.asarray(inputs["gla_norm_g"][l], f32).reshape(2, 128).T
        sp[l, :, 24:46] = np.asarray(inputs["conv_b"][l], f32).reshape(NJ, 128).T
        cw = np.asarray(inputs["conv_w"][l], f32)
        sp[l, :, 46:112] = cw.reshape(3, NJ, 128).transpose(2, 1, 0).reshape(128, NJ * 3)
        sp[l, :, 112:120] = np.asarray(inputs["sinks"][l], f32)[None, :]
    shared = {
        "sp": sp,
        "w_gk_up": np.ascontiguousarray(np.asarray(inputs["w_gk_up"], f32)),
        "w_in": np.ascontiguousarray(np.asarray(inputs["w_in"], f32)),
        "w_branch_a": np.ascontiguousarray(np.asarray(inputs["w_branch_a"], f32)),
        "w_branch_b": np.ascontiguousarray(np.asarray(inputs["w_branch_b"], f32)),
        "w_out": np.ascontiguousarray(np.asarray(inputs["w_out"], f32)),
        "w_ffn_in": np.ascontiguousarray(np.asarray(inputs["w_ffn_in"], f32)),
        "w_ffn_out": np.ascontiguousarray(np.asarray(inputs["w_ffn_out"], f32)),
    }
    in_maps = []
    for c in range(8):
        b, r = c // 4, c % 4
        xs = x[b, r * TOK:(r + 1) * TOK, :]
        xt = np.ascontiguousarray(xs.T.reshape(8, 128, TOK).transpose(1, 0, 2))
        fl = np.zeros((128, 8), f32)
        if r > 0:
            fl[:, r - 1] = 1.0
            fl[:, 7] = 1.0
        for j in range(3):
            fl[:, 4 + j] = 1.0 if j < r else 0.0
        m = dict(shared)
        m["xT"] = xt
        m["flags"] = fl
        in_maps.append(m)
    return in_maps


def _gather_out(results):
    out = np.zeros((2, 4 * TOK, D), np.float32)
    for c in range(8):
        b, r = c // 4, c % 4
        yt = np.asarray(results[c]["yT"])
        out[b, r * TOK:(r + 1) * TOK, :] = yt.transpose(2, 1, 0).reshape(TOK, D)
    return out


_NC_CACHE = {}


def kernel(**inputs):
    if "nc" not in _NC_CACHE:
        _NC_CACHE["nc"] = build()
    nc = _NC_CACHE["nc"]
    in_maps = _layout_inputs(inputs)
    res = run_bass_kernel_spmd(nc, in_maps, core_ids=list(range(8)))
    return _gather_out(res.results)
```
